# Optimizing a Trainium2 kernel written in Bass

```python
import jax, jax.numpy as jnp
from jax import lax
import numpy as np

D_MODEL = 2048
BATCH = 2
SEQ = 16384
DEPTH = 2

N_EVEN = (DEPTH + 1) // 2
N_ODD = DEPTH // 2
PLE_DIM = 256

HG_HEADS = 8
HG_DK = 128
HG_DV = 128
HG_W = HG_HEADS * HG_DK
HG_CHUNK = 64

MLA_HEADS = 8
MLA_Q_RANK = 512
MLA_KV_RANK = 512
MLA_NOPE = 128
MLA_ROPE = 64
MLA_V = 128
MLA_QK = MLA_NOPE + MLA_ROPE
Q_BLOCK = 128

EVEN_IN = 4 * HG_W + MLA_Q_RANK + MLA_KV_RANK + MLA_ROPE
EVEN_MIX = HG_HEADS * HG_DV + MLA_HEADS * MLA_V

RET_HEADS = 8
RET_DK = 256
RET_DV = 512
RET_CHUNK = 128
ODD_IN = RET_HEADS * (2 * RET_DK + 2 * RET_DV)
ODD_MIX = RET_HEADS * RET_DV

D_FF = 5632
CONV_W = 3

ROPE_BASE = 10000.0
EPS = 1e-6

kernel_name = "hybrid_hgrn2_mla_retnet_convffn"


def rmsnorm(x, g=None):
    xf = x.astype(jnp.float32)
    y = xf * lax.rsqrt(jnp.mean(xf * xf, axis=-1, keepdims=True) + EPS)
    if g is not None:
        y = y * g.astype(jnp.float32)
    return y.astype(x.dtype)


def rope(x, positions):
    half = x.shape[-1] // 2
    inv_freq = ROPE_BASE ** (-jnp.arange(half, dtype=jnp.float32) / half)
    ang = positions.astype(jnp.float32)[:, :, None, None] * inv_freq
    cos, sin = jnp.cos(ang), jnp.sin(ang)
    x1 = x[..., :half].astype(jnp.float32)
    x2 = x[..., half:].astype(jnp.float32)
    return jnp.concatenate([x1 * cos - x2 * sin, x2 * cos + x1 * sin], axis=-1).astype(x.dtype)


def to_chunks(t, c):
    b, s, h, d = t.shape
    return t.reshape(b, s // c, c, h, d).transpose(1, 0, 3, 2, 4)


def from_chunks(t):
    n, b, h, c, d = t.shape
    return t.transpose(1, 0, 3, 2, 4).reshape(b, n * c, h, d)


def hgrn2(q, f_logit, v, lb):
    dt = v.dtype
    c = HG_CHUNK
    lb = lb.reshape(HG_HEADS, HG_DK).astype(jnp.float32)
    f = lb + (1.0 - lb) * jax.nn.sigmoid(f_logit.astype(jnp.float32))
    k = 1.0 - f
    qc, kc, vc, lfc = (to_chunks(t.astype(jnp.float32), c) for t in (q, k, v, jnp.log(f)))
    bc = jnp.cumsum(lfc, axis=3)
    causal = jnp.tril(jnp.ones((c, c), dtype=bool))

    def step(state, xs):
        qj, kj, vj, bj = xs
        ref = bj[:, :, c // 2 - 1:c // 2, :]
        q_rel = qj * jnp.exp(bj - ref)
        k_rel = kj * jnp.exp(ref - bj)
        a = jnp.where(causal, jnp.einsum('bhtk,bhsk->bhts', q_rel, k_rel), 0.0)
        out = (jnp.einsum('bhts,bhsv->bhtv', a, vj)
               + jnp.einsum('bhtk,bhkv->bhtv', qj * jnp.exp(bj), state))
        b_last = bj[:, :, -1:, :]
        state = (jnp.exp(b_last[:, :, 0, :, None]) * state
                 + jnp.einsum('bhsk,bhsv->bhkv', kj * jnp.exp(b_last - bj), vj))
        return state, out

    s0 = jnp.zeros((q.shape[0], HG_HEADS, HG_DK, HG_DV), jnp.float32)
    _, out = lax.scan(step, s0, (qc, kc, vc, bc))
    return from_chunks(out).astype(dt)


def causal_attention(q, k, v):
    b, s, h, dqk = q.shape
    nb = s // Q_BLOCK
    scale = dqk ** -0.5
    qb = q.reshape(b, nb, Q_BLOCK, h, dqk).transpose(1, 0, 2, 3, 4)
    kpos = jnp.arange(s)

    def block(args):
        qi, bi = args
        sc = jnp.einsum('bqhd,bkhd->bhqk', qi, k, preferred_element_type=jnp.float32) * scale
        qpos = bi * Q_BLOCK + jnp.arange(Q_BLOCK)
        sc = jnp.where(kpos[None, :] <= qpos[:, None], sc, -jnp.inf)
        w = jax.nn.softmax(sc, axis=-1)
        return jnp.einsum('bhqk,bkhd->bqhd', w.astype(v.dtype), v,
                          preferred_element_type=jnp.float32).astype(v.dtype)

    out = lax.map(block, (qb, jnp.arange(nb)))
    return out.transpose(1, 0, 2, 3, 4).reshape(b, s, h, v.shape[-1])


def retention(q, k, v):
    dt = v.dtype
    c = RET_CHUNK
    log_g = jnp.log(1.0 - 2.0 ** (-5.0 - jnp.arange(RET_HEADS, dtype=jnp.float32)))
    idx = jnp.arange(c, dtype=jnp.float32)
    diff = idx[:, None] - idx[None, :]
    causal = diff >= 0
    d_intra = jnp.where(causal, jnp.exp(jnp.where(causal, diff, 0.0) * log_g[:, None, None]), 0.0)
    q_dec = jnp.exp((idx + 1.0) * log_g[:, None])[..., None]
    k_dec = jnp.exp((c - 1.0 - idx) * log_g[:, None])[..., None]
    c_dec = jnp.exp(c * log_g)[:, None, None]
    qc, kc, vc = (to_chunks(t.astype(jnp.float32), c) for t in (q, k * RET_DK ** -0.5, v))

    def step(r, xs):
        qj, kj, vj = xs
        a = jnp.einsum('bhtk,bhsk->bhts', qj, kj) * d_intra
        out = (jnp.einsum('bhts,bhsv->bhtv', a, vj)
               + jnp.einsum('bhtk,bhkv->bhtv', qj * q_dec, r))
        r = c_dec * r + jnp.einsum('bhsk,bhsv->bhkv', kj * k_dec, vj)
        return r, out

    r0 = jnp.zeros((q.shape[0], RET_HEADS, RET_DK, RET_DV), jnp.float32)
    _, out = lax.scan(step, r0, (qc, kc, vc))
    return from_chunks(out).astype(dt)


def even_mixer(u, positions, w_in, lb, q_a_norm, kv_a_norm, w_uq, w_ukv,
               q_norm, k_norm, hg_onorm, w_out):
    b, s, _ = u.shape
    z = u @ w_in
    cuts = [HG_W, 2 * HG_W, 3 * HG_W, 4 * HG_W, 4 * HG_W + MLA_Q_RANK,
            4 * HG_W + MLA_Q_RANK + MLA_KV_RANK]
    hq, hf, hi, hg, cq, ckv, kpe = jnp.split(z, cuts, axis=-1)
    o_a = hgrn2(hq.reshape(b, s, HG_HEADS, HG_DK), hf.reshape(b, s, HG_HEADS, HG_DK),
                hi.reshape(b, s, HG_HEADS, HG_DV), lb)
    o_a = rmsnorm(o_a, hg_onorm) * jax.nn.silu(hg.reshape(b, s, HG_HEADS, HG_DV))
    q = (rmsnorm(cq, q_a_norm) @ w_uq).reshape(b, s, MLA_HEADS, MLA_QK)
    kv = (rmsnorm(ckv, kv_a_norm) @ w_ukv).reshape(b, s, MLA_HEADS, MLA_NOPE + MLA_V)
    k_nope, v = kv[..., :MLA_NOPE], kv[..., MLA_NOPE:]
    k = jnp.concatenate([k_nope, jnp.broadcast_to(kpe[:, :, None, :], (b, s, MLA_HEADS, MLA_ROPE))], axis=-1)
    q = rmsnorm(q, q_norm)
    k = rmsnorm(k, k_norm)
    q = jnp.concatenate([q[..., :MLA_NOPE], rope(q[..., MLA_NOPE:], positions)], axis=-1)
    k = jnp.concatenate([k[..., :MLA_NOPE], rope(k[..., MLA_NOPE:], positions)], axis=-1)
    o_b = causal_attention(q, k, v)
    mix = jnp.concatenate([o_a.reshape(b, s, -1), o_b.reshape(b, s, -1)], axis=-1)
    return mix @ w_out


def odd_mixer(u, positions, w_in, w_out):
    b, s, _ = u.shape
    z = u @ w_in
    qw = RET_HEADS * RET_DK
    q, k, v, g = jnp.split(z, [qw, 2 * qw, 2 * qw + ODD_MIX], axis=-1)
    q = rope(q.reshape(b, s, RET_HEADS, RET_DK), positions)
    k = rope(k.reshape(b, s, RET_HEADS, RET_DK), positions)
    o = retention(q, k, v.reshape(b, s, RET_HEADS, RET_DV))
    o = rmsnorm(o) * jax.nn.silu(g.reshape(b, s, RET_HEADS, RET_DV))
    return o.reshape(b, s, ODD_MIX) @ w_out


def conv_ffn(u, w_gate, w_up, conv_w, conv_b, w_down):
    s = u.shape[1]
    a = u @ w_gate
    a_pad = jnp.pad(a, ((0, 0), (CONV_W - 1, 0), (0, 0)))
    c = conv_b
    for t in range(CONV_W):
        c = c + conv_w[t] * a_pad[:, t:t + s]
    return (jax.nn.silu(c) * (u @ w_up)) @ w_down


def setup_inputs(seed: int = 0) -> dict:
    key = jax.random.key(seed)
    ks = iter(jax.random.split(key, 32))

    def nrm(shape, scale):
        return jax.random.normal(next(ks), shape, jnp.float32) * scale

    def gain(shape):
        return 1.0 + 0.05 * jax.random.normal(next(ks), shape, jnp.float32)

    x = nrm((BATCH, SEQ, D_MODEL), 1.0)
    p = nrm((DEPTH, BATCH, SEQ, PLE_DIM), 1.0)
    positions = (jax.random.randint(next(ks), (BATCH, 1), 0, 1024, dtype=jnp.int32)
                 + jnp.arange(SEQ, dtype=jnp.int32)[None, :])
    return {
        "x": x,
        "p": p,
        "positions": positions,
        "norm_mix": gain((DEPTH, D_MODEL)),
        "norm_ffn": gain((DEPTH, D_MODEL)),
        "norm_ple": gain((DEPTH, D_MODEL)),
        "e_w_in": nrm((N_EVEN, D_MODEL, EVEN_IN), D_MODEL ** -0.5),
        "e_lb_logits": nrm((N_EVEN + 1, HG_W), 0.1),
        "e_q_a_norm": gain((N_EVEN, MLA_Q_RANK)),
        "e_kv_a_norm": gain((N_EVEN, MLA_KV_RANK)),
        "e_w_uq": nrm((N_EVEN, MLA_Q_RANK, MLA_HEADS * MLA_QK), MLA_Q_RANK ** -0.5),
        "e_w_ukv": nrm((N_EVEN, MLA_KV_RANK, MLA_HEADS * (MLA_NOPE + MLA_V)), MLA_KV_RANK ** -0.5),
        "e_q_norm": gain((N_EVEN, MLA_QK)),
        "e_k_norm": gain((N_EVEN, MLA_QK)),
        "e_hg_onorm": gain((N_EVEN, HG_DV)),
        "e_w_out": nrm((N_EVEN, EVEN_MIX, D_MODEL), EVEN_MIX ** -0.5),
        "o_w_in": nrm((N_ODD, D_MODEL, ODD_IN), D_MODEL ** -0.5),
        "o_w_out": nrm((N_ODD, ODD_MIX, D_MODEL), ODD_MIX ** -0.5),
        "ffn_w_gate": nrm((DEPTH, D_MODEL, D_FF), D_MODEL ** -0.5),
        "ffn_w_up": nrm((DEPTH, D_MODEL, D_FF), D_MODEL ** -0.5),
        "ffn_conv_w": nrm((DEPTH, CONV_W, D_FF), CONV_W ** -0.5),
        "ffn_conv_b": nrm((DEPTH, D_FF), 0.02),
        "ffn_w_down": nrm((DEPTH, D_FF, D_MODEL), D_FF ** -0.5),
        "ple_w_proj": nrm((DEPTH, PLE_DIM, D_MODEL), PLE_DIM ** -0.5),
        "ple_w_gate": nrm((DEPTH, D_MODEL, D_MODEL), D_MODEL ** -0.5),
    }


def reference(x, p, positions, norm_mix, norm_ffn, norm_ple,
              e_w_in, e_lb_logits, e_q_a_norm, e_kv_a_norm, e_w_uq, e_w_ukv,
              e_q_norm, e_k_norm, e_hg_onorm, e_w_out,
              o_w_in, o_w_out,
              ffn_w_gate, ffn_w_up, ffn_conv_w, ffn_conv_b, ffn_w_down,
              ple_w_proj, ple_w_gate):
    h = x
    lb_all = jnp.cumsum(jax.nn.softmax(e_lb_logits.astype(jnp.float32), axis=0), axis=0)
    for i in range(DEPTH):
        j = i // 2
        u = rmsnorm(h, norm_mix[i])
        if i % 2 == 0:
            mix = even_mixer(u, positions, e_w_in[j], lb_all[j], e_q_a_norm[j], e_kv_a_norm[j],
                             e_w_uq[j], e_w_ukv[j], e_q_norm[j], e_k_norm[j], e_hg_onorm[j], e_w_out[j])
        else:
            mix = odd_mixer(u, positions, o_w_in[j], o_w_out[j])
        h = h + mix
        h = h + conv_ffn(rmsnorm(h, norm_ffn[i]), ffn_w_gate[i], ffn_w_up[i],
                         ffn_conv_w[i], ffn_conv_b[i], ffn_w_down[i])
        gate = jax.nn.sigmoid((rmsnorm(h, norm_ple[i]) @ ple_w_gate[i]).astype(jnp.float32))
        h = h + (p[i] @ ple_w_proj[i]) * gate.astype(h.dtype)
    return h
```

```python
import contextlib
import numpy as np
import concourse.bass as bass
import concourse.mybir as mybir
from concourse.bass_utils import run_bass_kernel_spmd

F32 = mybir.dt.float32
BF16 = mybir.dt.bfloat16
I32 = mybir.dt.int32
AF = mybir.ActivationFunctionType
ALU = mybir.AluOpType

D = 2048
NCORE = 8
SEG = 4096
T = 512
NST = SEG // T
DFF = 5632
NFF = DFF // 128
PLE = 256
EPS = 1e-6


class Trk:
    __slots__ = ("name", "w", "r")

    def __init__(self, name):
        self.name = name
        self.w = None
        self.r = {}


class Stream:
    __slots__ = ("name", "sem", "cnt")

    def __init__(self, name, sem):
        self.name, self.sem, self.cnt = name, sem, 0


class Kern:
    ENG = ("pe", "act", "dve", "pool", "sp")

    def __init__(self, nc, stack):
        self.nc = nc
        self.stack = stack
        self.root = stack
        self.pfx = ""
        self.ncc = 0
        self.ops = {e: [] for e in self.ENG}
        self.streams = {}
        for e in ("pe", "act", "dve", "pool"):
            self.streams[e] = Stream(e, stack.enter_context(nc.semaphore("s_" + e)))
        self.seen = {e: {} for e in self.ENG}
        self.dma_streams = {}
        self.nbank = 0
        self.uid = 0

    def sb(self, name, shape, dt):
        return self.stack.enter_context(self.nc.sbuf_tensor(self.pfx + name, list(shape), dt))

    @contextlib.contextmanager
    def phase(self, pfx):
        old, oldp = self.stack, self.pfx
        with contextlib.ExitStack() as st2:
            self.stack = st2
            self.pfx = pfx
            yield
            self.barrier()
            self.stack, self.pfx = old, oldp

    def collective(self, kind, op, groups, in_ap, out_ap, reads=(), writes=()):
        key = "cc%d" % self.ncc
        self.ncc += 1
        st = Stream(key, self.root.enter_context(self.nc.semaphore("c_" + key)))
        self.dma_streams[key] = st
        deps = self._deps(reads, writes)
        self._emit_waits("pool", deps)
        st.cnt += 1
        self.ops["pool"].append(("ins", lambda e: e.collective_compute(kind, op, replica_groups=groups,
                                                                       ins=[in_ap.opt()], outs=[out_ap.opt()]),
                                 st.sem, None))
        for t in reads:
            if t.r.get(st, 0) < 1:
                t.r[st] = 1
        for t in writes:
            t.w = (st, 1)
            t.r = {}

    def psum_bank(self):
        t = self.stack.enter_context(self.nc.psum_tensor("bank%d" % self.nbank, [128, 512], F32))
        self.nbank += 1
        return t

    def trk(self, name="t"):
        self.uid += 1
        return Trk("%s%d" % (name, self.uid))

    def _deps(self, reads, writes):
        deps = {}
        for t in reads:
            if t.w is not None:
                s, v = t.w
                if deps.get(s, 0) < v:
                    deps[s] = v
        for t in writes:
            if t.w is not None:
                s, v = t.w
                if deps.get(s, 0) < v:
                    deps[s] = v
            for s, v in t.r.items():
                if deps.get(s, 0) < v:
                    deps[s] = v
        return deps

    def _emit_waits(self, eng, deps, own=None):
        seen = self.seen[eng]
        for s, v in deps.items():
            if s is own and eng == "pe":
                continue
            if seen.get(s, 0) >= v:
                continue
            seen[s] = v
            self.ops[eng].append(("wait", s.sem, v))

    def op(self, eng, fn, reads=(), writes=(), inc=True):
        st = self.streams[eng]
        deps = self._deps(reads, writes)
        self._emit_waits(eng, deps, own=st)
        if inc:
            st.cnt += 1
            val = st.cnt
            self.ops[eng].append(("ins", fn, st.sem, 1))
        else:
            val = st.cnt + 1
            self.ops[eng].append(("ins", fn, None, 0))
        for t in reads:
            if t.r.get(st, 0) < val:
                t.r[st] = val
        for t in writes:
            t.w = (st, val)
            t.r = {}

    def dma(self, q, key, out, in_, reads=(), writes=(), **kw):
        st = self.dma_streams.get(key)
        if st is None:
            st = Stream("dma_" + key, self.root.enter_context(self.nc.semaphore("d_" + key)))
            self.dma_streams[key] = st
        deps = self._deps(reads, writes)
        if st.cnt > 0:
            deps[st] = max(deps.get(st, 0), st.cnt)
        self._emit_waits(q, deps)
        st.cnt += 16
        val = st.cnt
        self.ops[q].append(("ins", lambda e, o=out, i=in_: e.dma_start(out=o, in_=i, **kw), st.sem, 16))
        for t in reads:
            if t.r.get(st, 0) < val:
                t.r[st] = val
        for t in writes:
            t.w = (st, val)
            t.r = {}

    def barrier(self):
        allst = list(self.streams.values()) + list(self.dma_streams.values())
        for e in self.ENG:
            for st in allst:
                if st.cnt > 0 and self.seen[e].get(st, 0) < st.cnt:
                    if e == "pe" and st is self.streams["pe"]:
                        continue
                    self.seen[e][st] = st.cnt
                    self.ops[e].append(("wait", st.sem, st.cnt))

    def finish(self):
        for st in self.dma_streams.values():
            if st.cnt > 0 and self.seen["sp"].get(st, 0) < st.cnt:
                self.ops["sp"].append(("wait", st.sem, st.cnt))
        for e in ("pe", "act", "dve", "pool"):
            st = self.streams[e]
            if st.cnt > 0:
                self.ops["sp"].append(("wait", st.sem, st.cnt))

    def replay(self):
        nc = self.nc
        with nc.Block() as block:
            def run(eng_obj, lst):
                for o in lst:
                    if o[0] == "wait":
                        eng_obj.wait_ge(o[1], o[2])
                    else:
                        ins = o[1](eng_obj)
                        if o[2] is not None:
                            if o[3] is None:
                                ins.then_inc(o[2])
                            else:
                                ins.then_inc(o[2], o[3])

            @block.tensor
            def _(e):
                run(e, self.ops["pe"])

            @block.scalar
            def _(e):
                run(e, self.ops["act"])

            @block.vector
            def _(e):
                run(e, self.ops["dve"])

            @block.gpsimd
            def _(e):
                run(e, self.ops["pool"])

            @block.sync
            def _(e):
                run(e, self.ops["sp"])

    def mm(self, out, lhsT, rhs, start, stop, reads, writes, inc=None):
        if inc is None:
            inc = stop
        self.op("pe", lambda e: e.matmul(out, lhsT, rhs, start=start, stop=stop), reads, writes, inc=inc)

    def tr(self, out, in_, ident, reads, writes, inc=True):
        self.op("pe", lambda e: e.transpose(out, in_, ident), reads, writes, inc=inc)

    def act(self, out, in_, func, reads, writes, bias=None, scale=None, eng="act"):
        kw = {}
        if bias is not None:
            kw["bias"] = bias
        if scale is not None:
            kw["scale"] = scale
        self.op(eng, lambda e: e.activation(out, in_, func, **kw), reads, writes)

    def tt(self, eng, out, in0, in1, op, reads, writes):
        self.op(eng, lambda e: e.tensor_tensor(out, in0, in1, op), reads, writes)

    def ts(self, eng, out, in0, s1, s2, op0, op1, reads, writes):
        if op1 is None:
            self.op(eng, lambda e: e.tensor_scalar(out, in0, s1, None, op0), reads, writes)
        else:
            self.op(eng, lambda e: e.tensor_scalar(out, in0, s1, s2, op0, op1), reads, writes)

    def stt(self, out, in0, scalar, in1, op0, op1, reads, writes):
        self.op("dve", lambda e: e.scalar_tensor_tensor(out, in0, scalar, in1, op0, op1), reads, writes)

    def copy(self, eng, out, in_, reads, writes):
        if eng == "act":
            self.op(eng, lambda e: e.copy(out, in_), reads, writes)
        else:
            self.op(eng, lambda e: e.tensor_copy(out, in_), reads, writes)


class Ctx:
    def __init__(self, K):
        self.K = K
        nc = K.nc
        self.banks = [K.psum_bank() for _ in range(8)]
        self.bank_trk = [K.trk("bank") for _ in range(8)]
        self.bank_i = 0
        self.held = set()
        self.ident = K.sb("ident", [128, 128], F32)
        self.t_ident = K.trk("ident")
        idd = nc.inline_tensor(np.eye(128, dtype=np.float32), name="ident_d").ap()
        K.dma("sp", "const", self.ident[:], idd[:, :], writes=[self.t_ident])
        self.ones_f = K.sb("ones_f", [128, 128], F32)
        self.ones = K.sb("ones_b", [128, 128], BF16)
        self.t_ones = K.trk("ones")
        K.op("pool", lambda e: e.memset(self.ones_f[:], 1.0), writes=[self.t_ones])
        K.copy("pool", self.ones[:], self.ones_f[:], [self.t_ones], [self.t_ones])

    def bank(self, hold=False):
        for _ in range(8):
            i = self.bank_i
            self.bank_i = (i + 1) % 8
            if i not in self.held:
                break
        else:
            raise RuntimeError("all PSUM banks held")
        if hold:
            self.held.add(i)
        return self.banks[i], self.bank_trk[i]

    def release(self, t_bk):
        self.held.discard(self.bank_trk.index(t_bk))


class WCast:
    def __init__(self, K, nslots=3):
        self.K = K
        self.n = nslots
        self.f = [K.sb("wc_f%d" % i, [128, 8, 512], F32) for i in range(nslots)]
        self.b = [K.sb("wc_b%d" % i, [128, 8, 512], BF16) for i in range(nslots)]
        self.cap = 8 * 512
        self.tf = [K.trk("wcf") for _ in range(nslots)]
        self.tb = [K.trk("wcb") for _ in range(nslots)]
        self.i = 0
        self.engs = ("pool", "dve", "act")

    def cast(self, w, Kd, N, name, cw=512):
        K = self.K
        nkc = Kd // 128
        nb = N // cw
        scr = K.nc.dram_tensor(name, [nb, 128, nkc, cw], BF16, kind="Internal").ap()
        wv = w.rearrange("(kc p) n -> p kc n", p=128)
        kstep = max(1, min(nkc, self.cap // cw, 8))
        for b in range(nb):
            for k0 in range(0, nkc, kstep):
                kn = min(kstep, nkc - k0)
                s = self.i % self.n
                self.i += 1
                fv = self.f[s][:].rearrange("p a b -> p (a b)")[:, 0:kn * cw].rearrange("p (a b) -> p a b", a=kn)
                bv = self.b[s][:].rearrange("p a b -> p (a b)")[:, 0:kn * cw].rearrange("p (a b) -> p a b", a=kn)
                K.dma("sp", "wcl%d" % s, fv, wv[:, k0:k0 + kn, b * cw:(b + 1) * cw], writes=[self.tf[s]])
                eng = self.engs[self.i % 3]
                K.copy(eng, bv, fv, [self.tf[s]], [self.tb[s]])
                K.dma("pool", "wcs%d" % s, scr[b, :, k0:k0 + kn, :], bv, reads=[self.tb[s]], writes=[])
        return scr


class Stage:
    def __init__(self, K, name, ncol, nslots):
        self.buf = [K.sb("%s%d" % (name, i), [128, ncol], F32) for i in range(nslots)]
        self.t = [K.trk(name) for _ in range(nslots)]
        self.n = nslots
        self.i = 0
        self.name = name

    def next(self):
        s = self.i % self.n
        self.i += 1
        return self.buf[s], self.t[s], "%s%d" % (self.name, s)


def load_tokmajor_T(C, src, ncol, dst, t_dst, stage):
    K = C.K
    nch = ncol // 128
    for tb in range(T // 128):
        sb_, t_sb, key = stage.next()
        K.dma("sp", key, sb_[:, 0:ncol], src[tb * 128:(tb + 1) * 128, :], writes=[t_sb])
        for c0 in range(0, nch, 4):
            cn = min(4, nch - c0)
            bk, tb_ = C.bank()
            for cc in range(cn):
                kc = c0 + cc
                K.tr(bk[:, cc * 128:(cc + 1) * 128], sb_[:, kc * 128:(kc + 1) * 128], C.ident[:],
                     [t_sb, C.t_ident], [tb_], inc=(cc == cn - 1))
            K.copy(("act", "dve")[(c0 // 4) % 2], dst[:, c0:c0 + cn, tb * 128:(tb + 1) * 128],
                   bk[:, 0:cn * 128].rearrange("p (c t) -> p c t", c=cn), [tb_], t_dst[c0:c0 + cn])


def rmsnorm_fm(C, hT, t_h, nch, g_sb, t_g, uT, t_u, sq, t_sq, rstd, t_rstd, dim, W=T):
    K = C.K
    for kc in range(nch):
        K.act(sq[:, kc, :], hT[:, kc, :], AF.Square, [t_h[kc]], [t_sq[kc]])
    bk, tbk = C.bank()
    for kc in range(nch):
        K.mm(bk[:, 0:W], C.ones[:], sq[:, kc, :], kc == 0, kc == nch - 1, [C.t_ones, t_sq[kc]], [tbk])
    K.act(rstd[:, :], bk[:, 0:W], AF.Ln, [tbk], [t_rstd], bias=EPS, scale=1.0 / dim)
    K.act(rstd[:, :], rstd[:, :], AF.Exp, [t_rstd], [t_rstd], scale=-0.5)
    for kc in range(nch):
        K.stt(uT[:, kc, :], hT[:, kc, :], g_sb[:, kc:kc + 1], rstd[:, :], ALU.mult, ALU.mult,
              [t_h[kc], t_g, t_rstd], [t_u[kc]])


class WStream:
    def __init__(self, K, name, nkc, cw, nslots):
        self.K, self.name, self.n = K, name, nslots
        self.buf = [K.sb("%s_%d" % (name, i), [128, nkc, cw], BF16) for i in range(nslots)]
        self.t = [K.trk(name) for _ in range(nslots)]
        self.i = 0

    def load(self, scr, t_scr, b, k0=0, kn=None):
        s = self.i % self.n
        self.i += 1
        nk = scr.shape[2] if kn is None else kn
        self.K.dma("sp", "%s%d" % (self.name, s), self.buf[s][:, 0:nk, :], scr[b, :, k0:k0 + nk, :],
                   reads=[t_scr], writes=[self.t[s]])
        return self.buf[s], self.t[s]


def dram_in(nc, name, shape, dt=F32):
    return nc.dram_tensor(name, list(shape), dt, kind="ExternalInput").ap()


def dram_out(nc, name, shape, dt=F32):
    return nc.dram_tensor(name, list(shape), dt, kind="ExternalOutput").ap()


def store_tokmajor(C, hT, t_h, dst, stage, q="pool"):
    K = C.K
    for tb in range(T // 128):
        sb_, t_sb, key = stage.next()
        for c4 in range(4):
            bk, t_bk = C.bank()
            for cc in range(4):
                kc = c4 * 4 + cc
                K.tr(bk[:, cc * 128:(cc + 1) * 128], hT[:, kc, tb * 128:(tb + 1) * 128], C.ident[:],
                     [t_h[kc], C.t_ident], [t_bk], inc=(cc == 3))
            K.copy(("act", "dve")[c4 % 2], sb_[:, c4 * 512:(c4 + 1) * 512], bk[:, :], [t_bk], [t_sb])
        K.dma(q, key + "s", dst[tb * 128:(tb + 1) * 128, :], sb_[:, :], reads=[t_sb], writes=[])


def ffn_phase_body(K, C, h_in, hprev, p_in, vecs, scr, h_out, gath=None, t_gath=None):
    vec = K.sb("vec", [128, NVF], F32)
    t_vec = K.trk("vec")
    K.dma("sp", "const", vec[:], vecs[:, :], writes=[t_vec])
    g_ffn = vec[:, 0:16]
    g_ple = vec[:, 16:32]
    cw = [vec[:, 32 + i * NFF:32 + (i + 1) * NFF] for i in range(3)]
    cb = vec[:, 32 + 3 * NFF:32 + 4 * NFF]
    NOT = K.trk("none")

    hT = K.sb("hT", [128, 16, T], F32)
    t_h = [K.trk("h") for _ in range(16)]
    uT = K.sb("uT", [128, 16, T], BF16)
    t_u = [K.trk("u") for _ in range(16)]
    hid = K.sb("hid", [128, NFF, T], BF16)
    t_hid = [K.trk("hid") for _ in range(NFF)]
    rstd = K.sb("rstd", [128, T], F32)
    t_rstd = K.trk("rstd")
    stage = Stage(K, "stg", D, 2)
    pstage = Stage(K, "pstg", PLE, 2)
    pT = K.sb("pT", [128, 2, T], BF16)
    t_p = [K.trk("p") for _ in range(2)]
    halo_sb = K.sb("halo_sb", [128, NFF, 2], F32)
    t_halo = [K.trk("halo") for _ in range(NFF)]
    abuf = [K.sb("abuf%d" % i, [128, T + 2], F32) for i in range(2)]
    t_abuf = [K.trk("abuf") for _ in range(2)]
    cbuf = [K.sb("cbuf%d" % i, [128, T], F32) for i in range(2)]
    t_cbuf = [K.trk("cbuf") for _ in range(2)]
    sig = [K.sb("sig%d" % i, [128, T], F32) for i in range(2)]
    t_sig = [K.trk("sig") for _ in range(2)]
    WG = WStream(K, "wg", 16, 128, 3)
    WU = WStream(K, "wu", 16, 128, 3)
    WD = WStream(K, "wd", NFF, 128, 2)
    WP = WStream(K, "wp", 16, 256, 2)
    wpp_sb = K.sb("wpp", [128, 2, D], BF16)
    t_wpp = K.trk("wpp")
    for b in range(4):
        K.dma("sp", "const", wpp_sb[:, :, b * 512:(b + 1) * 512], scr["pp"][b], writes=[t_wpp])
    sq = hid
    t_sq = t_hid
    hpT = K.sb("hpT", [128, 16, 2], F32)
    hpu = K.sb("hpu", [128, 16, 2], BF16)
    hpsq = K.sb("hpsq", [128, 16, 2], BF16)
    hprs = K.sb("hprs", [128, 2], F32)
    t_hp = [K.trk("hp") for _ in range(16)]
    t_hpu = [K.trk("hpu") for _ in range(16)]
    t_hpsq = [K.trk("hpsq") for _ in range(16)]
    t_hprs = K.trk("hprs")
    if gath is None:
        for t_ in range(2):
            K.dma("sp", "const", hpT[:, :, t_], hprev[t_, :].rearrange("(c p) -> p c", p=128), writes=t_hp,
                  allow_slow_non_contiguous=True)
    else:
        hpa = K.sb("hpa", [128, 16, 8], F32)
        t_hpa = K.trk("hpa")
        hrows = K.sb("hrows", [8, D], F32)
        t_hrows = K.trk("hrows")
        K.dma("sp", "const", hrows[:, :], gath[:, :], reads=[t_gath], writes=[t_hrows])
        bkh, t_bkh = C.bank()
        for c_ in range(16):
            K.tr(bkh[:, c_ * 8:(c_ + 1) * 8], hrows[0:8, c_ * 128:(c_ + 1) * 128], C.ident[0:8, 0:8],
                 [t_hrows, C.t_ident], [t_bkh], inc=(c_ == 15))
        K.copy("act", hpa[:, :, :].rearrange("p c r -> p (c r)"), bkh[:, 0:128], [t_bkh], [t_hpa])
        sel = vec[:, 32 + 4 * NFF:32 + 4 * NFF + 4]
        for t_ in range(2):
            K.ts("dve", hpT[:, :, t_], hpa[:, :, t_], sel[:, 0:1], None, ALU.mult, None, [t_hpa, t_vec], t_hp)
            for r_ in range(1, 4):
                K.stt(hpT[:, :, t_], hpa[:, :, 2 * r_ + t_], sel[:, r_:r_ + 1], hpT[:, :, t_], ALU.mult, ALU.add,
                      [t_hpa, t_vec] + t_hp, t_hp)
    rmsnorm_fm(C, hpT, t_hp, 16, g_ffn, t_vec, hpu, t_hpu, hpsq, t_hpsq, hprs, t_hprs, D, W=2)

    for st in range(NST):
        tok = slice(st * T, (st + 1) * T)
        load_tokmajor_T(C, h_in[tok, :], D, hT, t_h, stage)
        load_tokmajor_T(C, p_in[tok, :], PLE, pT, t_p, pstage)
        rmsnorm_fm(C, hT, t_h, 16, g_ffn, t_vec, uT, t_u, sq, t_sq, rstd, t_rstd, D)
        for j in range(NFF):
            wg, t_wg = WG.load(scr["gate"], NOT, j)
            wu, t_wu = WU.load(scr["up"], NOT, j)
            if st == 0:
                bh, t_bh = C.bank()
                for kc in range(16):
                    K.mm(bh[:, 0:2], wg[:, kc, :], hpu[:, kc, :], kc == 0, kc == 15, [t_wg, t_hpu[kc]], [t_bh])
                K.copy("act", halo_sb[:, j, :], bh[:, 0:2], [t_bh], [t_halo[j]])
            bg, t_bg = C.bank()
            for kc in range(16):
                K.mm(bg[:, :], wg[:, kc, :], uT[:, kc, :], kc == 0, kc == 15, [t_wg, t_u[kc]], [t_bg])
            bu, t_bu = C.bank()
            for kc in range(16):
                K.mm(bu[:, :], wu[:, kc, :], uT[:, kc, :], kc == 0, kc == 15, [t_wu, t_u[kc]], [t_bu])
            ab, t_ab = abuf[j % 2], t_abuf[j % 2]
            cbf, t_cb = cbuf[j % 2], t_cbuf[j % 2]
            K.copy("pool", ab[:, 0:2], halo_sb[:, j, :], [t_halo[j]], [t_ab])
            K.copy("act", ab[:, 2:T + 2], bg[:, :], [t_bg], [t_ab])
            K.copy("pool", halo_sb[:, j, :], ab[:, T:T + 2], [t_ab], [t_halo[j]])
            K.ts("dve", cbf[:, :], ab[:, 2:T + 2], cw[2][:, j:j + 1], cb[:, j:j + 1], ALU.mult, ALU.add,
                 [t_ab, t_vec], [t_cb])
            K.stt(cbf[:, :], ab[:, 1:T + 1], cw[1][:, j:j + 1], cbf[:, :], ALU.mult, ALU.add,
                  [t_ab, t_vec, t_cb], [t_cb])
            K.stt(cbf[:, :], ab[:, 0:T], cw[0][:, j:j + 1], cbf[:, :], ALU.mult, ALU.add,
                  [t_ab, t_vec, t_cb], [t_cb])
            K.act(cbf[:, :], cbf[:, :], AF.Silu, [t_cb], [t_cb])
            K.tt("dve", hid[:, j, :], cbf[:, :], bu[:, :], ALU.mult, [t_cb, t_bu], [t_hid[j]])
        for n in range(16):
            wd, t_wd = WD.load(scr["down"], NOT, n)
            bd, t_bd = C.bank()
            for j in range(NFF):
                K.mm(bd[:, :], wd[:, j, :], hid[:, j, :], j == 0, j == NFF - 1, [t_wd, t_hid[j]], [t_bd])
            K.tt("dve", hT[:, n, :], hT[:, n, :], bd[:, :], ALU.add, [t_h[n], t_bd], [t_h[n]])
        rmsnorm_fm(C, hT, t_h, 16, g_ple, t_vec, uT, t_u, sq, t_sq, rstd, t_rstd, D)
        for n2 in range(8):
            wp, t_wp = WP.load(scr["pg"], NOT, n2)
            for nn in range(2):
                n = n2 * 2 + nn
                bg, t_bg = C.bank()
                for kc in range(16):
                    K.mm(bg[:, :], wp[:, kc, nn * 128:(nn + 1) * 128], uT[:, kc, :], kc == 0, kc == 15,
                         [t_wp, t_u[kc]], [t_bg])
                bp, t_bp = C.bank()
                for kc in range(2):
                    K.mm(bp[:, :], wpp_sb[:, kc, n * 128:(n + 1) * 128], pT[:, kc, :], kc == 0, kc == 1,
                         [t_wpp, t_p[kc]], [t_bp])
                sg, t_sg = sig[n % 2], t_sig[n % 2]
                K.act(sg[:, :], bg[:, :], AF.Sigmoid, [t_bg], [t_sg])
                K.tt("dve", sg[:, :], sg[:, :], bp[:, :], ALU.mult, [t_sg, t_bp], [t_sg])
                K.tt("pool", hT[:, n, :], hT[:, n, :], sg[:, :], ALU.add, [t_h[n], t_sg], [t_h[n]])
        store_tokmajor(C, hT, t_h, h_out[tok, :], stage)


def cast_ffn_weights(K, w_gate, w_up, w_down, w_pg, w_pp, pfx=""):
    with contextlib.ExitStack() as st2:
        K2 = K
        old = K.stack
        K.stack = st2
        WC = WCast(K, nslots=3)
        scr = {}
        scr["gate"] = WC.cast(w_gate, D, DFF, pfx + "s_gate", cw=128)
        scr["up"] = WC.cast(w_up, D, DFF, pfx + "s_up", cw=128)
        scr["down"] = WC.cast(w_down, DFF, D, pfx + "s_down", cw=128)
        scr["pg"] = WC.cast(w_pg, D, D, pfx + "s_pg", cw=256)
        scr["pp"] = WC.cast(w_pp, PLE, D, pfx + "s_pp", cw=512)
        K.barrier()
        K.stack = old
    return scr


def build_ffn_phase(nc):
    stack = contextlib.ExitStack()
    with stack:
        K = Kern(nc, stack)
        h_in = dram_in(nc, "h_in", [SEG, D])
        hprev = dram_in(nc, "hprev", [2, D])
        p_in = dram_in(nc, "p_in", [SEG, PLE])
        w_gate = dram_in(nc, "w_gate", [D, DFF])
        w_up = dram_in(nc, "w_up", [D, DFF])
        w_down = dram_in(nc, "w_down", [DFF, D])
        w_pg = dram_in(nc, "w_pg", [D, D])
        w_pp = dram_in(nc, "w_pp", [PLE, D])
        vecs = dram_in(nc, "vecs", [128, NVF])
        h_out = dram_out(nc, "h_out", [SEG, D])
        C = Ctx(K)
        scr = cast_ffn_weights(K, w_gate, w_up, w_down, w_pg, w_pp)
        ffn_phase_body(K, C, h_in, hprev, p_in, vecs, scr, h_out)
        K.finish()
        K.replay()
    return nc


NVF = 32 + NFF * 4 + 4


def ffn_vecs(norm_ffn, norm_ple, conv_w, conv_b, q=0):
    v = np.zeros((128, NVF), np.float32)
    if q > 0:
        v[:, 32 + 4 * NFF + q - 1] = 1.0
    v[:, 0:16] = norm_ffn.reshape(16, 128).T
    v[:, 16:32] = norm_ple.reshape(16, 128).T
    for i in range(3):
        v[:, 32 + i * NFF:32 + (i + 1) * NFF] = conv_w[i].reshape(NFF, 128).T
    v[:, 32 + 3 * NFF:32 + 4 * NFF] = conv_b.reshape(NFF, 128).T
    return v


NSLOT = 3
SLOT_ST = 8
O_ST = 8
TWO_PI = 2.0 * np.pi
MAGIC = 12582912.0
C1 = 6.28125
C2 = float(TWO_PI - 6.28125)
PI_LO = 3.1415925
NVA = 80


def vecsA_host(inp, q):
    v = np.zeros((128, NVA), np.float32)
    v[:, 0:16] = inp["norm_mix"][0].reshape(16, 128).T
    v[:, 16:24] = inp["e_lb_logits"][0].reshape(8, 128).T
    v[:, 24:32] = inp["e_lb_logits"][1].reshape(8, 128).T
    v[:, 32:36] = inp["e_q_a_norm"][0].reshape(4, 128).T
    v[:, 36:40] = inp["e_kv_a_norm"][0].reshape(4, 128).T
    qn, kn = inp["e_q_norm"][0], inp["e_k_norm"][0]
    for base, g in ((40, qn), (43, kn)):
        v[:, base] = g[0:128]
        v[0:64, base + 1] = g[128:192]
        v[0:32, base + 2] = g[160:192]
        v[32:64, base + 2] = g[128:160]
    v[:, 46] = inp["e_hg_onorm"][0]
    invf = (10000.0 ** (-np.arange(32, dtype=np.float32) / np.float32(32))).astype(np.float32)
    v[0:64, 47] = np.concatenate([invf, invf])
    v[0:32, 48] = -1.0
    v[32:64, 48] = 1.0
    for i in range(NSLOT):
        valid = (q - NSLOT + i) >= 0
        v[:, 65 + i] = 0.0 if valid else -30000.0
        v[:, 69 + i] = 1.0 if valid else 0.0
    v[:, 68] = 0.0
    return v


class PhaseA:
    def __init__(self, K, C, nc):
        self.K, self.C, self.nc = K, C, nc

    def prologue(self, w_in, w_uq, w_ukv, w_out):
        K = self.K
        with contextlib.ExitStack() as st2:
            old = K.stack
            K.stack = st2
            WC = WCast(K, nslots=3)
            scr = {}
            scr["win"] = WC.cast(w_in[:, 0:5120], D, 5120, "sA_win", cw=128)
            scr["hi"] = WC.cast(w_in[:, 2048:3072], D, 1024, "sA_hi", cw=256)
            scr["wout"] = WC.cast(w_out, D, D, "sA_wout", cw=128)
            scr["small"] = K.nc.dram_tensor("sA_small", [4, 128, 4, 1024], BF16, kind="Internal").ap()
            scr["kpe"] = K.nc.dram_tensor("sA_kpe", [1, 128, 16, 128], BF16, kind="Internal").ap()

            def piece(dst, src, n, nkc):
                srcv = src.rearrange("(kc p) n -> p kc n", p=128)
                for k0 in range(0, nkc, 8):
                    kn = min(8, nkc - k0)
                    s = WC.i % WC.n
                    WC.i += 1
                    fv = WC.f[s][:].rearrange("p a b -> p (a b)")[:, 0:kn * n].rearrange("p (a b) -> p a b", a=kn)
                    bv = WC.b[s][:].rearrange("p a b -> p (a b)")[:, 0:kn * n].rearrange("p (a b) -> p a b", a=kn)
                    K.dma("sp", "wcl%d" % s, fv, srcv[:, k0:k0 + kn, :], writes=[WC.tf[s]])
                    K.copy(WC.engs[WC.i % 3], bv, fv, [WC.tf[s]], [WC.tb[s]])
                    K.dma("pool", "wcs%d" % s, dst[:, k0:k0 + kn, :], bv, reads=[WC.tb[s]], writes=[])

            sm = scr["small"]
            for h in range(8):
                piece(sm[0, :, :, h * 128:(h + 1) * 128], w_ukv[:, h * 256:h * 256 + 128], 128, 4)
                piece(sm[1, :, :, h * 128:(h + 1) * 128], w_ukv[:, h * 256 + 128:h * 256 + 256], 128, 4)
                piece(sm[2, :, :, h * 128:(h + 1) * 128], w_uq[:, h * 192:h * 192 + 128], 128, 4)
                piece(sm[3, :, :, h * 64:(h + 1) * 64], w_uq[:, h * 192 + 128:h * 192 + 192], 64, 4)
                piece(sm[3, :, :, 512 + h * 64:512 + h * 64 + 32], w_uq[:, h * 192 + 160:h * 192 + 192], 32, 4)
                piece(sm[3, :, :, 512 + h * 64 + 32:512 + (h + 1) * 64], w_uq[:, h * 192 + 128:h * 192 + 160], 32, 4)
            piece(scr["kpe"][0, :, :, 0:64], w_in[:, 5120:5184], 64, 16)
            piece(scr["kpe"][0, :, :, 64:96], w_in[:, 5152:5184], 32, 16)
            piece(scr["kpe"][0, :, :, 96:128], w_in[:, 5120:5152], 32, 16)
            K.barrier()
            K.stack = old
        self.scr = scr

    def alloc(self, vecs, pos_all, ntok_all):
        K, C, nc = self.K, self.C, self.nc
        self.vec = K.sb("vecA", [128, NVA], F32)
        self.t_vec = K.trk("vecA")
        K.dma("sp", "const", self.vec[:], vecs[:, :], writes=[self.t_vec])
        v = self.vec
        self.lb = K.sb("lb", [128, 8], F32)
        self.oml = K.sb("oml", [128, 8], F32)
        K.tt("dve", self.lb[:], v[:, 16:24], v[:, 24:32], ALU.subtract, [self.t_vec], [self.t_vec])
        K.act(self.lb[:], self.lb[:], AF.Sigmoid, [self.t_vec], [self.t_vec])
        K.ts("dve", self.oml[:], self.lb[:], -1.0, 1.0, ALU.mult, ALU.add, [self.t_vec], [self.t_vec])
        tri = np.triu(np.ones((128, 128), np.float32))
        self.tri = K.sb("tri", [128, 128], F32)
        rm = np.ones((128, T), np.float32)
        rm[:, 0::128] = 0.0
        self.rmask = K.sb("rmask", [128, T], F32)
        self.t_cst = K.trk("cstA")
        K.dma("sp", "const", self.tri[:], nc.inline_tensor(tri, name="tri_d").ap()[:, :], writes=[self.t_cst])
        K.dma("sp", "const", self.rmask[:], nc.inline_tensor(rm, name="rm_d").ap()[:, :], writes=[self.t_cst])
        self.trib = K.sb("trib", [128, 128], BF16)
        K.copy("dve", self.trib[:], self.tri[:], [self.t_cst], [self.t_cst])
        sel4 = np.zeros((128, 128), np.float32)
        sel4[0::32, :] = 1.0
        self.sel4 = K.sb("sel4", [128, 128], F32)
        K.dma("sp", "const", self.sel4[:], nc.inline_tensor(sel4, name="sel4_d").ap()[:, :], writes=[self.t_cst])
        self.pos_all = pos_all
        self.ntok = ntok_all
        self.Kn = nc.dram_tensor("Kn_scr", [8, 128, ntok_all], BF16, kind="Internal").ap()
        self.Kr = nc.dram_tensor("Kr_scr", [8, 64, ntok_all], BF16, kind="Internal").ap()
        self.Vs = nc.dram_tensor("V_scr", [ntok_all, 1024], BF16, kind="Internal").ap()
        ng = ntok_all // T
        self.t_kn = [[K.trk("kn") for _ in range(8)] for _ in range(ng)]
        self.t_kr = [[K.trk("kr") for _ in range(8)] for _ in range(ng)]
        self.t_vs = [[K.trk("vs") for _ in range(4)] for _ in range(ng)]
        self.xT = K.sb("xT", [128, 16, T], F32)
        self.t_x = [K.trk("x") for _ in range(16)]
        self.wk = [self.xT[:, i, :] for i in range(16)]
        self.t_wk = self.t_x
        self.wk_i = 0
        self.uT = K.sb("uTa", [128, 16, T], BF16)
        self.t_u = [K.trk("u") for _ in range(16)]
        self.mixT = K.sb("mixT", [128, 16, T], BF16)
        self.t_mix = [K.trk("mix") for _ in range(16)]
        self.sq = self.mixT
        self.t_sq = self.t_mix
        self.sq4 = K.sb("sq4", [128, 4, T], BF16)
        self.t_sq4 = [K.trk("sq4") for _ in range(4)]
        self.rstd = K.sb("rstdA", [128, T], F32)
        self.t_rstd = K.trk("rstd")
        self.stage = Stage(K, "stgA", D, 2)
        self.W = WStream(K, "wA", 16, 128, 3)
        self.WH = WStream(K, "wAh", 16, 256, 2)
        self.WS = WStream(K, "wAs", 4, 1024, 2)
        self.posi = K.sb("posi", [64, T], I32)
        self.cos = K.sb("cosT", [64, T], F32)
        self.sin = K.sb("sinT", [64, T], F32)
        self.t_rope = K.trk("rope")
        self.t_cs = K.trk("cossin")
        self.wb = [K.sb("wbA%d" % i, [128, T], BF16) for i in range(6)]
        self.t_wb = [K.trk("wb") for _ in range(6)]
        self.wb_i = 0
        self.kvn = K.sb("kvn", [128, 4, T], BF16)
        self.t_kvn = [K.trk("kvn") for _ in range(4)]
        self.kpe = K.sb("kpe", [64, 2, T], F32)
        self.t_kpe = K.trk("kpe")
        self.KR = K.sb("KR", [64, T], F32)
        self.t_KR = K.trk("KR")
        self.sqpe = K.sb("sqpe", [64, T], BF16)
        self.t_sqpe = K.trk("sqpe")
        self.vtm = K.sb("vtm", [128, 4, 1024], BF16)
        self.t_vtm = [K.trk("vtm") for _ in range(4)]
        self.hv = self.vtm
        self.t_hv = self.t_vtm
        self.Qn = K.sb("Qn", [128, 8, T], BF16)
        self.Qr = K.sb("Qr", [64, 8, T], BF16)
        self.t_Q = [K.trk("Q") for _ in range(8)]
        self.S = K.sb("S", [128, 8, 128], F32)
        self.Sb = K.sb("Sb", [128, 8, 128], BF16)
        self.t_S = [K.trk("S") for _ in range(8)]
        self.t_Sb = [K.trk("Sb") for _ in range(8)]
        for h in range(8):
            K.op("pool", lambda e, h=h: e.memset(self.S[:, h, :], 0.0), writes=[self.t_S[h]])
            K.op("pool", lambda e, h=h: e.memset(self.Sb[:, h, :], 0.0), writes=[self.t_Sb[h]])
        self.fbuf = [K.sb("fbuf%d" % i, [128, T], F32) for i in range(2)]
        self.t_fbuf = [K.trk("fbuf") for _ in range(2)]
        self.kktm = K.sb("kktm", [128, 2, 4, 128], BF16)
        self.t_kktm = [K.trk("kktm") for _ in range(2)]
        self.ebl = K.sb("ebl", [128, 2, 4], F32)
        self.aK = [K.sb("aK%d" % i, [128, T], BF16) for i in range(3)]
        self.aKr = [K.sb("aKr%d" % i, [64, T], BF16) for i in range(3)]
        self.aV = [K.sb("aV%d" % i, [128, 4, 128], BF16) for i in range(3)]
        self.t_aK = [K.trk("aK") for _ in range(3)]
        self.t_aKr = [K.trk("aKr") for _ in range(3)]
        self.t_aV = [K.trk("aV") for _ in range(3)]
        self.aP = [K.sb("aP%d" % i, [128, T], BF16) for i in range(4)]
        self.t_aP = [K.trk("aP") for _ in range(4)]
        self.a_i = 0
        self.p_i = 0

    def work(self):
        i = self.wk_i % 16
        self.wk_i += 1
        return self.wk[i], self.t_wk[i]

    def workb(self):
        i = self.wb_i % 6
        self.wb_i += 1
        return self.wb[i], self.t_wb[i]

    def rope_tables(self, g):
        K = self.K
        v = self.vec
        K.dma("sp", "posi", self.posi[:, :], self.pos_all[0:1, g * T:(g + 1) * T].broadcast_to([64, T]),
              writes=[self.t_rope])
        pf, t_pf = self.work()
        ra, t_ra = self.work()
        rn, t_rn = self.work()
        posf, ra, rn = pf[0:64, :], ra[0:64, :], rn[0:64, :]
        K.copy("dve", posf, self.posi[:, :], [self.t_rope], [t_pf])
        for off, out in ((0.0, self.sin), (float(np.pi / 2), self.cos)):
            K.ts("dve", ra, posf, v[0:64, 47:48], off, ALU.mult, ALU.add, [t_pf, self.t_vec], [t_ra])
            K.ts("dve", rn, ra, float(1.0 / TWO_PI), MAGIC, ALU.mult, ALU.add, [t_ra], [t_rn])
            K.ts("dve", rn, rn, -MAGIC, None, ALU.add, None, [t_rn], [t_rn])
            K.stt(ra, rn, -C1, ra, ALU.mult, ALU.add, [t_rn, t_ra], [t_ra])
            K.stt(ra, rn, -C2, ra, ALU.mult, ALU.add, [t_rn, t_ra], [t_ra])
            K.ts("dve", ra, ra, -PI_LO, PI_LO, ALU.max, ALU.min, [t_ra], [t_ra])
            K.act(out[:, :], ra, AF.Sin, [t_ra], [self.t_cs])
        K.ts("dve", self.sin[:, :], self.sin[:, :], v[0:64, 48:49], None, ALU.mult, None,
             [self.t_cs, self.t_vec], [self.t_cs])

    def proj_fm(self, lhs_fn, lhs_trk, nkc, rhsT, t_rhs, np_out=128):
        K, C = self.K, self.C
        bk, t_bk = C.bank()
        for kc in range(nkc):
            K.mm(bk[0:np_out, :], lhs_fn(kc), rhsT[:, kc, :], kc == 0, kc == nkc - 1, [lhs_trk, t_rhs[kc]], [t_bk])
        return bk, t_bk

    def rstd_from(self, parts, dim, out, t_out, np_out=128):
        K, C = self.K, self.C
        bk, t_bk = C.bank()
        n = len(parts)
        for i, (ap, t, kp) in enumerate(parts):
            K.mm(bk[:, :], C.ones[0:kp, :], ap, i == 0, i == n - 1, [C.t_ones, t], [t_bk])
        K.act(out, bk[0:np_out, :], AF.Ln, [t_bk], [t_out], bias=EPS, scale=1.0 / dim)
        K.act(out, out, AF.Exp, [t_out], [t_out], scale=-0.5)

    def front(self, x_all, g):
        C = self.C
        load_tokmajor_T(C, x_all[g * T:(g + 1) * T, :], D, self.xT, self.t_x, self.stage)
        rmsnorm_fm(C, self.xT, self.t_x, 16, self.vec[:, 0:16], self.t_vec, self.uT, self.t_u,
                   self.sq, self.t_sq, self.rstd, self.t_rstd, D)

    def mla_kv(self, g):
        K, C, v = self.K, self.C, self.vec
        NOT = K.trk("none")
        self.rope_tables(g)
        sqs = []
        ckv = []
        for c in range(4):
            w, t_w = self.W.load(self.scr["win"], NOT, 36 + c)
            bk, t_bk = self.proj_fm(lambda kc, w=w: w[:, kc, :], t_w, 16, self.uT, self.t_u)
            cf, t_cf = self.work()
            ckv.append((cf, t_cf))
            K.copy("act", cf[:, :], bk[:, :], [t_bk], [t_cf])
            K.act(self.sq4[:, c, :], bk[:, :], AF.Square, [t_bk], [self.t_sq4[c]])
            sqs.append((self.sq4[:, c, :], self.t_sq4[c], 128))
        self.rstd_from(sqs, 512, self.rstd[:, :], self.t_rstd)
        for c in range(4):
            K.stt(self.kvn[:, c, :], ckv[c][0][:, :], v[:, 36 + c:37 + c], self.rstd[:, :], ALU.mult, ALU.mult,
                  [ckv[c][1], self.t_vec, self.t_rstd], [self.t_kvn[c]])
        wk_, t_wk_ = self.W.load(self.scr["kpe"], NOT, 0)
        for i in range(2):
            bk, t_bk = self.proj_fm(lambda kc, i=i: wk_[:, kc, i * 64:(i + 1) * 64], t_wk_, 16,
                                    self.uT, self.t_u, np_out=64)
            K.copy("act", self.kpe[:, i, :], bk[0:64, :], [t_bk], [self.t_kpe])
        K.act(self.sqpe[:, :], self.kpe[:, 0, :], AF.Square, [self.t_kpe], [self.t_sqpe])
        tmp, t_tmp = self.work()
        K.stt(self.KR[:, :], self.kpe[:, 0, :], v[0:64, 44:45], self.cos[:, :], ALU.mult, ALU.mult,
              [self.t_kpe, self.t_vec, self.t_cs], [self.t_KR])
        K.stt(tmp[0:64, :], self.kpe[:, 1, :], v[0:64, 45:46], self.sin[:, :], ALU.mult, ALU.mult,
              [self.t_kpe, self.t_vec, self.t_cs], [t_tmp])
        K.tt("pool", self.KR[:, :], self.KR[:, :], tmp[0:64, :], ALU.add, [self.t_KR, t_tmp], [self.t_KR])
        ukn, t_ukn = self.WS.load(self.scr["small"], NOT, 0)
        for h in range(8):
            bk, t_bk = self.proj_fm(lambda kc, h=h: ukn[:, kc, h * 128:(h + 1) * 128], t_ukn, 4,
                                    self.kvn, self.t_kvn)
            kf, t_kf = self.work()
            K.copy("act", kf[:, :], bk[:, :], [t_bk], [t_kf])
            sqn, t_sqn = self.workb()
            K.act(sqn[:, :], bk[:, :], AF.Square, [t_bk], [t_sqn])
            rk, t_rk = self.work()
            self.rstd_from([(sqn[:, :], t_sqn, 128), (self.sqpe[:, :], self.t_sqpe, 64)], 192, rk[:, :], t_rk)
            kb, t_kb = self.workb()
            K.stt(kb[:, :], kf[:, :], v[:, 43:44], rk[:, :], ALU.mult, ALU.mult, [t_kf, self.t_vec, t_rk], [t_kb])
            K.dma("pool", "kns%d" % (h % 4), self.Kn[h, :, g * T:(g + 1) * T], kb[:, :], reads=[t_kb],
                  writes=[self.t_kn[g][h]])
            krb, t_krb = self.workb()
            K.tt("pool", krb[0:64, :], self.KR[:, :], rk[0:64, :], ALU.mult, [self.t_KR, t_rk], [t_krb])
            K.dma("pool", "krs%d" % (h % 4), self.Kr[h, :, g * T:(g + 1) * T], krb[0:64, :], reads=[t_krb],
                  writes=[self.t_kr[g][h]])
        uv, t_uv = self.WS.load(self.scr["small"], NOT, 1)
        for tb in range(4):
            for half in range(2):
                bk, t_bk = C.bank()
                for kc in range(4):
                    K.mm(bk[:, :], self.kvn[:, kc, tb * 128:(tb + 1) * 128], uv[:, kc, half * 512:(half + 1) * 512],
                         kc == 0, kc == 3, [self.t_kvn[kc], t_uv], [t_bk])
                K.copy(("act", "dve")[half], self.vtm[:, tb, half * 512:(half + 1) * 512], bk[:, :], [t_bk],
                       [self.t_vtm[tb]])
            K.dma("pool", "vs%d" % tb, self.Vs[g * T + tb * 128:g * T + (tb + 1) * 128, :], self.vtm[:, tb, :],
                  reads=[self.t_vtm[tb]], writes=[self.t_vs[g][tb]])

    def hgrn(self, own, vflag_col):
        K, C, v = self.K, self.C, self.vec
        NOT = K.trk("none")
        for qd_ in range(4):
            w, t_w = self.WH.load(self.scr["hi"], NOT, qd_)
            for tb in range(4):
                bk, t_bk = C.bank()
                for kc in range(16):
                    K.mm(bk[:, 0:256], self.uT[:, kc, tb * 128:(tb + 1) * 128], w[:, kc, :], kc == 0, kc == 15,
                         [self.t_u[kc], t_w], [t_bk])
                if vflag_col is None:
                    K.copy("act", self.hv[:, tb, qd_ * 256:(qd_ + 1) * 256], bk[:, 0:256], [t_bk], [self.t_hv[tb]])
                else:
                    K.ts("dve", self.hv[:, tb, qd_ * 256:(qd_ + 1) * 256], bk[:, 0:256], vflag_col, None, ALU.mult, None,
                         [t_bk, self.t_vec], [self.t_hv[tb]])
        def hg_proj(h):
            w, t_w = self.W.load(self.scr["win"], NOT, 8 + h)
            bk, t_bk = self.proj_fm(lambda kc, w=w: w[:, kc, :], t_w, 16, self.uT, self.t_u)
            f, t_f = self.fbuf[h % 2], self.t_fbuf[h % 2]
            K.act(f[:, :], bk[:, :], AF.Sigmoid, [t_bk], [t_f])
            return f, t_f

        nxt = hg_proj(0)
        for h in range(8):
            par = h % 2
            f, t_f = nxt
            if h < 7:
                nxt = hg_proj(h + 1)
            K.ts("dve", f[:, :], f[:, :], self.oml[:, h:h + 1], self.lb[:, h:h + 1], ALU.mult, ALU.add,
                 [t_f, self.t_vec], [t_f])
            b, t_b = self.work()
            K.act(b[:, :], f[:, :], AF.Ln, [t_f], [t_b])
            K.op("dve", lambda e, b=b: e.tensor_tensor_scan(b[:, :], self.rmask[:, :], b[:, :], 0.0, ALU.mult, ALU.add),
                 [t_b, self.t_cst], [t_b])
            K.ts("pool", f[:, :], f[:, :], -1.0, 1.0, ALU.mult, ALU.add, [t_f], [t_f])
            b3 = b[:, :].rearrange("p (c t) -> p c t", c=4)
            e4, t_e4 = self.work()
            for c in range(4):
                K.act(e4[:, c * 128:(c + 1) * 128], b[:, c * 128:(c + 1) * 128], AF.Exp, [t_b], [t_e4],
                      bias=b[:, c * 128 + 127:c * 128 + 128], scale=-1.0)
            K.tt("pool", e4[:, :], e4[:, :], f[:, :], ALU.mult, [t_e4, t_f], [t_e4])
            bkt, t_bkt = C.bank()
            for c in range(4):
                K.tr(bkt[:, c * 128:(c + 1) * 128], e4[:, c * 128:(c + 1) * 128], C.ident[:], [t_e4, C.t_ident],
                     [t_bkt], inc=(c == 3))
            K.copy("act", self.kktm[:, par, :, :].rearrange("p c k -> p (c k)"), bkt[:, :], [t_bkt], [self.t_kktm[par]])
            K.act(self.ebl[:, par, :], b3[:, :, 127], AF.Exp, [t_b], [self.t_kktm[par]])
            if own:
                wq, t_wq = self.W.load(self.scr["win"], NOT, h)
                bq, t_bq = self.proj_fm(lambda kc, wq=wq: wq[:, kc, :], t_wq, 16, self.uT, self.t_u)
                qf, t_qf = self.work()
                K.copy("act", qf[:, :], bq[:, :], [t_bq], [t_qf])
                e3, t_e3 = self.work()
                K.act(e3[:, :], b[:, :], AF.Exp, [t_b], [t_e3])
                qd, t_qd = self.workb()
                K.tt("dve", qd[:, :], qf[:, :], e3[:, :], ALU.mult, [t_qf, t_e3], [t_qd])
                e1, t_e1 = self.work()
                e2, t_e2 = self.work()
                nref, t_nref = self.work()
                K.ts("dve", nref[:, 0:4], b3[:, :, 63], -1.0, None, ALU.mult, None, [t_b], [t_nref])
                for c in range(4):
                    K.act(e1[:, c * 128:(c + 1) * 128], b[:, c * 128:(c + 1) * 128], AF.Exp, [t_b, t_nref], [t_e1],
                          bias=nref[:, c:c + 1], scale=1.0)
                    K.act(e2[:, c * 128:(c + 1) * 128], b[:, c * 128:(c + 1) * 128], AF.Exp, [t_b], [t_e2],
                          bias=b[:, c * 128 + 63:c * 128 + 64], scale=-1.0)
                qr, t_qr = self.workb()
                kr, t_kr = self.workb()
                K.tt("dve", qr[:, :], qf[:, :], e1[:, :], ALU.mult, [t_qf, t_e1], [t_qr])
                K.tt("pool", kr[:, :], f[:, :], e2[:, :], ALU.mult, [t_f, t_e2], [t_kr])
                bo, t_bo = C.bank(hold=True)
            for c in range(4):
                cs = slice(c * 128, (c + 1) * 128)
                if own:
                    ba, t_ba = C.bank()
                    K.mm(ba[:, 0:128], kr[:, cs], qr[:, cs], True, True, [t_kr, t_qr], [t_ba])
                    am, t_am = self.workb()
                    K.tt("dve", am[:, 0:128], ba[:, 0:128], self.tri[:, :], ALU.mult, [t_ba, self.t_cst], [t_am])
                    K.mm(bo[:, cs], self.hv[:, c, h * 128:(h + 1) * 128], am[:, 0:128], True, False,
                         [self.t_hv[c], t_am], [t_bo], inc=False)
                    K.mm(bo[:, cs], self.Sb[:, h, :], qd[:, cs], False, True, [self.t_Sb[h], t_qd], [t_bo], inc=True)
                bs, t_bs = C.bank()
                K.mm(bs[:, 0:128], self.kktm[:, par, c, :], self.hv[:, c, h * 128:(h + 1) * 128], True, True,
                     [self.t_kktm[par], self.t_hv[c]], [t_bs])
                K.stt(self.S[:, h, :], self.S[:, h, :], self.ebl[:, par, c:c + 1], bs[:, 0:128], ALU.mult, ALU.add,
                      [self.t_S[h], self.t_kktm[par], t_bs], [self.t_S[h]])
                K.copy("pool", self.Sb[:, h, :], self.S[:, h, :], [self.t_S[h]], [self.t_Sb[h]])
            if own:
                of, t_of = self.work()
                K.copy("act", of[:, :], bo[:, :], [t_bo], [t_of])
                sqo, t_sqo = self.workb()
                K.act(sqo[:, :], bo[:, :], AF.Square, [t_bo], [t_sqo])
                C.release(t_bo)
                ro, t_ro = self.work()
                self.rstd_from([(sqo[:, :], t_sqo, 128)], 128, ro[:, :], t_ro)
                wg_, t_wg = self.W.load(self.scr["win"], NOT, 24 + h)
                bg, t_bg = self.proj_fm(lambda kc, wg_=wg_: wg_[:, kc, :], t_wg, 16, self.uT, self.t_u)
                sg, t_sg = self.work()
                K.act(sg[:, :], bg[:, :], AF.Silu, [t_bg], [t_sg])
                K.stt(of[:, :], of[:, :], v[:, 46:47], ro[:, :], ALU.mult, ALU.mult, [t_of, self.t_vec, t_ro], [t_of])
                K.tt("pool", self.mixT[:, h, :], of[:, :], sg[:, :], ALU.mult, [t_of, t_sg], [self.t_mix[h]])

    def mla_q(self):
        K, C, v = self.K, self.C, self.vec
        NOT = K.trk("none")
        sqs = []
        cq = []
        for c in range(4):
            w, t_w = self.W.load(self.scr["win"], NOT, 32 + c)
            bk, t_bk = self.proj_fm(lambda kc, w=w: w[:, kc, :], t_w, 16, self.uT, self.t_u)
            cf, t_cf = self.work()
            cq.append((cf, t_cf))
            K.copy("act", cf[:, :], bk[:, :], [t_bk], [t_cf])
            K.act(self.sq4[:, c, :], bk[:, :], AF.Square, [t_bk], [self.t_sq4[c]])
            sqs.append((self.sq4[:, c, :], self.t_sq4[c], 128))
        self.rstd_from(sqs, 512, self.rstd[:, :], self.t_rstd)
        qn = self.kvn
        t_qn = self.t_kvn
        for c in range(4):
            K.stt(qn[:, c, :], cq[c][0][:, :], v[:, 32 + c:33 + c], self.rstd[:, :], ALU.mult, ALU.mult,
                  [cq[c][1], self.t_vec, self.t_rstd], [t_qn[c]])
        sc = float(192 ** -0.5)
        uqn, t_uqn = self.WS.load(self.scr["small"], NOT, 2)
        uqr, t_uqr = self.WS.load(self.scr["small"], NOT, 3)
        for h in range(8):
            bk, t_bk = self.proj_fm(lambda kc, h=h: uqn[:, kc, h * 128:(h + 1) * 128], t_uqn, 4, qn, t_qn)
            qf, t_qf = self.work()
            K.copy("act", qf[:, :], bk[:, :], [t_bk], [t_qf])
            sqn, t_sqn = self.workb()
            K.act(sqn[:, :], bk[:, :], AF.Square, [t_bk], [t_sqn])
            b1, t_b1 = self.proj_fm(lambda kc, h=h: uqr[:, kc, h * 64:(h + 1) * 64], t_uqr, 4, qn, t_qn, 64)
            b2, t_b2 = self.proj_fm(lambda kc, h=h: uqr[:, kc, 512 + h * 64:512 + (h + 1) * 64], t_uqr, 4, qn, t_qn, 64)
            sqr, t_sqr = self.workb()
            K.act(sqr[0:64, :], b1[0:64, :], AF.Square, [t_b1], [t_sqr])
            rq, t_rq = self.work()
            self.rstd_from([(sqn[:, :], t_sqn, 128), (sqr[0:64, :], t_sqr, 64)], 192, rq[:, :], t_rq)
            K.ts("pool", rq[:, :], rq[:, :], sc, None, ALU.mult, None, [t_rq], [t_rq])
            K.stt(self.Qn[:, h, :], qf[:, :], v[:, 40:41], rq[:, :], ALU.mult, ALU.mult, [t_qf, self.t_vec, t_rq],
                  [self.t_Q[h]])
            r1, t_r1 = self.work()
            r2, t_r2 = self.work()
            K.stt(r1[0:64, :], b1[0:64, :], v[0:64, 41:42], self.cos[:, :], ALU.mult, ALU.mult,
                  [t_b1, self.t_vec, self.t_cs], [t_r1])
            K.stt(r2[0:64, :], b2[0:64, :], v[0:64, 42:43], self.sin[:, :], ALU.mult, ALU.mult,
                  [t_b2, self.t_vec, self.t_cs], [t_r2])
            K.tt("pool", r1[0:64, :], r1[0:64, :], r2[0:64, :], ALU.add, [t_r1, t_r2], [t_r1])
            K.tt("pool", self.Qr[:, h, :], r1[0:64, :], rq[0:64, :], ALU.mult, [t_r1, t_rq], [self.t_Q[h]])

    def attention(self, g, n_prior_groups):
        K, C, v = self.K, self.C, self.vec
        for h in range(8):
            bo, t_bo = C.bank(hold=True)
            bd, t_bd = C.bank(hold=True)
            groups = list(range(0, g + 1))
            pend = None
            first = True
            blk = 0
            nblk = 4 * len(groups)
            for gi in groups:
                diag = (gi == g)
                slot = gi // SLOT_ST if gi < n_prior_groups else NSLOT
                bias = v[:, 65 + slot:66 + slot] if gi < n_prior_groups else v[:, 68:69]
                s = self.a_i % 3
                self.a_i += 1
                K.dma("sp", "aK%d" % s, self.aK[s][:, :], self.Kn[h, :, gi * T:(gi + 1) * T],
                      reads=[self.t_kn[gi][h]], writes=[self.t_aK[s]])
                K.dma("sp", "aKr%d" % s, self.aKr[s][:, :], self.Kr[h, :, gi * T:(gi + 1) * T],
                      reads=[self.t_kr[gi][h]], writes=[self.t_aKr[s]])
                K.dma("sp", "aV%d" % s, self.aV[s][:, :, :],
                      self.Vs[gi * T:(gi + 1) * T, h * 128:(h + 1) * 128].rearrange("(tb p) v -> p tb v", p=128),
                      reads=self.t_vs[gi], writes=[self.t_aV[s]])
                for kb in range(4):
                    q0 = kb * 128 if diag else 0
                    bs, t_bs = C.bank()
                    K.mm(bs[:, q0:T], self.aK[s][:, kb * 128:(kb + 1) * 128], self.Qn[:, h, q0:T], True, False,
                         [self.t_aK[s], self.t_Q[h]], [t_bs], inc=False)
                    K.mm(bs[:, q0:T], self.aKr[s][:, kb * 128:(kb + 1) * 128], self.Qr[:, h, q0:T], False, True,
                         [self.t_aKr[s], self.t_Q[h]], [t_bs], inc=True)
                    pi = self.p_i % 4
                    self.p_i += 1
                    P, t_P = self.aP[pi], self.t_aP[pi]
                    K.act(P[:, q0:T], bs[:, q0:T], AF.Exp, [t_bs, self.t_vec], [t_P], bias=bias, scale=1.0)
                    if diag:
                        K.tt("dve", P[:, q0:q0 + 128], P[:, q0:q0 + 128], self.trib[:, :], ALU.mult,
                             [t_P, self.t_cst], [t_P])
                    if pend is not None:
                        self._pv(*pend)
                    pend = (bo, t_bo, bd, t_bd, P, t_P, s, kb, q0, first, False, blk, nblk)
                    first = False
                    blk += 1
            lst = list(pend)
            lst[10] = True
            self._pv(*lst)
            dsb, t_dsb = self.work()
            K.copy("act", dsb[:, :], bd[:, :], [t_bd], [t_dsb])
            bt_, t_bt_ = C.bank()
            K.mm(bt_[:, :], self.sel4[:, :], dsb[:, :], True, True, [self.t_cst, t_dsb], [t_bt_])
            rd, t_rd = self.work()
            K.act(rd[:, :], bt_[:, :], AF.Ln, [t_bt_], [t_rd])
            K.act(rd[:, :], rd[:, :], AF.Exp, [t_rd], [t_rd], scale=-1.0)
            K.tt("dve", self.mixT[:, 8 + h, :], bo[:, :], rd[:, :], ALU.mult, [t_bo, t_rd], [self.t_mix[8 + h]])
            C.release(t_bo)
            C.release(t_bd)

    def _pv(self, bo, t_bo, bd, t_bd, P, t_P, s, kb, q0, first, last, blk, nblk):
        K, C = self.K, self.C
        K.mm(bo[:, q0:T], self.aV[s][:, kb, :], P[:, q0:T], first, last, [self.t_aV[s], t_P], [t_bo], inc=last)
        j = blk % 4
        K.op("pe", lambda e: e.matmul(bd[32 * j:32 * j + 32, q0:T], C.ones[:, 0:32], P[:, q0:T], start=(blk < 4),
                                      stop=(blk >= nblk - 4), tile_position=(0, 32 * j)),
             [C.t_ones, t_P], [t_bd], inc=True)

    def out_proj(self, x_all, g, h_mid, o):
        K, C = self.K, self.C
        NOT = K.trk("none")
        for n in range(16):
            w, t_w = self.W.load(self.scr["wout"], NOT, n)
            bk, t_bk = self.proj_fm(lambda kc, w=w: w[:, kc, :], t_w, 16, self.mixT, self.t_mix)
            K.copy("act", self.xT[:, n, :], bk[:, :], [t_bk], [self.t_x[n]])
        for tb in range(4):
            sb_, t_sb, key = self.stage.next()
            K.dma("sp", key, sb_[:, :], x_all[g * T + tb * 128:g * T + (tb + 1) * 128, :], writes=[t_sb])
            for c4 in range(4):
                bk, t_bk = C.bank()
                for cc in range(4):
                    kc = c4 * 4 + cc
                    K.tr(bk[:, cc * 128:(cc + 1) * 128], self.xT[:, kc, tb * 128:(tb + 1) * 128], C.ident[:],
                         [self.t_x[kc], C.t_ident], [t_bk], inc=(cc == 3))
                K.tt("dve", sb_[:, c4 * 512:(c4 + 1) * 512], bk[:, :], sb_[:, c4 * 512:(c4 + 1) * 512], ALU.add,
                     [t_bk, t_sb], [t_sb])
            K.dma("pool", key + "s", h_mid[o * T + tb * 128:o * T + (tb + 1) * 128, :], sb_[:, :], reads=[t_sb], writes=[])


def build_phaseA(nc):
    stack = contextlib.ExitStack()
    with stack:
        K = Kern(nc, stack)
        NP = NSLOT * SLOT_ST
        ntok = (NP + O_ST) * T
        x_all = dram_in(nc, "x_all", [ntok, D])
        pos_all = dram_in(nc, "pos_all", [1, ntok], I32)
        w_in = dram_in(nc, "w_in", [D, 5184])
        w_uq = dram_in(nc, "w_uq", [512, 1536])
        w_ukv = dram_in(nc, "w_ukv", [512, 2048])
        w_out = dram_in(nc, "w_out", [D, D])
        vecs = dram_in(nc, "vecsA", [128, NVA])
        h_mid = dram_out(nc, "h_mid", [O_ST * T, D])
        C = Ctx(K)
        A = PhaseA(K, C, nc)
        A.prologue(w_in, w_uq, w_ukv, w_out)
        A.alloc(vecs, pos_all, ntok)
        for g in range(NP):
            A.front(x_all, g)
            A.mla_kv(g)
            A.hgrn(False, A.vec[:, 69 + g // SLOT_ST:70 + g // SLOT_ST])
        for o in range(O_ST):
            g = NP + o
            A.front(x_all, g)
            A.mla_kv(g)
            A.hgrn(True, None)
            A.mla_q()
            A.attention(g, NP)
            A.out_proj(x_all, g, h_mid, o)
        K.finish()
        K.replay()
    return nc


def phaseA_inputs(inp, b, q):
    NP = NSLOT * SLOT_ST * T
    NO = O_ST * T
    x = inp["x"][b]
    pos = inp["positions"][b]
    x_all = np.empty((NP + NO, D), np.float32)
    pos_all = np.empty((1, NP + NO), np.int32)
    for i in range(NSLOT):
        src = max(q - NSLOT + i, 0)
        x_all[i * SEG:(i + 1) * SEG] = x[src * SEG:(src + 1) * SEG]
        pos_all[0, i * SEG:(i + 1) * SEG] = pos[src * SEG:(src + 1) * SEG]
    x_all[NP:] = x[q * SEG:(q + 1) * SEG]
    pos_all[0, NP:] = pos[q * SEG:(q + 1) * SEG]
    return {"x_all": x_all, "pos_all": pos_all, "w_in": inp["e_w_in"][0], "w_uq": inp["e_w_uq"][0],
            "w_ukv": inp["e_w_ukv"][0], "w_out": inp["e_w_out"][0], "vecsA": vecsA_host(inp, q)}


RH = 8
NRANK = 4
NVC = 32


def ret_tables():
    g = 1.0 - 2.0 ** (-5.0 - np.arange(8, dtype=np.float64))
    idx = np.arange(128, dtype=np.float64)
    DT = np.zeros((128, 8, 128), np.float32)
    for h in range(8):
        diff = idx[None, :] - idx[:, None]
        DT[:, h, :] = np.where(diff >= 0, g[h] ** np.maximum(diff, 0), 0.0) / 16.0
    decq = np.zeros((128, 8, 128), np.float32)
    for h in range(8):
        decq[:, h, :] = (g[h] ** (idx + 1.0))[None, :]
    kdec = np.zeros((128, 8), np.float32)
    for h in range(8):
        kdec[:, h] = g[h] ** (127.0 - idx) / 16.0
    cdec = [float(g[h] ** 128.0) for h in range(8)]
    return DT, decq, kdec, cdec, g


def vecsC_host(inp, q):
    v = np.zeros((128, 16 + 1 + NVC), np.float32)
    v[:, 0:16] = inp["norm_mix"][1].reshape(16, 128).T
    v[:, 16] = (10000.0 ** (-np.arange(128, dtype=np.float32) / np.float32(128))).astype(np.float32)
    g = 1.0 - 2.0 ** (-5.0 - np.arange(8, dtype=np.float64))
    for r in range(NRANK):
        for h in range(8):
            v[:, 17 + r * 8 + h] = float(g[h] ** (float(SEG) * (q - 1 - r))) if r < q else 0.0
    return v


class PhaseC:
    def __init__(self, K, C, nc, full, kvs=None):
        self.K, self.C, self.nc, self.full, self.kvs = K, C, nc, full, kvs

    def prologue(self, w_in, w_out, scr0=None):
        K = self.K
        with contextlib.ExitStack() as st2:
            old = K.stack
            K.stack = st2
            WC = WCast(K, nslots=3)
            scr = dict(scr0) if scr0 else {}
            if "k" not in scr and not (self.full and self.kvs is not None):
                scr["k"] = WC.cast(w_in[:, 2048:4096], D, 2048, "sC_k", cw=128)
                scr["v"] = WC.cast(w_in[:, 4096:8192], D, 4096, "sC_v", cw=256)
            if self.full:
                scr["q"] = WC.cast(w_in[:, 0:2048], D, 2048, "sC_q", cw=128)
                scr["g"] = WC.cast(w_in[:, 8192:12288], D, 4096, "sC_g", cw=128)
                scr["wout"] = WC.cast(w_out, 4096, D, "sC_wout", cw=128)
            K.barrier()
            K.stack = old
        self.scr = scr

    def alloc(self, vecs, pos_own):
        K, C, nc, full = self.K, self.C, self.nc, self.full
        self.vec = K.sb("vecC", [128, 17 + NVC], F32)
        self.t_vec = K.trk("vecC")
        K.dma("sp", "const", self.vec[:], vecs[:, :], writes=[self.t_vec])
        DT, decq, kdec, cdec, _ = ret_tables()
        self.cdec = cdec
        self.t_cst = K.trk("cstC")
        self.kdec = K.sb("kdec", [128, 8], F32)
        K.dma("sp", "const", self.kdec[:], nc.inline_tensor(kdec, name="kdec_d%d" % int(full)).ap()[:, :], writes=[self.t_cst])
        if full:
            self.DT = K.sb("DT", [128, 8, 128], F32)
            self.decq = K.sb("decq", [128, 8, 128], F32)
            K.dma("sp", "const", self.DT[:], nc.inline_tensor(DT, name="DT_d%d" % int(full)).ap()[:, :, :], writes=[self.t_cst])
            K.dma("sp", "const", self.decq[:], nc.inline_tensor(decq, name="decq_d%d" % int(full)).ap()[:, :, :], writes=[self.t_cst])
        self.pos = pos_own
        self.xT = K.sb("hTc", [128, 16, T], F32)
        self.t_x = [K.trk("x") for _ in range(16)]
        self.wk_i = 0
        self.uT = K.sb("uTc", [128, 16, T], BF16)
        self.t_u = [K.trk("u") for _ in range(16)]
        self.rstd = K.sb("rstdC", [128, T], F32)
        self.t_rstd = K.trk("rstd")
        self.stage = Stage(K, "stgC", D, 2 if not full else 1)
        self.W = WStream(K, "wC", 16, 128, 2 if full else 3)
        self.use_kvs = self.kvs is not None
        nb_ = 2 if (full and self.use_kvs) else 1
        if not (full and self.use_kvs):
            self.WV = WStream(K, "wCv", 16, 256, 2)
        self.posi = K.sb("posiC", [128, T], I32)
        self.cos = K.sb("cosC", [128, T], F32)
        self.sin = K.sb("sinC", [128, T], F32)
        self.t_rope = K.trk("rope")
        self.t_cs = K.trk("cossin")
        self.wb = [K.sb("wbC%d" % i, [128, T], BF16) for i in range(4)]
        self.t_wb = [K.trk("wb") for _ in range(4)]
        self.wb_i = 0
        self.R = K.sb("R", [128, 8, 2, 512], F32)
        self.t_R = [[K.trk("R") for _ in range(2)] for _ in range(8)]
        self.vtm_b = [K.sb("vtmC%d" % i, [128, 4, 512], BF16) for i in range(nb_)]
        self.t_vtm_b = [[K.trk("vtm") for _ in range(4)] for _ in range(nb_)]
        self.kdtm_b = [K.sb("kdtm%d" % i, [128, 4, 256], BF16) for i in range(nb_)]
        self.t_kdtm_b = [[K.trk("kdtm") for _ in range(2)] for _ in range(nb_)]
        self.kr_b = [K.sb("krC%d" % i, [128, 2, T], BF16) for i in range(nb_)] if (full or self.use_kvs) else None
        self.t_kr_b = [K.trk("kr") for _ in range(nb_)]
        self.hb_i = 0
        if full:
            self.Rb = K.sb("Rb", [128, 2, 2, 512], BF16)
            self.t_Rb = [[K.trk("Rb") for _ in range(2)] for _ in range(2)]
            self.onT = K.sb("onT", [128, 32, T], BF16)
            self.t_on = [K.trk("on") for _ in range(32)]
            self.sq = self.onT
            self.t_sq = self.t_on
            self.WO = WStream(K, "wCo", 32, 128, 2)
            self.qr = K.sb("qrC", [128, 2, T], BF16)
            self.qd = K.sb("qdC", [128, 2, T], BF16)
            self.t_qr = K.trk("qr")
            self.t_qd = K.trk("qd")
            self.AD = [K.sb("AD%d" % i, [128, 128], BF16) for i in range(2)]
            self.t_AD = [K.trk("AD") for _ in range(2)]
        else:
            self.sq = K.sb("sqC", [128, 16, T], BF16)
            self.t_sq = [K.trk("sq") for _ in range(16)]

    def work(self):
        i = self.wk_i % 16
        self.wk_i += 1
        return self.xT[:, i, :], self.t_x[i]

    def workb(self):
        i = self.wb_i % 4
        self.wb_i += 1
        return self.wb[i], self.t_wb[i]

    def init_state(self, Rprev, t_Rprev=None):
        K = self.K
        self.t_Rprev = t_Rprev if t_Rprev is not None else K.trk("none")
        for h in range(8):
            for dc in range(2):
                if Rprev is None:
                    K.op("pool", lambda e, h=h, dc=dc: e.memset(self.R[:, h, dc, :], 0.0), writes=[self.t_R[h][dc]])
                    continue
                for i in range(NRANK - 1):
                    tmp, t_tmp = self.work()
                    src = Rprev[h // 2][i, h % 2, dc, :, :] if isinstance(Rprev, list) else Rprev[i, h, dc, :, :]
                    K.dma("sp", "rin%d" % (i % 2), tmp[:, :], src, reads=[self.t_Rprev], writes=[t_tmp])
                    c = self.vec[:, 17 + i * 8 + h:18 + i * 8 + h]
                    if i == 0:
                        K.ts("dve", self.R[:, h, dc, :], tmp[:, :], c, None, ALU.mult, None, [t_tmp, self.t_vec],
                             [self.t_R[h][dc]])
                    else:
                        K.stt(self.R[:, h, dc, :], tmp[:, :], c, self.R[:, h, dc, :], ALU.mult, ALU.add,
                              [t_tmp, self.t_vec, self.t_R[h][dc]], [self.t_R[h][dc]])

    def rope_tables(self, o):
        K, v = self.K, self.vec
        K.dma("sp", "posi", self.posi[:, :], self.pos[0:1, o * T:(o + 1) * T].broadcast_to([128, T]),
              writes=[self.t_rope])
        posf, t_pf = self.work()
        ra, t_ra = self.work()
        rn, t_rn = self.work()
        K.copy("dve", posf, self.posi[:, :], [self.t_rope], [t_pf])
        for off, out in ((0.0, self.sin), (float(np.pi / 2), self.cos)):
            K.ts("dve", ra, posf, v[:, 16:17], off, ALU.mult, ALU.add, [t_pf, self.t_vec], [t_ra])
            K.ts("dve", rn, ra, float(1.0 / TWO_PI), MAGIC, ALU.mult, ALU.add, [t_ra], [t_rn])
            K.ts("dve", rn, rn, -MAGIC, None, ALU.add, None, [t_rn], [t_rn])
            K.stt(ra, rn, -C1, ra, ALU.mult, ALU.add, [t_rn, t_ra], [t_ra])
            K.stt(ra, rn, -C2, ra, ALU.mult, ALU.add, [t_rn, t_ra], [t_ra])
            K.ts("dve", ra, ra, -PI_LO, PI_LO, ALU.max, ALU.min, [t_ra], [t_ra])
            K.act(out[:, :], ra, AF.Sin, [t_ra], [self.t_cs])

    def proj_fm(self, w, t_w, rhsT, t_rhs, nkc=16):
        K, C = self.K, self.C
        bk, t_bk = C.bank()
        for kc in range(nkc):
            K.mm(bk[:, :], w[:, kc, :], rhsT[:, kc, :], kc == 0, kc == nkc - 1, [t_w, t_rhs[kc]], [t_bk])
        return bk, t_bk

    def rope_pair(self, b1, t_b1, b2, t_b2, out1, out2, t_outs, f32=False):
        K = self.K
        a, t_a = self.work()
        b, t_b = self.work()
        K.tt("dve", a, b1[:, :], self.cos[:, :], ALU.mult, [t_b1, self.t_cs], [t_a])
        K.tt("dve", b, b2[:, :], self.sin[:, :], ALU.mult, [t_b2, self.t_cs], [t_b])
        K.tt("pool", out1, a, b, ALU.subtract, [t_a, t_b], t_outs)
        c, t_c = self.work()
        d, t_d = self.work()
        K.tt("dve", c, b2[:, :], self.cos[:, :], ALU.mult, [t_b2, self.t_cs], [t_c])
        K.tt("dve", d, b1[:, :], self.sin[:, :], ALU.mult, [t_b1, self.t_cs], [t_d])
        K.tt("pool", out2, c, d, ALU.add, [t_c, t_d], t_outs)

    def supertile(self, h_in, o, h_out=None):
        K, C, full = self.K, self.C, self.full
        NOT = K.trk("none")
        load_tokmajor_T(C, h_in[o * T:(o + 1) * T, :], D, self.xT, self.t_x, self.stage)
        rmsnorm_fm(C, self.xT, self.t_x, 16, self.vec[:, 0:16], self.t_vec, self.uT, self.t_u,
                   self.sq, self.t_sq, self.rstd, self.t_rstd, D)
        self.rope_tables(o)
        for h in range(8):
            bi = self.hb_i % len(self.vtm_b)
            self.hb_i += 1
            self.vtm, self.t_vtm = self.vtm_b[bi], self.t_vtm_b[bi]
            self.kdtm, self.t_kdtm = self.kdtm_b[bi], self.t_kdtm_b[bi]
            if self.kr_b is not None:
                self.kr, self.t_kr = self.kr_b[bi], self.t_kr_b[bi]
            load_kv = full and self.use_kvs
            if load_kv:
                K.dma("sp", "kvl%d" % bi, self.kdtm[:, :, :], self.kvs["kd"][o, h], writes=self.t_kdtm)
                K.dma("sp", "kvv%d" % bi, self.vtm[:, :, :], self.kvs["v"][o, h], writes=self.t_vtm)
                K.dma("sp", "kvr%d" % bi, self.kr[:, :, :], self.kvs["kr"][o, h], writes=[self.t_kr])
            else:
                wk1, t_wk1 = self.W.load(self.scr["k"], NOT, 2 * h)
                b1, t_b1 = self.proj_fm(wk1, t_wk1, self.uT, self.t_u)
                wk2, t_wk2 = self.W.load(self.scr["k"], NOT, 2 * h + 1)
                b2, t_b2 = self.proj_fm(wk2, t_wk2, self.uT, self.t_u)
                k1, t_k1 = self.work()
                k2, t_k2 = self.work()
                self.rope_pair(b1, t_b1, b2, t_b2, k1, k2, [t_k1, t_k2])
                for half in range(2):
                    wv, t_wv = self.WV.load(self.scr["v"], NOT, 2 * h + half)
                    for tb in range(4):
                        bk, t_bk = C.bank()
                        for kc in range(16):
                            K.mm(bk[:, 0:256], self.uT[:, kc, tb * 128:(tb + 1) * 128], wv[:, kc, :], kc == 0, kc == 15,
                                 [self.t_u[kc], t_wv], [t_bk])
                        K.copy("act", self.vtm[:, tb, half * 256:(half + 1) * 256], bk[:, 0:256], [t_bk], [self.t_vtm[tb]])
            if full:
                wq1, t_wq1 = self.W.load(self.scr["q"], NOT, 2 * h)
                q1, t_q1 = self.proj_fm(wq1, t_wq1, self.uT, self.t_u)
                wq2, t_wq2 = self.W.load(self.scr["q"], NOT, 2 * h + 1)
                q2, t_q2 = self.proj_fm(wq2, t_wq2, self.uT, self.t_u)
                self.rope_pair(q1, t_q1, q2, t_q2, self.qr[:, 0, :], self.qr[:, 1, :], [self.t_qr])
                dq = self.decq[:, h:h + 1, :].broadcast_to([128, 4, 128])
                for dc in range(2):
                    K.tt("dve", self.qd[:, dc, :].rearrange("p (c t) -> p c t", c=4),
                         self.qr[:, dc, :].rearrange("p (c t) -> p c t", c=4), dq, ALU.mult,
                         [self.t_qr, self.t_cst], [self.t_qd])
            for dc, (kf, t_kf) in (enumerate(((k1, t_k1), (k2, t_k2))) if not load_kv else ()):
                bt, t_bt = C.bank()
                for c in range(4):
                    K.tr(bt[:, c * 128:(c + 1) * 128], kf[:, c * 128:(c + 1) * 128], C.ident[:], [t_kf, C.t_ident],
                         [t_bt], inc=(c == 3))
                K.ts("dve", self.kdtm[:, :, dc * 128:(dc + 1) * 128], bt[:, :].rearrange("p (c d) -> p c d", c=4),
                     self.kdec[:, h:h + 1], None, ALU.mult, None, [t_bt, self.t_cst], [self.t_kdtm[dc]])
                if self.kr_b is not None:
                    K.copy("act", self.kr[:, dc, :], kf, [t_kf], [self.t_kr])
            if self.use_kvs and not full:
                K.dma("pool", "kvs0", self.kvs["kd"][o, h], self.kdtm[:, :, :], reads=self.t_kdtm, writes=[])
                K.dma("pool", "kvs1", self.kvs["v"][o, h], self.vtm[:, :, :], reads=self.t_vtm, writes=[])
                K.dma("pool", "kvs2", self.kvs["kr"][o, h], self.kr[:, :, :], reads=[self.t_kr], writes=[])
            if full:
                bo = [C.bank(hold=True) for _ in range(4)]
                for dc in range(2):
                    K.copy("pool", self.Rb[:, h % 2, dc, :], self.R[:, h, dc, :], [self.t_R[h][dc]], [self.t_Rb[h % 2][dc]])
            for c in range(4):
                cs = slice(c * 128, (c + 1) * 128)
                if full:
                    ba, t_ba = C.bank()
                    for dc in range(2):
                        K.mm(ba[:, 0:128], self.kr[:, dc, cs], self.qr[:, dc, cs], dc == 0, dc == 1,
                             [self.t_kr, self.t_qr], [t_ba])
                    ad, t_ad = self.AD[c % 2], self.t_AD[c % 2]
                    K.tt("dve", ad[:, :], ba[:, 0:128], self.DT[:, h, :], ALU.mult, [t_ba, self.t_cst], [t_ad])
                    for vc in range(4):
                        bov, t_bov = bo[vc]
                        K.mm(bov[:, cs], self.vtm[:, c, vc * 128:(vc + 1) * 128], ad[:, :], True, False,
                             [self.t_vtm[c], t_ad], [t_bov], inc=False)
                        for dc in range(2):
                            K.mm(bov[:, cs], self.Rb[:, h % 2, dc, vc * 128:(vc + 1) * 128], self.qd[:, dc, cs], False, dc == 1,
                                 [self.t_Rb[h % 2][dc], self.t_qd], [t_bov], inc=(dc == 1))
                for dc in range(2):
                    bs, t_bs = C.bank()
                    K.mm(bs[:, :], self.kdtm[:, c, dc * 128:(dc + 1) * 128], self.vtm[:, c, :], True, True,
                         [self.t_kdtm[dc], self.t_vtm[c]], [t_bs])
                    K.stt(self.R[:, h, dc, :], self.R[:, h, dc, :], self.cdec[h], bs[:, :], ALU.mult, ALU.add,
                          [self.t_R[h][dc], t_bs], [self.t_R[h][dc]])
                    if full and c < 3:
                        K.copy("pool", self.Rb[:, h % 2, dc, :], self.R[:, h, dc, :], [self.t_R[h][dc]], [self.t_Rb[h % 2][dc]])
            if full:
                ofs = []
                sqs = []
                for vc in range(4):
                    bov, t_bov = bo[vc]
                    of, t_of = self.work()
                    K.copy("act", of, bov[:, :], [t_bov], [t_of])
                    sqb, t_sqb = self.workb()
                    K.act(sqb[:, :], bov[:, :], AF.Square, [t_bov], [t_sqb])
                    C.release(t_bov)
                    ofs.append((of, t_of))
                    sqs.append((sqb, t_sqb))
                bk, t_bk = C.bank()
                for vc in range(4):
                    K.mm(bk[:, :], C.ones[:, :], sqs[vc][0][:, :], vc == 0, vc == 3, [C.t_ones, sqs[vc][1]], [t_bk])
                ro, t_ro = self.work()
                K.act(ro, bk[:, :], AF.Ln, [t_bk], [t_ro], bias=EPS, scale=1.0 / 512)
                K.act(ro, ro, AF.Exp, [t_ro], [t_ro], scale=-0.5)
                for vc in range(4):
                    wg_, t_wg = self.W.load(self.scr["g"], NOT, 4 * h + vc)
                    bg, t_bg = self.proj_fm(wg_, t_wg, self.uT, self.t_u)
                    sg, t_sg = self.work()
                    K.act(sg, bg[:, :], AF.Silu, [t_bg], [t_sg])
                    of, t_of = ofs[vc]
                    K.tt("dve", of, of, ro, ALU.mult, [t_of, t_ro], [t_of])
                    K.tt("pool", self.onT[:, 4 * h + vc, :], of, sg, ALU.mult, [t_of, t_sg], [self.t_on[4 * h + vc]])
        if full:
            for n in range(16):
                w, t_w = self.WO.load(self.scr["wout"], NOT, n)
                bk, t_bk = self.proj_fm(w, t_w, self.onT, self.t_on, nkc=32)
                K.copy("act", self.xT[:, n, :], bk[:, :], [t_bk], [self.t_x[n]])
            for tb in range(4):
                sb_, t_sb, key = self.stage.next()
                K.dma("sp", key, sb_[:, :], h_in[o * T + tb * 128:o * T + (tb + 1) * 128, :], writes=[t_sb])
                for c4 in range(4):
                    bk, t_bk = C.bank()
                    for cc in range(4):
                        kc = c4 * 4 + cc
                        K.tr(bk[:, cc * 128:(cc + 1) * 128], self.xT[:, kc, tb * 128:(tb + 1) * 128], C.ident[:],
                             [self.t_x[kc], C.t_ident], [t_bk], inc=(cc == 3))
                    K.tt("dve", sb_[:, c4 * 512:(c4 + 1) * 512], bk[:, :], sb_[:, c4 * 512:(c4 + 1) * 512], ALU.add,
                         [t_bk, t_sb], [t_sb])
                K.dma("pool", key + "s", h_out[o * T + tb * 128:o * T + (tb + 1) * 128, :], sb_[:, :], reads=[t_sb],
                      writes=[])

    def store_state(self, R_out):
        K = self.K
        for h in range(8):
            for dc in range(2):
                dst = R_out[h // 2][h % 2, dc, :, :] if isinstance(R_out, list) else R_out[h, dc, :, :]
                K.dma("pool", "rout%d" % dc, dst, self.R[:, h, dc, :], reads=[self.t_R[h][dc]], writes=[])


def build_phaseC(nc, full):
    stack = contextlib.ExitStack()
    with stack:
        K = Kern(nc, stack)
        h_in = dram_in(nc, "h_in", [O_ST * T, D])
        pos = dram_in(nc, "pos", [1, O_ST * T], I32)
        w_in = dram_in(nc, "w_in", [D, 12288])
        vecs = dram_in(nc, "vecsC", [128, 17 + NVC])
        if full:
            w_out = dram_in(nc, "w_out", [4096, D])
            Rprev = dram_in(nc, "Rprev", [NRANK, 8, 2, 128, 512])
            h_out = dram_out(nc, "h_mid", [O_ST * T, D])
        else:
            w_out = None
            R_out = dram_out(nc, "R_out", [8, 2, 128, 512])
        C = Ctx(K)
        P = PhaseC(K, C, nc, full)
        P.prologue(w_in, w_out)
        P.alloc(vecs, pos)
        P.init_state(Rprev if full else None)
        for o in range(O_ST):
            P.supertile(h_in, o, h_out if full else None)
        if not full:
            P.store_state(R_out)
        K.finish()
        K.replay()
    return nc


def _launch(build, maps):
    nc = bass.Bass("TRN2", target_bir_lowering=False)
    build(nc)
    res = run_bass_kernel_spmd(nc, maps, core_ids=list(range(NCORE)))
    return res.results


def _ffn_maps(inp, li, hs):
    maps = []
    vec = ffn_vecs(inp["norm_ffn"][li], inp["norm_ple"][li], inp["ffn_conv_w"][li], inp["ffn_conv_b"][li])
    for c in range(NCORE):
        b, q = c // 4, c % 4
        hprev = np.ascontiguousarray(hs[c - 1][-2:, :]) if q > 0 else np.zeros((2, D), np.float32)
        maps.append({"h_in": hs[c], "hprev": hprev,
                     "p_in": np.ascontiguousarray(inp["p"][li, b, q * SEG:(q + 1) * SEG]),
                     "w_gate": inp["ffn_w_gate"][li], "w_up": inp["ffn_w_up"][li], "w_down": inp["ffn_w_down"][li],
                     "w_pg": inp["ple_w_gate"][li], "w_pp": inp["ple_w_proj"][li], "vecs": vec})
    return maps


def kernel_unfused(**inputs):
    inp = {k: np.asarray(v) for k, v in inputs.items()}
    r = _launch(build_phaseA, [phaseA_inputs(inp, c // 4, c % 4) for c in range(NCORE)])
    hmid0 = [np.ascontiguousarray(x["h_mid"]) for x in r]
    r = _launch(build_ffn_phase, _ffn_maps(inp, 0, hmid0))
    h1 = [np.ascontiguousarray(x["h_out"]) for x in r]
    posm = [np.ascontiguousarray(inp["positions"][c // 4][None, (c % 4) * SEG:(c % 4 + 1) * SEG]) for c in range(NCORE)]
    w_in1 = inp["o_w_in"][0]
    r = _launch(lambda nc: build_phaseC(nc, False),
                [{"h_in": h1[c], "pos": posm[c], "w_in": w_in1, "vecsC": vecsC_host(inp, c % 4)} for c in range(NCORE)])
    Rl = [x["R_out"] for x in r]
    maps = []
    for c in range(NCORE):
        q = c % 4
        Rprev = np.zeros((NRANK, 8, 2, 128, 512), np.float32)
        for r_ in range(q):
            Rprev[r_] = Rl[c - q + r_]
        maps.append({"h_in": h1[c], "pos": posm[c], "w_in": w_in1, "w_out": inp["o_w_out"][0], "Rprev": Rprev,
                     "vecsC": vecsC_host(inp, q)})
    r = _launch(lambda nc: build_phaseC(nc, True), maps)
    hmid1 = [np.ascontiguousarray(x["h_mid"]) for x in r]
    r = _launch(build_ffn_phase, _ffn_maps(inp, 1, hmid1))
    out = np.empty((2, 4 * SEG, D), np.float32)
    for c in range(NCORE):
        out[c // 4, (c % 4) * SEG:(c % 4 + 1) * SEG] = r[c]["h_out"]
    return out


GROUPS = [[0, 1, 2, 3], [4, 5, 6, 7]]
DEBUG_OUT = False


def build_fused(nc):
    stack = contextlib.ExitStack()
    with stack:
        K = Kern(nc, stack)
        NP = NSLOT * SLOT_ST
        ntok = (NP + O_ST) * T
        x_all = dram_in(nc, "x_all", [ntok, D])
        pos_all = dram_in(nc, "pos_all", [1, ntok], I32)
        w_in = dram_in(nc, "w_in", [D, 5184])
        w_uq = dram_in(nc, "w_uq", [512, 1536])
        w_ukv = dram_in(nc, "w_ukv", [512, 2048])
        w_out = dram_in(nc, "w_out", [D, D])
        vecsA = dram_in(nc, "vecsA", [128, NVA])
        ffw = []
        for li in range(2):
            ffw.append(dict(
                p=dram_in(nc, "p%d" % li, [SEG, PLE]),
                gate=dram_in(nc, "w_gate%d" % li, [D, DFF]), up=dram_in(nc, "w_up%d" % li, [D, DFF]),
                down=dram_in(nc, "w_down%d" % li, [DFF, D]), pg=dram_in(nc, "w_pg%d" % li, [D, D]),
                pp=dram_in(nc, "w_pp%d" % li, [PLE, D]), vecs=dram_in(nc, "vecsF%d" % li, [128, NVF])))
        o_w_in = dram_in(nc, "o_w_in", [D, 12288])
        o_w_out = dram_in(nc, "o_w_out", [4096, D])
        vecsC = dram_in(nc, "vecsC", [128, 17 + NVC])
        out = dram_out(nc, "out", [SEG, D])
        internal = lambda name, shape: nc.dram_tensor(name, list(shape), F32, kind="Internal").ap()
        mk = (lambda name, shape: dram_out(nc, name, shape)) if DEBUG_OUT else internal
        hA = mk("hA", [SEG, D])
        hB = mk("hB", [SEG, D])
        hC = mk("hC", [SEG, D])
        hl = [internal("hl%d" % i, [2, D]) for i in range(2)]
        hg = [internal("hg%d" % i, [8, D]) for i in range(2)]
        Rloc = [internal("Rloc%d" % i, [512, 512]) for i in range(4)]
        Rall = [internal("Rall%d" % i, [NRANK * 512, 512]) for i in range(4)]
        C = Ctx(K)
        BYP = mybir.AluOpType.bypass

        with K.phase("A_"):
            A = PhaseA(K, C, nc)
            A.prologue(w_in, w_uq, w_ukv, w_out)
            A.alloc(vecsA, pos_all, ntok)
            for g in range(NP):
                A.front(x_all, g)
                A.mla_kv(g)
                A.hgrn(False, A.vec[:, 69 + g // SLOT_ST:70 + g // SLOT_ST])
            for o in range(O_ST):
                g = NP + o
                A.front(x_all, g)
                A.mla_kv(g)
                A.hgrn(True, None)
                A.mla_q()
                A.attention(g, NP)
                A.out_proj(x_all, g, hA, o)

        def exchange(i, hsrc):
            t_hl = K.trk("hl")
            K.dma("sp", "xch", hl[i][:, :], hsrc[SEG - 2:SEG, :], writes=[t_hl])
            t_g = K.trk("hg")
            K.barrier()
            K.collective("AllGather", BYP, GROUPS, hl[i], hg[i], reads=[t_hl], writes=[t_g])
            K.barrier()
            return t_g

        t_g0 = exchange(0, hA)
        f = ffw[0]
        with K.phase("B_"):
            scr = cast_ffn_weights(K, f["gate"], f["up"], f["down"], f["pg"], f["pp"], pfx="B")
            ffn_phase_body(K, C, hA, None, f["p"], f["vecs"], scr, hB, gath=hg[0], t_gath=t_g0)
        pos_own = pos_all[:, NP * T:NP * T + SEG]
        ibf = lambda name, shape: nc.dram_tensor(name, list(shape), BF16, kind="Internal").ap()
        kvs = {"kd": ibf("kvs_kd", [O_ST, 8, 128, 4, 256]), "v": ibf("kvs_v", [O_ST, 8, 128, 4, 512]),
               "kr": ibf("kvs_kr", [O_ST, 8, 128, 2, T])}
        with K.phase("C1_"):
            P1 = PhaseC(K, C, nc, False, kvs)
            P1.prologue(o_w_in, None)
            P1.alloc(vecsC, pos_own)
            P1.init_state(None)
            for o in range(O_ST):
                P1.supertile(hB, o)
            P1.store_state([Rloc[i].rearrange("(h dc p) n -> h dc p n", h=2, dc=2) for i in range(4)])
        t_R = K.trk("Rall")
        for i in range(4):
            K.collective("AllGather", BYP, GROUPS, Rloc[i], Rall[i], reads=[], writes=[t_R])
            K.barrier()
        with K.phase("C2_"):
            P2 = PhaseC(K, C, nc, True, kvs)
            P2.prologue(o_w_in, o_w_out, scr0=P1.scr)
            P2.alloc(vecsC, pos_own)
            P2.init_state([Rall[i].rearrange("(r h dc p) n -> r h dc p n", r=NRANK, h=2, dc=2) for i in range(4)], t_R)
            for o in range(O_ST):
                P2.supertile(hB, o, hC)
        t_g1 = exchange(1, hC)
        f = ffw[1]
        with K.phase("D_"):
            scr = cast_ffn_weights(K, f["gate"], f["up"], f["down"], f["pg"], f["pp"], pfx="D")
            ffn_phase_body(K, C, hC, None, f["p"], f["vecs"], scr, out, gath=hg[1], t_gath=t_g1)
        K.finish()
        K.replay()
    return nc


def fused_inputs(inp, c):
    b, q = c // 4, c % 4
    m = phaseA_inputs(inp, b, q)
    for li in range(2):
        m["p%d" % li] = np.ascontiguousarray(inp["p"][li, b, q * SEG:(q + 1) * SEG])
        m["w_gate%d" % li] = inp["ffn_w_gate"][li]
        m["w_up%d" % li] = inp["ffn_w_up"][li]
        m["w_down%d" % li] = inp["ffn_w_down"][li]
        m["w_pg%d" % li] = inp["ple_w_gate"][li]
        m["w_pp%d" % li] = inp["ple_w_proj"][li]
        m["vecsF%d" % li] = ffn_vecs(inp["norm_ffn"][li], inp["norm_ple"][li], inp["ffn_conv_w"][li],
                                      inp["ffn_conv_b"][li], q)
    m["o_w_in"] = inp["o_w_in"][0]
    m["o_w_out"] = inp["o_w_out"][0]
    m["vecsC"] = vecsC_host(inp, q)
    return m


def kernel(**inputs):
    inp = {k: np.asarray(v) for k, v in inputs.items()}
    nc = bass.Bass("TRN2", target_bir_lowering=False)
    build_fused(nc)
    maps = [fused_inputs(inp, c) for c in range(NCORE)]
    res = run_bass_kernel_spmd(nc, maps, core_ids=list(range(NCORE)))
    out = np.empty((2, 4 * SEG, D), np.float32)
    for c in range(NCORE):
        out[c // 4, (c % 4) * SEG:(c % 4 + 1) * SEG] = res.results[c]["out"]
    return out
```

```python
import contextlib
import numpy as np
import concourse.bass as bass
import concourse.mybir as mybir
from concourse.bass_utils import run_bass_kernel_spmd

F32 = mybir.dt.float32
BF16 = mybir.dt.bfloat16
I32 = mybir.dt.int32
AF = mybir.ActivationFunctionType
ALU = mybir.AluOpType

D = 2048
NCORE = 8
SEG = 4096
T = 512
NST = SEG // T
DFF = 5632
NFF = DFF // 128
PLE = 256
EPS = 1e-6


class Trk:
    __slots__ = ("name", "w", "r")

    def __init__(self, name):
        self.name = name
        self.w = None
        self.r = {}


class Stream:
    __slots__ = ("name", "sem", "cnt")

    def __init__(self, name, sem):
        self.name, self.sem, self.cnt = name, sem, 0


class Kern:
    ENG = ("pe", "act", "dve", "pool", "sp")

    def __init__(self, nc, stack):
        self.nc = nc
        self.stack = stack
        self.root = stack
        self.pfx = ""
        self.ncc = 0
        self.ops = {e: [] for e in self.ENG}
        self.streams = {}
        for e in ("pe", "act", "dve", "pool"):
            self.streams[e] = Stream(e, stack.enter_context(nc.semaphore("s_" + e)))
        self.seen = {e: {} for e in self.ENG}
        self.dma_streams = {}
        self.nbank = 0
        self.uid = 0

    def sb(self, name, shape, dt):
        return self.stack.enter_context(self.nc.sbuf_tensor(self.pfx + name, list(shape), dt))

    @contextlib.contextmanager
    def phase(self, pfx):
        old, oldp = self.stack, self.pfx
        with contextlib.ExitStack() as st2:
            self.stack = st2
            self.pfx = pfx
            yield
            self.barrier()
            self.stack, self.pfx = old, oldp

    def collective(self, kind, op, groups, in_ap, out_ap, reads=(), writes=()):
        key = "cc%d" % self.ncc
        self.ncc += 1
        st = Stream(key, self.root.enter_context(self.nc.semaphore("c_" + key)))
        self.dma_streams[key] = st
        deps = self._deps(reads, writes)
        self._emit_waits("pool", deps)
        st.cnt += 1
        self.ops["pool"].append(("ins", lambda e: e.collective_compute(kind, op, replica_groups=groups,
                                                                       ins=[in_ap.opt()], outs=[out_ap.opt()]),
                                 st.sem, None))
        for t in reads:
            if t.r.get(st, 0) < 1:
                t.r[st] = 1
        for t in writes:
            t.w = (st, 1)
            t.r = {}

    def psum_bank(self):
        t = self.stack.enter_context(self.nc.psum_tensor("bank%d" % self.nbank, [128, 512], F32))
        self.nbank += 1
        return t

    def trk(self, name="t"):
        self.uid += 1
        return Trk("%s%d" % (name, self.uid))

    def _deps(self, reads, writes):
        deps = {}
        for t in reads:
            if t.w is not None:
                s, v = t.w
                if deps.get(s, 0) < v:
                    deps[s] = v
        for t in writes:
            if t.w is not None:
                s, v = t.w
                if deps.get(s, 0) < v:
                    deps[s] = v
            for s, v in t.r.items():
                if deps.get(s, 0) < v:
                    deps[s] = v
        return deps

    def _emit_waits(self, eng, deps, own=None):
        seen = self.seen[eng]
        for s, v in deps.items():
            if s is own and eng == "pe":
                continue
            if seen.get(s, 0) >= v:
                continue
            seen[s] = v
            self.ops[eng].append(("wait", s.sem, v))

    def op(self, eng, fn, reads=(), writes=(), inc=True):
        st = self.streams[eng]
        deps = self._deps(reads, writes)
        self._emit_waits(eng, deps, own=st)
        if inc:
            st.cnt += 1
            val = st.cnt
            self.ops[eng].append(("ins", fn, st.sem, 1))
        else:
            val = st.cnt + 1
            self.ops[eng].append(("ins", fn, None, 0))
        for t in reads:
            if t.r.get(st, 0) < val:
                t.r[st] = val
        for t in writes:
            t.w = (st, val)
            t.r = {}

    def dma(self, q, key, out, in_, reads=(), writes=(), **kw):
        st = self.dma_streams.get(key)
        if st is None:
            st = Stream("dma_" + key, self.root.enter_context(self.nc.semaphore("d_" + key)))
            self.dma_streams[key] = st
        deps = self._deps(reads, writes)
        if st.cnt > 0:
            deps[st] = max(deps.get(st, 0), st.cnt)
        self._emit_waits(q, deps)
        st.cnt += 16
        val = st.cnt
        self.ops[q].append(("ins", lambda e, o=out, i=in_: e.dma_start(out=o, in_=i, **kw), st.sem, 16))
        for t in reads:
            if t.r.get(st, 0) < val:
                t.r[st] = val
        for t in writes:
            t.w = (st, val)
            t.r = {}

    def barrier(self):
        allst = list(self.streams.values()) + list(self.dma_streams.values())
        for e in self.ENG:
            for st in allst:
                if st.cnt > 0 and self.seen[e].get(st, 0) < st.cnt:
                    if e == "pe" and st is self.streams["pe"]:
                        continue
                    self.seen[e][st] = st.cnt
                    self.ops[e].append(("wait", st.sem, st.cnt))

    def finish(self):
        for st in self.dma_streams.values():
            if st.cnt > 0 and self.seen["sp"].get(st, 0) < st.cnt:
                self.ops["sp"].append(("wait", st.sem, st.cnt))
        for e in ("pe", "act", "dve", "pool"):
            st = self.streams[e]
            if st.cnt > 0:
                self.ops["sp"].append(("wait", st.sem, st.cnt))

    def replay(self):
        nc = self.nc
        with nc.Block() as block:
            def run(eng_obj, lst):
                for o in lst:
                    if o[0] == "wait":
                        eng_obj.wait_ge(o[1], o[2])
                    else:
                        ins = o[1](eng_obj)
                        if o[2] is not None:
                            if o[3] is None:
                                ins.then_inc(o[2])
                            else:
                                ins.then_inc(o[2], o[3])

            @block.tensor
            def _(e):
                run(e, self.ops["pe"])

            @block.scalar
            def _(e):
                run(e, self.ops["act"])

            @block.vector
            def _(e):
                run(e, self.ops["dve"])

            @block.gpsimd
            def _(e):
                run(e, self.ops["pool"])

            @block.sync
            def _(e):
                run(e, self.ops["sp"])

    def mm(self, out, lhsT, rhs, start, stop, reads, writes, inc=None):
        if inc is None:
            inc = stop
        self.op("pe", lambda e: e.matmul(out, lhsT, rhs, start=start, stop=stop), reads, writes, inc=inc)

    def tr(self, out, in_, ident, reads, writes, inc=True):
        self.op("pe", lambda e: e.transpose(out, in_, ident), reads, writes, inc=inc)

    def act(self, out, in_, func, reads, writes, bias=None, scale=None, eng="act"):
        kw = {}
        if bias is not None:
            kw["bias"] = bias
        if scale is not None:
            kw["scale"] = scale
        self.op(eng, lambda e: e.activation(out, in_, func, **kw), reads, writes)

    def tt(self, eng, out, in0, in1, op, reads, writes):
        self.op(eng, lambda e: e.tensor_tensor(out, in0, in1, op), reads, writes)

    def ts(self, eng, out, in0, s1, s2, op0, op1, reads, writes):
        if op1 is None:
            self.op(eng, lambda e: e.tensor_scalar(out, in0, s1, None, op0), reads, writes)
        else:
            self.op(eng, lambda e: e.tensor_scalar(out, in0, s1, s2, op0, op1), reads, writes)

    def stt(self, out, in0, scalar, in1, op0, op1, reads, writes):
        self.op("dve", lambda e: e.scalar_tensor_tensor(out, in0, scalar, in1, op0, op1), reads, writes)

    def copy(self, eng, out, in_, reads, writes):
        if eng == "act":
            self.op(eng, lambda e: e.copy(out, in_), reads, writes)
        else:
            self.op(eng, lambda e: e.tensor_copy(out, in_), reads, writes)


class Ctx:
    def __init__(self, K):
        self.K = K
        nc = K.nc
        self.banks = [K.psum_bank() for _ in range(8)]
        self.bank_trk = [K.trk("bank") for _ in range(8)]
        self.bank_i = 0
        self.held = set()
        self.ident = K.sb("ident", [128, 128], F32)
        self.t_ident = K.trk("ident")
        idd = nc.inline_tensor(np.eye(128, dtype=np.float32), name="ident_d").ap()
        K.dma("sp", "const", self.ident[:], idd[:, :], writes=[self.t_ident])
        self.ones_f = K.sb("ones_f", [128, 128], F32)
        self.ones = K.sb("ones_b", [128, 128], BF16)
        self.t_ones = K.trk("ones")
        K.op("pool", lambda e: e.memset(self.ones_f[:], 1.0), writes=[self.t_ones])
        K.copy("pool", self.ones[:], self.ones_f[:], [self.t_ones], [self.t_ones])

    def bank(self, hold=False):
        for _ in range(8):
            i = self.bank_i
            self.bank_i = (i + 1) % 8
            if i not in self.held:
                break
        else:
            raise RuntimeError("all PSUM banks held")
        if hold:
            self.held.add(i)
        return self.banks[i], self.bank_trk[i]

    def release(self, t_bk):
        self.held.discard(self.bank_trk.index(t_bk))


class WCast:
    def __init__(self, K, nslots=3):
        self.K = K
        self.n = nslots
        self.f = [K.sb("wc_f%d" % i, [128, 8, 512], F32) for i in range(nslots)]
        self.b = [K.sb("wc_b%d" % i, [128, 8, 512], BF16) for i in range(nslots)]
        self.cap = 8 * 512
        self.tf = [K.trk("wcf") for _ in range(nslots)]
        self.tb = [K.trk("wcb") for _ in range(nslots)]
        self.i = 0
        self.engs = ("pool", "dve", "act")

    def cast(self, w, Kd, N, name, cw=512):
        K = self.K
        nkc = Kd // 128
        nb = N // cw
        scr = K.nc.dram_tensor(name, [nb, 128, nkc, cw], BF16, kind="Internal").ap()
        wv = w.rearrange("(kc p) n -> p kc n", p=128)
        kstep = max(1, min(nkc, self.cap // cw, 8))
        for b in range(nb):
            for k0 in range(0, nkc, kstep):
                kn = min(kstep, nkc - k0)
                s = self.i % self.n
                self.i += 1
                fv = self.f[s][:].rearrange("p a b -> p (a b)")[:, 0:kn * cw].rearrange("p (a b) -> p a b", a=kn)
                bv = self.b[s][:].rearrange("p a b -> p (a b)")[:, 0:kn * cw].rearrange("p (a b) -> p a b", a=kn)
                K.dma("sp", "wcl%d" % s, fv, wv[:, k0:k0 + kn, b * cw:(b + 1) * cw], writes=[self.tf[s]])
                eng = self.engs[self.i % 3]
                K.copy(eng, bv, fv, [self.tf[s]], [self.tb[s]])
                K.dma("pool", "wcs%d" % s, scr[b, :, k0:k0 + kn, :], bv, reads=[self.tb[s]], writes=[])
        return scr


class Stage:
    def __init__(self, K, name, ncol, nslots):
        self.buf = [K.sb("%s%d" % (name, i), [128, ncol], F32) for i in range(nslots)]
        self.t = [K.trk(name) for _ in range(nslots)]
        self.n = nslots
        self.i = 0
        self.name = name

    def next(self):
        s = self.i % self.n
        self.i += 1
        return self.buf[s], self.t[s], "%s%d" % (self.name, s)


def load_tokmajor_T(C, src, ncol, dst, t_dst, stage):
    K = C.K
    nch = ncol // 128
    for tb in range(T // 128):
        sb_, t_sb, key = stage.next()
        K.dma("sp", key, sb_[:, 0:ncol], src[tb * 128:(tb + 1) * 128, :], writes=[t_sb])
        for c0 in range(0, nch, 4):
            cn = min(4, nch - c0)
            bk, tb_ = C.bank()
            for cc in range(cn):
                kc = c0 + cc
                K.tr(bk[:, cc * 128:(cc + 1) * 128], sb_[:, kc * 128:(kc + 1) * 128], C.ident[:],
                     [t_sb, C.t_ident], [tb_], inc=(cc == cn - 1))
            K.copy(("act", "dve")[(c0 // 4) % 2], dst[:, c0:c0 + cn, tb * 128:(tb + 1) * 128],
                   bk[:, 0:cn * 128].rearrange("p (c t) -> p c t", c=cn), [tb_], t_dst[c0:c0 + cn])


def rmsnorm_fm(C, hT, t_h, nch, g_sb, t_g, uT, t_u, sq, t_sq, rstd, t_rstd, dim, W=T):
    K = C.K
    for kc in range(nch):
        K.act(sq[:, kc, :], hT[:, kc, :], AF.Square, [t_h[kc]], [t_sq[kc]])
    bk, tbk = C.bank()
    for kc in range(nch):
        K.mm(bk[:, 0:W], C.ones[:], sq[:, kc, :], kc == 0, kc == nch - 1, [C.t_ones, t_sq[kc]], [tbk])
    K.act(rstd[:, :], bk[:, 0:W], AF.Ln, [tbk], [t_rstd], bias=EPS, scale=1.0 / dim)
    K.act(rstd[:, :], rstd[:, :], AF.Exp, [t_rstd], [t_rstd], scale=-0.5)
    for kc in range(nch):
        K.stt(uT[:, kc, :], hT[:, kc, :], g_sb[:, kc:kc + 1], rstd[:, :], ALU.mult, ALU.mult,
              [t_h[kc], t_g, t_rstd], [t_u[kc]])


class WStream:
    def __init__(self, K, name, nkc, cw, nslots):
        self.K, self.name, self.n = K, name, nslots
        self.buf = [K.sb("%s_%d" % (name, i), [128, nkc, cw], BF16) for i in range(nslots)]
        self.t = [K.trk(name) for _ in range(nslots)]
        self.i = 0

    def load(self, scr, t_scr, b, k0=0, kn=None):
        s = self.i % self.n
        self.i += 1
        nk = scr.shape[2] if kn is None else kn
        self.K.dma("sp", "%s%d" % (self.name, s), self.buf[s][:, 0:nk, :], scr[b, :, k0:k0 + nk, :],
                   reads=[t_scr], writes=[self.t[s]])
        return self.buf[s], self.t[s]


def dram_in(nc, name, shape, dt=F32):
    return nc.dram_tensor(name, list(shape), dt, kind="ExternalInput").ap()


def dram_out(nc, name, shape, dt=F32):
    return nc.dram_tensor(name, list(shape), dt, kind="ExternalOutput").ap()


def store_tokmajor(C, hT, t_h, dst, stage, q="pool"):
    K = C.K
    for tb in range(T // 128):
        sb_, t_sb, key = stage.next()
        for c4 in range(4):
            bk, t_bk = C.bank()
            for cc in range(4):
                kc = c4 * 4 + cc
                K.tr(bk[:, cc * 128:(cc + 1) * 128], hT[:, kc, tb * 128:(tb + 1) * 128], C.ident[:],
                     [t_h[kc], C.t_ident], [t_bk], inc=(cc == 3))
            K.copy(("act", "dve")[c4 % 2], sb_[:, c4 * 512:(c4 + 1) * 512], bk[:, :], [t_bk], [t_sb])
        K.dma(q, key + "s", dst[tb * 128:(tb + 1) * 128, :], sb_[:, :], reads=[t_sb], writes=[])


def ffn_phase_body(K, C, h_in, hprev, p_in, vecs, scr, h_out, gath=None, t_gath=None):
    vec = K.sb("vec", [128, NVF], F32)
    t_vec = K.trk("vec")
    K.dma("sp", "const", vec[:], vecs[:, :], writes=[t_vec])
    g_ffn = vec[:, 0:16]
    g_ple = vec[:, 16:32]
    cw = [vec[:, 32 + i * NFF:32 + (i + 1) * NFF] for i in range(3)]
    cb = vec[:, 32 + 3 * NFF:32 + 4 * NFF]
    NOT = K.trk("none")

    hT = K.sb("hT", [128, 16, T], F32)
    t_h = [K.trk("h") for _ in range(16)]
    uT = K.sb("uT", [128, 16, T], BF16)
    t_u = [K.trk("u") for _ in range(16)]
    hid = K.sb("hid", [128, NFF, T], BF16)
    t_hid = [K.trk("hid") for _ in range(NFF)]
    rstd = K.sb("rstd", [128, T], F32)
    t_rstd = K.trk("rstd")
    stage = Stage(K, "stg", D, 2)
    pstage = Stage(K, "pstg", PLE, 2)
    pT = K.sb("pT", [128, 2, T], BF16)
    t_p = [K.trk("p") for _ in range(2)]
    halo_sb = K.sb("halo_sb", [128, NFF, 2], F32)
    t_halo = [K.trk("halo") for _ in range(NFF)]
    abuf = [K.sb("abuf%d" % i, [128, T + 2], F32) for i in range(2)]
    t_abuf = [K.trk("abuf") for _ in range(2)]
    cbuf = [K.sb("cbuf%d" % i, [128, T], F32) for i in range(2)]
    t_cbuf = [K.trk("cbuf") for _ in range(2)]
    sig = [K.sb("sig%d" % i, [128, T], F32) for i in range(2)]
    t_sig = [K.trk("sig") for _ in range(2)]
    WG = WStream(K, "wg", 16, 128, 3)
    WU = WStream(K, "wu", 16, 128, 3)
    WD = WStream(K, "wd", NFF, 128, 2)
    WP = WStream(K, "wp", 16, 256, 2)
    wpp_sb = K.sb("wpp", [128, 2, D], BF16)
    t_wpp = K.trk("wpp")
    for b in range(4):
        K.dma("sp", "const", wpp_sb[:, :, b * 512:(b + 1) * 512], scr["pp"][b], writes=[t_wpp])
    sq = hid
    t_sq = t_hid
    hpT = K.sb("hpT", [128, 16, 2], F32)
    hpu = K.sb("hpu", [128, 16, 2], BF16)
    hpsq = K.sb("hpsq", [128, 16, 2], BF16)
    hprs = K.sb("hprs", [128, 2], F32)
    t_hp = [K.trk("hp") for _ in range(16)]
    t_hpu = [K.trk("hpu") for _ in range(16)]
    t_hpsq = [K.trk("hpsq") for _ in range(16)]
    t_hprs = K.trk("hprs")
    if gath is None:
        for t_ in range(2):
            K.dma("sp", "const", hpT[:, :, t_], hprev[t_, :].rearrange("(c p) -> p c", p=128), writes=t_hp,
                  allow_slow_non_contiguous=True)
    else:
        hpa = K.sb("hpa", [128, 16, 8], F32)
        t_hpa = K.trk("hpa")
        hrows = K.sb("hrows", [8, D], F32)
        t_hrows = K.trk("hrows")
        K.dma("sp", "const", hrows[:, :], gath[:, :], reads=[t_gath], writes=[t_hrows])
        bkh, t_bkh = C.bank()
        for c_ in range(16):
            K.tr(bkh[:, c_ * 8:(c_ + 1) * 8], hrows[0:8, c_ * 128:(c_ + 1) * 128], C.ident[0:8, 0:8],
                 [t_hrows, C.t_ident], [t_bkh], inc=(c_ == 15))
        K.copy("act", hpa[:, :, :].rearrange("p c r -> p (c r)"), bkh[:, 0:128], [t_bkh], [t_hpa])
        sel = vec[:, 32 + 4 * NFF:32 + 4 * NFF + 4]
        for t_ in range(2):
            K.ts("dve", hpT[:, :, t_], hpa[:, :, t_], sel[:, 0:1], None, ALU.mult, None, [t_hpa, t_vec], t_hp)
            for r_ in range(1, 4):
                K.stt(hpT[:, :, t_], hpa[:, :, 2 * r_ + t_], sel[:, r_:r_ + 1], hpT[:, :, t_], ALU.mult, ALU.add,
                      [t_hpa, t_vec] + t_hp, t_hp)
    rmsnorm_fm(C, hpT, t_hp, 16, g_ffn, t_vec, hpu, t_hpu, hpsq, t_hpsq, hprs, t_hprs, D, W=2)

    for st in range(NST):
        tok = slice(st * T, (st + 1) * T)
        load_tokmajor_T(C, h_in[tok, :], D, hT, t_h, stage)
        load_tokmajor_T(C, p_in[tok, :], PLE, pT, t_p, pstage)
        rmsnorm_fm(C, hT, t_h, 16, g_ffn, t_vec, uT, t_u, sq, t_sq, rstd, t_rstd, D)
        for j in range(NFF):
            wg, t_wg = WG.load(scr["gate"], NOT, j)
            wu, t_wu = WU.load(scr["up"], NOT, j)
            if st == 0:
                bh, t_bh = C.bank()
                for kc in range(16):
                    K.mm(bh[:, 0:2], wg[:, kc, :], hpu[:, kc, :], kc == 0, kc == 15, [t_wg, t_hpu[kc]], [t_bh])
                K.copy("act", halo_sb[:, j, :], bh[:, 0:2], [t_bh], [t_halo[j]])
            bg, t_bg = C.bank()
            for kc in range(16):
                K.mm(bg[:, :], wg[:, kc, :], uT[:, kc, :], kc == 0, kc == 15, [t_wg, t_u[kc]], [t_bg])
            bu, t_bu = C.bank()
            for kc in range(16):
                K.mm(bu[:, :], wu[:, kc, :], uT[:, kc, :], kc == 0, kc == 15, [t_wu, t_u[kc]], [t_bu])
            ab, t_ab = abuf[j % 2], t_abuf[j % 2]
            cbf, t_cb = cbuf[j % 2], t_cbuf[j % 2]
            K.copy("pool", ab[:, 0:2], halo_sb[:, j, :], [t_halo[j]], [t_ab])
            K.copy("act", ab[:, 2:T + 2], bg[:, :], [t_bg], [t_ab])
            K.copy("pool", halo_sb[:, j, :], ab[:, T:T + 2], [t_ab], [t_halo[j]])
            K.ts("dve", cbf[:, :], ab[:, 2:T + 2], cw[2][:, j:j + 1], cb[:, j:j + 1], ALU.mult, ALU.add,
                 [t_ab, t_vec], [t_cb])
            K.stt(cbf[:, :], ab[:, 1:T + 1], cw[1][:, j:j + 1], cbf[:, :], ALU.mult, ALU.add,
                  [t_ab, t_vec, t_cb], [t_cb])
            K.stt(cbf[:, :], ab[:, 0:T], cw[0][:, j:j + 1], cbf[:, :], ALU.mult, ALU.add,
                  [t_ab, t_vec, t_cb], [t_cb])
            K.act(cbf[:, :], cbf[:, :], AF.Silu, [t_cb], [t_cb])
            K.tt("dve", hid[:, j, :], cbf[:, :], bu[:, :], ALU.mult, [t_cb, t_bu], [t_hid[j]])
        for n in range(16):
            wd, t_wd = WD.load(scr["down"], NOT, n)
            bd, t_bd = C.bank()
            for j in range(NFF):
                K.mm(bd[:, :], wd[:, j, :], hid[:, j, :], j == 0, j == NFF - 1, [t_wd, t_hid[j]], [t_bd])
            K.tt("dve", hT[:, n, :], hT[:, n, :], bd[:, :], ALU.add, [t_h[n], t_bd], [t_h[n]])
        rmsnorm_fm(C, hT, t_h, 16, g_ple, t_vec, uT, t_u, sq, t_sq, rstd, t_rstd, D)
        for n2 in range(8):
            wp, t_wp = WP.load(scr["pg"], NOT, n2)
            for nn in range(2):
                n = n2 * 2 + nn
                bg, t_bg = C.bank()
                for kc in range(16):
                    K.mm(bg[:, :], wp[:, kc, nn * 128:(nn + 1) * 128], uT[:, kc, :], kc == 0, kc == 15,
                         [t_wp, t_u[kc]], [t_bg])
                bp, t_bp = C.bank()
                for kc in range(2):
                    K.mm(bp[:, :], wpp_sb[:, kc, n * 128:(n + 1) * 128], pT[:, kc, :], kc == 0, kc == 1,
                         [t_wpp, t_p[kc]], [t_bp])
                sg, t_sg = sig[n % 2], t_sig[n % 2]
                K.act(sg[:, :], bg[:, :], AF.Sigmoid, [t_bg], [t_sg])
                K.tt("dve", sg[:, :], sg[:, :], bp[:, :], ALU.mult, [t_sg, t_bp], [t_sg])
                K.tt("pool", hT[:, n, :], hT[:, n, :], sg[:, :], ALU.add, [t_h[n], t_sg], [t_h[n]])
        store_tokmajor(C, hT, t_h, h_out[tok, :], stage)


def cast_ffn_weights(K, w_gate, w_up, w_down, w_pg, w_pp, pfx=""):
    with contextlib.ExitStack() as st2:
        K2 = K
        old = K.stack
        K.stack = st2
        WC = WCast(K, nslots=3)
        scr = {}
        scr["gate"] = WC.cast(w_gate, D, DFF, pfx + "s_gate", cw=128)
        scr["up"] = WC.cast(w_up, D, DFF, pfx + "s_up", cw=128)
        scr["down"] = WC.cast(w_down, DFF, D, pfx + "s_down", cw=128)
        scr["pg"] = WC.cast(w_pg, D, D, pfx + "s_pg", cw=256)
        scr["pp"] = WC.cast(w_pp, PLE, D, pfx + "s_pp", cw=512)
        K.barrier()
        K.stack = old
    return scr


def build_ffn_phase(nc):
    stack = contextlib.ExitStack()
    with stack:
        K = Kern(nc, stack)
        h_in = dram_in(nc, "h_in", [SEG, D])
        hprev = dram_in(nc, "hprev", [2, D])
        p_in = dram_in(nc, "p_in", [SEG, PLE])
        w_gate = dram_in(nc, "w_gate", [D, DFF])
        w_up = dram_in(nc, "w_up", [D, DFF])
        w_down = dram_in(nc, "w_down", [DFF, D])
        w_pg = dram_in(nc, "w_pg", [D, D])
        w_pp = dram_in(nc, "w_pp", [PLE, D])
        vecs = dram_in(nc, "vecs", [128, NVF])
        h_out = dram_out(nc, "h_out", [SEG, D])
        C = Ctx(K)
        scr = cast_ffn_weights(K, w_gate, w_up, w_down, w_pg, w_pp)
        ffn_phase_body(K, C, h_in, hprev, p_in, vecs, scr, h_out)
        K.finish()
        K.replay()
    return nc


NVF = 32 + NFF * 4 + 4


def ffn_vecs(norm_ffn, norm_ple, conv_w, conv_b, q=0):
    v = np.zeros((128, NVF), np.float32)
    if q > 0:
        v[:, 32 + 4 * NFF + q - 1] = 1.0
    v[:, 0:16] = norm_ffn.reshape(16, 128).T
    v[:, 16:32] = norm_ple.reshape(16, 128).T
    for i in range(3):
        v[:, 32 + i * NFF:32 + (i + 1) * NFF] = conv_w[i].reshape(NFF, 128).T
    v[:, 32 + 3 * NFF:32 + 4 * NFF] = conv_b.reshape(NFF, 128).T
    return v


NSLOT = 3
SLOT_ST = 8
O_ST = 8
TWO_PI = 2.0 * np.pi
MAGIC = 12582912.0
C1 = 6.28125
C2 = float(TWO_PI - 6.28125)
PI_LO = 3.1415925
NVA = 80


def vecsA_host(inp, q):
    v = np.zeros((128, NVA), np.float32)
    v[:, 0:16] = inp["norm_mix"][0].reshape(16, 128).T
    v[:, 16:24] = inp["e_lb_logits"][0].reshape(8, 128).T
    v[:, 24:32] = inp["e_lb_logits"][1].reshape(8, 128).T
    v[:, 32:36] = inp["e_q_a_norm"][0].reshape(4, 128).T
    v[:, 36:40] = inp["e_kv_a_norm"][0].reshape(4, 128).T
    qn, kn = inp["e_q_norm"][0], inp["e_k_norm"][0]
    for base, g in ((40, qn), (43, kn)):
        v[:, base] = g[0:128]
        v[0:64, base + 1] = g[128:192]
        v[0:32, base + 2] = g[160:192]
        v[32:64, base + 2] = g[128:160]
    v[:, 46] = inp["e_hg_onorm"][0]
    invf = (10000.0 ** (-np.arange(32, dtype=np.float32) / np.float32(32))).astype(np.float32)
    v[0:64, 47] = np.concatenate([invf, invf])
    v[0:32, 48] = -1.0
    v[32:64, 48] = 1.0
    for i in range(NSLOT):
        valid = (q - NSLOT + i) >= 0
        v[:, 65 + i] = 0.0 if valid else -30000.0
        v[:, 69 + i] = 1.0 if valid else 0.0
    v[:, 68] = 0.0
    return v


class PhaseA:
    def __init__(self, K, C, nc):
        self.K, self.C, self.nc = K, C, nc

    def prologue(self, w_in, w_uq, w_ukv, w_out):
        K = self.K
        with contextlib.ExitStack() as st2:
            old = K.stack
            K.stack = st2
            WC = WCast(K, nslots=3)
            scr = {}
            scr["win"] = WC.cast(w_in[:, 0:5120], D, 5120, "sA_win", cw=128)
            scr["hi"] = WC.cast(w_in[:, 2048:3072], D, 1024, "sA_hi", cw=256)
            scr["wout"] = WC.cast(w_out, D, D, "sA_wout", cw=128)
            scr["small"] = K.nc.dram_tensor("sA_small", [4, 128, 4, 1024], BF16, kind="Internal").ap()
            scr["kpe"] = K.nc.dram_tensor("sA_kpe", [1, 128, 16, 128], BF16, kind="Internal").ap()

            def piece(dst, src, n, nkc):
                srcv = src.rearrange("(kc p) n -> p kc n", p=128)
                for k0 in range(0, nkc, 8):
                    kn = min(8, nkc - k0)
                    s = WC.i % WC.n
                    WC.i += 1
                    fv = WC.f[s][:].rearrange("p a b -> p (a b)")[:, 0:kn * n].rearrange("p (a b) -> p a b", a=kn)
                    bv = WC.b[s][:].rearrange("p a b -> p (a b)")[:, 0:kn * n].rearrange("p (a b) -> p a b", a=kn)
                    K.dma("sp", "wcl%d" % s, fv, srcv[:, k0:k0 + kn, :], writes=[WC.tf[s]])
                    K.copy(WC.engs[WC.i % 3], bv, fv, [WC.tf[s]], [WC.tb[s]])
                    K.dma("pool", "wcs%d" % s, dst[:, k0:k0 + kn, :], bv, reads=[WC.tb[s]], writes=[])

            sm = scr["small"]
            for h in range(8):
                piece(sm[0, :, :, h * 128:(h + 1) * 128], w_ukv[:, h * 256:h * 256 + 128], 128, 4)
                piece(sm[1, :, :, h * 128:(h + 1) * 128], w_ukv[:, h * 256 + 128:h * 256 + 256], 128, 4)
                piece(sm[2, :, :, h * 128:(h + 1) * 128], w_uq[:, h * 192:h * 192 + 128], 128, 4)
                piece(sm[3, :, :, h * 64:(h + 1) * 64], w_uq[:, h * 192 + 128:h * 192 + 192], 64, 4)
                piece(sm[3, :, :, 512 + h * 64:512 + h * 64 + 32], w_uq[:, h * 192 + 160:h * 192 + 192], 32, 4)
                piece(sm[3, :, :, 512 + h * 64 + 32:512 + (h + 1) * 64], w_uq[:, h * 192 + 128:h * 192 + 160], 32, 4)
            piece(scr["kpe"][0, :, :, 0:64], w_in[:, 5120:5184], 64, 16)
            piece(scr["kpe"][0, :, :, 64:96], w_in[:, 5152:5184], 32, 16)
            piece(scr["kpe"][0, :, :, 96:128], w_in[:, 5120:5152], 32, 16)
            K.barrier()
            K.stack = old
        self.scr = scr

    def alloc(self, vecs, pos_all, ntok_all):
        K, C, nc = self.K, self.C, self.nc
        self.vec = K.sb("vecA", [128, NVA], F32)
        self.t_vec = K.trk("vecA")
        K.dma("sp", "const", self.vec[:], vecs[:, :], writes=[self.t_vec])
        v = self.vec
        self.lb = K.sb("lb", [128, 8], F32)
        self.oml = K.sb("oml", [128, 8], F32)
        K.tt("dve", self.lb[:], v[:, 16:24], v[:, 24:32], ALU.subtract, [self.t_vec], [self.t_vec])
        K.act(self.lb[:], self.lb[:], AF.Sigmoid, [self.t_vec], [self.t_vec])
        K.ts("dve", self.oml[:], self.lb[:], -1.0, 1.0, ALU.mult, ALU.add, [self.t_vec], [self.t_vec])
        tri = np.triu(np.ones((128, 128), np.float32))
        self.tri = K.sb("tri", [128, 128], F32)
        rm = np.ones((128, T), np.float32)
        rm[:, 0::128] = 0.0
        self.rmask = K.sb("rmask", [128, T], F32)
        self.t_cst = K.trk("cstA")
        K.dma("sp", "const", self.tri[:], nc.inline_tensor(tri, name="tri_d").ap()[:, :], writes=[self.t_cst])
        K.dma("sp", "const", self.rmask[:], nc.inline_tensor(rm, name="rm_d").ap()[:, :], writes=[self.t_cst])
        self.trib = K.sb("trib", [128, 128], BF16)
        K.copy("dve", self.trib[:], self.tri[:], [self.t_cst], [self.t_cst])
        self.pos_all = pos_all
        self.ntok = ntok_all
        self.Kn = nc.dram_tensor("Kn_scr", [8, 128, ntok_all], BF16, kind="Internal").ap()
        self.Kr = nc.dram_tensor("Kr_scr", [8, 64, ntok_all], BF16, kind="Internal").ap()
        self.Vs = nc.dram_tensor("V_scr", [ntok_all, 1024], BF16, kind="Internal").ap()
        ng = ntok_all // T
        self.t_kn = [[K.trk("kn") for _ in range(8)] for _ in range(ng)]
        self.t_kr = [[K.trk("kr") for _ in range(8)] for _ in range(ng)]
        self.t_vs = [[K.trk("vs") for _ in range(4)] for _ in range(ng)]
        self.xT = K.sb("xT", [128, 16, T], F32)
        self.t_x = [K.trk("x") for _ in range(16)]
        self.wk = [self.xT[:, i, :] for i in range(16)]
        self.t_wk = self.t_x
        self.wk_i = 0
        self.uT = K.sb("uTa", [128, 16, T], BF16)
        self.t_u = [K.trk("u") for _ in range(16)]
        self.mixT = K.sb("mixT", [128, 16, T], BF16)
        self.t_mix = [K.trk("mix") for _ in range(16)]
        self.sq = self.mixT
        self.t_sq = self.t_mix
        self.sq4 = K.sb("sq4", [128, 4, T], BF16)
        self.t_sq4 = [K.trk("sq4") for _ in range(4)]
        self.rstd = K.sb("rstdA", [128, T], F32)
        self.t_rstd = K.trk("rstd")
        self.stage = Stage(K, "stgA", D, 2)
        self.W = WStream(K, "wA", 16, 128, 3)
        self.WH = WStream(K, "wAh", 16, 256, 2)
        self.WS = WStream(K, "wAs", 4, 1024, 2)
        self.posi = K.sb("posi", [64, T], I32)
        self.cos = K.sb("cosT", [64, T], F32)
        self.sin = K.sb("sinT", [64, T], F32)
        self.t_rope = K.trk("rope")
        self.t_cs = K.trk("cossin")
        self.wb = [K.sb("wbA%d" % i, [128, T], BF16) for i in range(6)]
        self.t_wb = [K.trk("wb") for _ in range(6)]
        self.wb_i = 0
        self.kvn = K.sb("kvn", [128, 4, T], BF16)
        self.t_kvn = [K.trk("kvn") for _ in range(4)]
        self.kpe = K.sb("kpe", [64, 2, T], F32)
        self.t_kpe = K.trk("kpe")
        self.KR = K.sb("KR", [64, T], F32)
        self.t_KR = K.trk("KR")
        self.sqpe = K.sb("sqpe", [64, T], BF16)
        self.t_sqpe = K.trk("sqpe")
        self.vtm = K.sb("vtm", [128, 4, 1024], BF16)
        self.t_vtm = [K.trk("vtm") for _ in range(4)]
        self.hv = self.vtm
        self.t_hv = self.t_vtm
        self.Qn = K.sb("Qn", [128, 8, T], BF16)
        self.Qr = K.sb("Qr", [64, 8, T], BF16)
        self.t_Q = [K.trk("Q") for _ in range(8)]
        self.S = K.sb("S", [128, 8, 128], F32)
        self.Sb = K.sb("Sb", [128, 8, 128], BF16)
        self.t_S = [K.trk("S") for _ in range(8)]
        self.t_Sb = [K.trk("Sb") for _ in range(8)]
        for h in range(8):
            K.op("pool", lambda e, h=h: e.memset(self.S[:, h, :], 0.0), writes=[self.t_S[h]])
            K.op("pool", lambda e, h=h: e.memset(self.Sb[:, h, :], 0.0), writes=[self.t_Sb[h]])
        self.fbuf = [K.sb("fbuf%d" % i, [128, T], F32) for i in range(2)]
        self.t_fbuf = [K.trk("fbuf") for _ in range(2)]
        self.kktm = K.sb("kktm", [128, 2, 4, 128], BF16)
        self.t_kktm = [K.trk("kktm") for _ in range(2)]
        self.ebl = K.sb("ebl", [128, 2, 4], F32)
        self.aK = [K.sb("aK%d" % i, [128, T], BF16) for i in range(3)]
        self.aKr = [K.sb("aKr%d" % i, [64, T], BF16) for i in range(3)]
        self.aV = [K.sb("aV%d" % i, [128, 4, 128], BF16) for i in range(3)]
        self.t_aK = [K.trk("aK") for _ in range(3)]
        self.t_aKr = [K.trk("aKr") for _ in range(3)]
        self.t_aV = [K.trk("aV") for _ in range(3)]
        self.aP = [K.sb("aP%d" % i, [128, T], BF16) for i in range(3)]
        self.t_aP = [K.trk("aP") for _ in range(3)]
        self.a_i = 0
        self.p_i = 0
        self.pacc = [K.sb("pacc0", [128, T], F32)] * 2
        self.t_pacc = [K.trk("pacc")] * 2

    def work(self):
        i = self.wk_i % 16
        self.wk_i += 1
        return self.wk[i], self.t_wk[i]

    def workb(self):
        i = self.wb_i % 6
        self.wb_i += 1
        return self.wb[i], self.t_wb[i]

    def rope_tables(self, g):
        K = self.K
        v = self.vec
        K.dma("sp", "posi", self.posi[:, :], self.pos_all[0:1, g * T:(g + 1) * T].broadcast_to([64, T]),
              writes=[self.t_rope])
        pf, t_pf = self.work()
        ra, t_ra = self.work()
        rn, t_rn = self.work()
        posf, ra, rn = pf[0:64, :], ra[0:64, :], rn[0:64, :]
        K.copy("dve", posf, self.posi[:, :], [self.t_rope], [t_pf])
        for off, out in ((0.0, self.sin), (float(np.pi / 2), self.cos)):
            K.ts("dve", ra, posf, v[0:64, 47:48], off, ALU.mult, ALU.add, [t_pf, self.t_vec], [t_ra])
            K.ts("dve", rn, ra, float(1.0 / TWO_PI), MAGIC, ALU.mult, ALU.add, [t_ra], [t_rn])
            K.ts("dve", rn, rn, -MAGIC, None, ALU.add, None, [t_rn], [t_rn])
            K.stt(ra, rn, -C1, ra, ALU.mult, ALU.add, [t_rn, t_ra], [t_ra])
            K.stt(ra, rn, -C2, ra, ALU.mult, ALU.add, [t_rn, t_ra], [t_ra])
            K.ts("dve", ra, ra, -PI_LO, PI_LO, ALU.max, ALU.min, [t_ra], [t_ra])
            K.act(out[:, :], ra, AF.Sin, [t_ra], [self.t_cs])
        K.ts("dve", self.sin[:, :], self.sin[:, :], v[0:64, 48:49], None, ALU.mult, None,
             [self.t_cs, self.t_vec], [self.t_cs])

    def proj_fm(self, lhs_fn, lhs_trk, nkc, rhsT, t_rhs, np_out=128):
        K, C = self.K, self.C
        bk, t_bk = C.bank()
        for kc in range(nkc):
            K.mm(bk[0:np_out, :], lhs_fn(kc), rhsT[:, kc, :], kc == 0, kc == nkc - 1, [lhs_trk, t_rhs[kc]], [t_bk])
        return bk, t_bk

    def rstd_from(self, parts, dim, out, t_out, np_out=128):
        K, C = self.K, self.C
        bk, t_bk = C.bank()
        n = len(parts)
        for i, (ap, t, kp) in enumerate(parts):
            K.mm(bk[:, :], C.ones[0:kp, :], ap, i == 0, i == n - 1, [C.t_ones, t], [t_bk])
        K.act(out, bk[0:np_out, :], AF.Ln, [t_bk], [t_out], bias=EPS, scale=1.0 / dim)
        K.act(out, out, AF.Exp, [t_out], [t_out], scale=-0.5)

    def front(self, x_all, g):
        C = self.C
        load_tokmajor_T(C, x_all[g * T:(g + 1) * T, :], D, self.xT, self.t_x, self.stage)
        rmsnorm_fm(C, self.xT, self.t_x, 16, self.vec[:, 0:16], self.t_vec, self.uT, self.t_u,
                   self.sq, self.t_sq, self.rstd, self.t_rstd, D)

    def mla_kv(self, g):
        K, C, v = self.K, self.C, self.vec
        NOT = K.trk("none")
        self.rope_tables(g)
        sqs = []
        ckv = []
        for c in range(4):
            w, t_w = self.W.load(self.scr["win"], NOT, 36 + c)
            bk, t_bk = self.proj_fm(lambda kc, w=w: w[:, kc, :], t_w, 16, self.uT, self.t_u)
            cf, t_cf = self.work()
            ckv.append((cf, t_cf))
            K.copy("act", cf[:, :], bk[:, :], [t_bk], [t_cf])
            K.act(self.sq4[:, c, :], bk[:, :], AF.Square, [t_bk], [self.t_sq4[c]])
            sqs.append((self.sq4[:, c, :], self.t_sq4[c], 128))
        self.rstd_from(sqs, 512, self.rstd[:, :], self.t_rstd)
        for c in range(4):
            K.stt(self.kvn[:, c, :], ckv[c][0][:, :], v[:, 36 + c:37 + c], self.rstd[:, :], ALU.mult, ALU.mult,
                  [ckv[c][1], self.t_vec, self.t_rstd], [self.t_kvn[c]])
        wk_, t_wk_ = self.W.load(self.scr["kpe"], NOT, 0)
        for i in range(2):
            bk, t_bk = self.proj_fm(lambda kc, i=i: wk_[:, kc, i * 64:(i + 1) * 64], t_wk_, 16,
                                    self.uT, self.t_u, np_out=64)
            K.copy("act", self.kpe[:, i, :], bk[0:64, :], [t_bk], [self.t_kpe])
        K.act(self.sqpe[:, :], self.kpe[:, 0, :], AF.Square, [self.t_kpe], [self.t_sqpe])
        tmp, t_tmp = self.work()
        K.stt(self.KR[:, :], self.kpe[:, 0, :], v[0:64, 44:45], self.cos[:, :], ALU.mult, ALU.mult,
              [self.t_kpe, self.t_vec, self.t_cs], [self.t_KR])
        K.stt(tmp[0:64, :], self.kpe[:, 1, :], v[0:64, 45:46], self.sin[:, :], ALU.mult, ALU.mult,
              [self.t_kpe, self.t_vec, self.t_cs], [t_tmp])
        K.tt("pool", self.KR[:, :], self.KR[:, :], tmp[0:64, :], ALU.add, [self.t_KR, t_tmp], [self.t_KR])
        ukn, t_ukn = self.WS.load(self.scr["small"], NOT, 0)
        for h in range(8):
            bk, t_bk = self.proj_fm(lambda kc, h=h: ukn[:, kc, h * 128:(h + 1) * 128], t_ukn, 4,
                                    self.kvn, self.t_kvn)
            kf, t_kf = self.work()
            K.copy("act", kf[:, :], bk[:, :], [t_bk], [t_kf])
            sqn, t_sqn = self.workb()
            K.act(sqn[:, :], bk[:, :], AF.Square, [t_bk], [t_sqn])
            rk, t_rk = self.work()
            self.rstd_from([(sqn[:, :], t_sqn, 128), (self.sqpe[:, :], self.t_sqpe, 64)], 192, rk[:, :], t_rk)
            kb, t_kb = self.workb()
            K.stt(kb[:, :], kf[:, :], v[:, 43:44], rk[:, :], ALU.mult, ALU.mult, [t_kf, self.t_vec, t_rk], [t_kb])
            K.dma("pool", "kns%d" % (h % 4), self.Kn[h, :, g * T:(g + 1) * T], kb[:, :], reads=[t_kb],
                  writes=[self.t_kn[g][h]])
            krb, t_krb = self.workb()
            K.tt("pool", krb[0:64, :], self.KR[:, :], rk[0:64, :], ALU.mult, [self.t_KR, t_rk], [t_krb])
            K.dma("pool", "krs%d" % (h % 4), self.Kr[h, :, g * T:(g + 1) * T], krb[0:64, :], reads=[t_krb],
                  writes=[self.t_kr[g][h]])
        uv, t_uv = self.WS.load(self.scr["small"], NOT, 1)
        for tb in range(4):
            for half in range(2):
                bk, t_bk = C.bank()
                for kc in range(4):
                    K.mm(bk[:, :], self.kvn[:, kc, tb * 128:(tb + 1) * 128], uv[:, kc, half * 512:(half + 1) * 512],
                         kc == 0, kc == 3, [self.t_kvn[kc], t_uv], [t_bk])
                K.copy(("act", "dve")[half], self.vtm[:, tb, half * 512:(half + 1) * 512], bk[:, :], [t_bk],
                       [self.t_vtm[tb]])
            K.dma("pool", "vs%d" % tb, self.Vs[g * T + tb * 128:g * T + (tb + 1) * 128, :], self.vtm[:, tb, :],
                  reads=[self.t_vtm[tb]], writes=[self.t_vs[g][tb]])

    def hgrn(self, own, vflag_col):
        K, C, v = self.K, self.C, self.vec
        NOT = K.trk("none")
        for qd_ in range(4):
            w, t_w = self.WH.load(self.scr["hi"], NOT, qd_)
            for tb in range(4):
                bk, t_bk = C.bank()
                for kc in range(16):
                    K.mm(bk[:, 0:256], self.uT[:, kc, tb * 128:(tb + 1) * 128], w[:, kc, :], kc == 0, kc == 15,
                         [self.t_u[kc], t_w], [t_bk])
                if vflag_col is None:
                    K.copy("act", self.hv[:, tb, qd_ * 256:(qd_ + 1) * 256], bk[:, 0:256], [t_bk], [self.t_hv[tb]])
                else:
                    K.ts("dve", self.hv[:, tb, qd_ * 256:(qd_ + 1) * 256], bk[:, 0:256], vflag_col, None, ALU.mult, None,
                         [t_bk, self.t_vec], [self.t_hv[tb]])
        def hg_proj(h):
            w, t_w = self.W.load(self.scr["win"], NOT, 8 + h)
            bk, t_bk = self.proj_fm(lambda kc, w=w: w[:, kc, :], t_w, 16, self.uT, self.t_u)
            f, t_f = self.fbuf[h % 2], self.t_fbuf[h % 2]
            K.act(f[:, :], bk[:, :], AF.Sigmoid, [t_bk], [t_f])
            return f, t_f

        nxt = hg_proj(0)
        for h in range(8):
            par = h % 2
            f, t_f = nxt
            if h < 7:
                nxt = hg_proj(h + 1)
            K.ts("dve", f[:, :], f[:, :], self.oml[:, h:h + 1], self.lb[:, h:h + 1], ALU.mult, ALU.add,
                 [t_f, self.t_vec], [t_f])
            b, t_b = self.work()
            K.act(b[:, :], f[:, :], AF.Ln, [t_f], [t_b])
            K.op("dve", lambda e, b=b: e.tensor_tensor_scan(b[:, :], self.rmask[:, :], b[:, :], 0.0, ALU.mult, ALU.add),
                 [t_b, self.t_cst], [t_b])
            K.ts("pool", f[:, :], f[:, :], -1.0, 1.0, ALU.mult, ALU.add, [t_f], [t_f])
            b3 = b[:, :].rearrange("p (c t) -> p c t", c=4)
            e4, t_e4 = self.work()
            for c in range(4):
                K.act(e4[:, c * 128:(c + 1) * 128], b[:, c * 128:(c + 1) * 128], AF.Exp, [t_b], [t_e4],
                      bias=b[:, c * 128 + 127:c * 128 + 128], scale=-1.0)
            K.tt("pool", e4[:, :], e4[:, :], f[:, :], ALU.mult, [t_e4, t_f], [t_e4])
            bkt, t_bkt = C.bank()
            for c in range(4):
                K.tr(bkt[:, c * 128:(c + 1) * 128], e4[:, c * 128:(c + 1) * 128], C.ident[:], [t_e4, C.t_ident],
                     [t_bkt], inc=(c == 3))
            K.copy("act", self.kktm[:, par, :, :].rearrange("p c k -> p (c k)"), bkt[:, :], [t_bkt], [self.t_kktm[par]])
            K.act(self.ebl[:, par, :], b3[:, :, 127], AF.Exp, [t_b], [self.t_kktm[par]])
            if own:
                wq, t_wq = self.W.load(self.scr["win"], NOT, h)
                bq, t_bq = self.proj_fm(lambda kc, wq=wq: wq[:, kc, :], t_wq, 16, self.uT, self.t_u)
                qf, t_qf = self.work()
                K.copy("act", qf[:, :], bq[:, :], [t_bq], [t_qf])
                e3, t_e3 = self.work()
                K.act(e3[:, :], b[:, :], AF.Exp, [t_b], [t_e3])
                qd, t_qd = self.workb()
                K.tt("dve", qd[:, :], qf[:, :], e3[:, :], ALU.mult, [t_qf, t_e3], [t_qd])
                e1, t_e1 = self.work()
                e2, t_e2 = self.work()
                nref, t_nref = self.work()
                K.ts("dve", nref[:, 0:4], b3[:, :, 63], -1.0, None, ALU.mult, None, [t_b], [t_nref])
                for c in range(4):
                    K.act(e1[:, c * 128:(c + 1) * 128], b[:, c * 128:(c + 1) * 128], AF.Exp, [t_b, t_nref], [t_e1],
                          bias=nref[:, c:c + 1], scale=1.0)
                    K.act(e2[:, c * 128:(c + 1) * 128], b[:, c * 128:(c + 1) * 128], AF.Exp, [t_b], [t_e2],
                          bias=b[:, c * 128 + 63:c * 128 + 64], scale=-1.0)
                qr, t_qr = self.workb()
                kr, t_kr = self.workb()
                K.tt("dve", qr[:, :], qf[:, :], e1[:, :], ALU.mult, [t_qf, t_e1], [t_qr])
                K.tt("pool", kr[:, :], f[:, :], e2[:, :], ALU.mult, [t_f, t_e2], [t_kr])
                bo, t_bo = C.bank(hold=True)
            for c in range(4):
                cs = slice(c * 128, (c + 1) * 128)
                if own:
                    ba, t_ba = C.bank()
                    K.mm(ba[:, 0:128], kr[:, cs], qr[:, cs], True, True, [t_kr, t_qr], [t_ba])
                    am, t_am = self.workb()
                    K.tt("dve", am[:, 0:128], ba[:, 0:128], self.tri[:, :], ALU.mult, [t_ba, self.t_cst], [t_am])
                    K.mm(bo[:, cs], self.hv[:, c, h * 128:(h + 1) * 128], am[:, 0:128], True, False,
                         [self.t_hv[c], t_am], [t_bo], inc=False)
                    K.mm(bo[:, cs], self.Sb[:, h, :], qd[:, cs], False, True, [self.t_Sb[h], t_qd], [t_bo], inc=True)
                bs, t_bs = C.bank()
                K.mm(bs[:, 0:128], self.kktm[:, par, c, :], self.hv[:, c, h * 128:(h + 1) * 128], True, True,
                     [self.t_kktm[par], self.t_hv[c]], [t_bs])
                K.stt(self.S[:, h, :], self.S[:, h, :], self.ebl[:, par, c:c + 1], bs[:, 0:128], ALU.mult, ALU.add,
                      [self.t_S[h], self.t_kktm[par], t_bs], [self.t_S[h]])
                K.copy("pool", self.Sb[:, h, :], self.S[:, h, :], [self.t_S[h]], [self.t_Sb[h]])
            if own:
                of, t_of = self.work()
                K.copy("act", of[:, :], bo[:, :], [t_bo], [t_of])
                sqo, t_sqo = self.workb()
                K.act(sqo[:, :], bo[:, :], AF.Square, [t_bo], [t_sqo])
                C.release(t_bo)
                ro, t_ro = self.work()
                self.rstd_from([(sqo[:, :], t_sqo, 128)], 128, ro[:, :], t_ro)
                wg_, t_wg = self.W.load(self.scr["win"], NOT, 24 + h)
                bg, t_bg = self.proj_fm(lambda kc, wg_=wg_: wg_[:, kc, :], t_wg, 16, self.uT, self.t_u)
                sg, t_sg = self.work()
                K.act(sg[:, :], bg[:, :], AF.Silu, [t_bg], [t_sg])
                K.stt(of[:, :], of[:, :], v[:, 46:47], ro[:, :], ALU.mult, ALU.mult, [t_of, self.t_vec, t_ro], [t_of])
                K.tt("pool", self.mixT[:, h, :], of[:, :], sg[:, :], ALU.mult, [t_of, t_sg], [self.t_mix[h]])

    def mla_q(self):
        K, C, v = self.K, self.C, self.vec
        NOT = K.trk("none")
        sqs = []
        cq = []
        for c in range(4):
            w, t_w = self.W.load(self.scr["win"], NOT, 32 + c)
            bk, t_bk = self.proj_fm(lambda kc, w=w: w[:, kc, :], t_w, 16, self.uT, self.t_u)
            cf, t_cf = self.work()
            cq.append((cf, t_cf))
            K.copy("act", cf[:, :], bk[:, :], [t_bk], [t_cf])
            K.act(self.sq4[:, c, :], bk[:, :], AF.Square, [t_bk], [self.t_sq4[c]])
            sqs.append((self.sq4[:, c, :], self.t_sq4[c], 128))
        self.rstd_from(sqs, 512, self.rstd[:, :], self.t_rstd)
        qn = self.kvn
        t_qn = self.t_kvn
        for c in range(4):
            K.stt(qn[:, c, :], cq[c][0][:, :], v[:, 32 + c:33 + c], self.rstd[:, :], ALU.mult, ALU.mult,
                  [cq[c][1], self.t_vec, self.t_rstd], [t_qn[c]])
        sc = float(192 ** -0.5)
        uqn, t_uqn = self.WS.load(self.scr["small"], NOT, 2)
        uqr, t_uqr = self.WS.load(self.scr["small"], NOT, 3)
        for h in range(8):
            bk, t_bk = self.proj_fm(lambda kc, h=h: uqn[:, kc, h * 128:(h + 1) * 128], t_uqn, 4, qn, t_qn)
            qf, t_qf = self.work()
            K.copy("act", qf[:, :], bk[:, :], [t_bk], [t_qf])
            sqn, t_sqn = self.workb()
            K.act(sqn[:, :], bk[:, :], AF.Square, [t_bk], [t_sqn])
            b1, t_b1 = self.proj_fm(lambda kc, h=h: uqr[:, kc, h * 64:(h + 1) * 64], t_uqr, 4, qn, t_qn, 64)
            b2, t_b2 = self.proj_fm(lambda kc, h=h: uqr[:, kc, 512 + h * 64:512 + (h + 1) * 64], t_uqr, 4, qn, t_qn, 64)
            sqr, t_sqr = self.workb()
            K.act(sqr[0:64, :], b1[0:64, :], AF.Square, [t_b1], [t_sqr])
            rq, t_rq = self.work()
            self.rstd_from([(sqn[:, :], t_sqn, 128), (sqr[0:64, :], t_sqr, 64)], 192, rq[:, :], t_rq)
            K.ts("pool", rq[:, :], rq[:, :], sc, None, ALU.mult, None, [t_rq], [t_rq])
            K.stt(self.Qn[:, h, :], qf[:, :], v[:, 40:41], rq[:, :], ALU.mult, ALU.mult, [t_qf, self.t_vec, t_rq],
                  [self.t_Q[h]])
            r1, t_r1 = self.work()
            r2, t_r2 = self.work()
            K.stt(r1[0:64, :], b1[0:64, :], v[0:64, 41:42], self.cos[:, :], ALU.mult, ALU.mult,
                  [t_b1, self.t_vec, self.t_cs], [t_r1])
            K.stt(r2[0:64, :], b2[0:64, :], v[0:64, 42:43], self.sin[:, :], ALU.mult, ALU.mult,
                  [t_b2, self.t_vec, self.t_cs], [t_r2])
            K.tt("pool", r1[0:64, :], r1[0:64, :], r2[0:64, :], ALU.add, [t_r1, t_r2], [t_r1])
            K.tt("pool", self.Qr[:, h, :], r1[0:64, :], rq[0:64, :], ALU.mult, [t_r1, t_rq], [self.t_Q[h]])

    def attention(self, g, n_prior_groups):
        K, C, v = self.K, self.C, self.vec
        for h in range(8):
            bo, t_bo = C.bank(hold=True)
            pacc, t_pacc = self.pacc[h % 2], self.t_pacc[h % 2]
            K.op("pool", lambda e, pacc=pacc: e.memset(pacc[:, :], 0.0), writes=[t_pacc])
            groups = list(range(0, g + 1))
            pend = None
            first = True
            for gi in groups:
                diag = (gi == g)
                slot = gi // SLOT_ST if gi < n_prior_groups else NSLOT
                bias = v[:, 65 + slot:66 + slot] if gi < n_prior_groups else v[:, 68:69]
                s = self.a_i % 3
                self.a_i += 1
                K.dma("sp", "aK%d" % s, self.aK[s][:, :], self.Kn[h, :, gi * T:(gi + 1) * T],
                      reads=[self.t_kn[gi][h]], writes=[self.t_aK[s]])
                K.dma("sp", "aKr%d" % s, self.aKr[s][:, :], self.Kr[h, :, gi * T:(gi + 1) * T],
                      reads=[self.t_kr[gi][h]], writes=[self.t_aKr[s]])
                K.dma("sp", "aV%d" % s, self.aV[s][:, :, :],
                      self.Vs[gi * T:(gi + 1) * T, h * 128:(h + 1) * 128].rearrange("(tb p) v -> p tb v", p=128),
                      reads=self.t_vs[gi], writes=[self.t_aV[s]])
                for kb in range(4):
                    q0 = kb * 128 if diag else 0
                    bs, t_bs = C.bank()
                    K.mm(bs[:, q0:T], self.aK[s][:, kb * 128:(kb + 1) * 128], self.Qn[:, h, q0:T], True, False,
                         [self.t_aK[s], self.t_Q[h]], [t_bs], inc=False)
                    K.mm(bs[:, q0:T], self.aKr[s][:, kb * 128:(kb + 1) * 128], self.Qr[:, h, q0:T], False, True,
                         [self.t_aKr[s], self.t_Q[h]], [t_bs], inc=True)
                    pi = self.p_i % 3
                    self.p_i += 1
                    P, t_P = self.aP[pi], self.t_aP[pi]
                    K.act(P[:, q0:T], bs[:, q0:T], AF.Exp, [t_bs, self.t_vec], [t_P], bias=bias, scale=1.0)
                    if diag:
                        K.tt("dve", P[:, q0:q0 + 128], P[:, q0:q0 + 128], self.trib[:, :], ALU.mult,
                             [t_P, self.t_cst], [t_P])
                    K.tt("dve", pacc[:, q0:T], pacc[:, q0:T], P[:, q0:T], ALU.add, [t_pacc, t_P], [t_pacc])
                    if pend is not None:
                        self._pv(*pend)
                    pend = (bo, t_bo, P, t_P, s, kb, q0, first, False)
                    first = False
            lst = list(pend)
            lst[-1] = True
            self._pv(*lst)
            bd, t_bd = C.bank()
            K.mm(bd[:, :], C.ones_f[:, :], pacc[:, :], True, True, [C.t_ones, t_pacc], [t_bd])
            rd, t_rd = self.work()
            K.act(rd[:, :], bd[:, :], AF.Ln, [t_bd], [t_rd])
            K.act(rd[:, :], rd[:, :], AF.Exp, [t_rd], [t_rd], scale=-1.0)
            K.tt("dve", self.mixT[:, 8 + h, :], bo[:, :], rd[:, :], ALU.mult, [t_bo, t_rd], [self.t_mix[8 + h]])
            C.release(t_bo)

    def _pv(self, bo, t_bo, P, t_P, s, kb, q0, first, last):
        K, C = self.K, self.C
        K.mm(bo[:, q0:T], self.aV[s][:, kb, :], P[:, q0:T], first, last, [self.t_aV[s], t_P], [t_bo], inc=True)

    def out_proj(self, x_all, g, h_mid, o):
        K, C = self.K, self.C
        NOT = K.trk("none")
        for n in range(16):
            w, t_w = self.W.load(self.scr["wout"], NOT, n)
            bk, t_bk = self.proj_fm(lambda kc, w=w: w[:, kc, :], t_w, 16, self.mixT, self.t_mix)
            K.copy("act", self.xT[:, n, :], bk[:, :], [t_bk], [self.t_x[n]])
        for tb in range(4):
            sb_, t_sb, key = self.stage.next()
            K.dma("sp", key, sb_[:, :], x_all[g * T + tb * 128:g * T + (tb + 1) * 128, :], writes=[t_sb])
            for c4 in range(4):
                bk, t_bk = C.bank()
                for cc in range(4):
                    kc = c4 * 4 + cc
                    K.tr(bk[:, cc * 128:(cc + 1) * 128], self.xT[:, kc, tb * 128:(tb + 1) * 128], C.ident[:],
                         [self.t_x[kc], C.t_ident], [t_bk], inc=(cc == 3))
                K.tt("dve", sb_[:, c4 * 512:(c4 + 1) * 512], bk[:, :], sb_[:, c4 * 512:(c4 + 1) * 512], ALU.add,
                     [t_bk, t_sb], [t_sb])
            K.dma("pool", key + "s", h_mid[o * T + tb * 128:o * T + (tb + 1) * 128, :], sb_[:, :], reads=[t_sb], writes=[])


def build_phaseA(nc):
    stack = contextlib.ExitStack()
    with stack:
        K = Kern(nc, stack)
        NP = NSLOT * SLOT_ST
        ntok = (NP + O_ST) * T
        x_all = dram_in(nc, "x_all", [ntok, D])
        pos_all = dram_in(nc, "pos_all", [1, ntok], I32)
        w_in = dram_in(nc, "w_in", [D, 5184])
        w_uq = dram_in(nc, "w_uq", [512, 1536])
        w_ukv = dram_in(nc, "w_ukv", [512, 2048])
        w_out = dram_in(nc, "w_out", [D, D])
        vecs = dram_in(nc, "vecsA", [128, NVA])
        h_mid = dram_out(nc, "h_mid", [O_ST * T, D])
        C = Ctx(K)
        A = PhaseA(K, C, nc)
        A.prologue(w_in, w_uq, w_ukv, w_out)
        A.alloc(vecs, pos_all, ntok)
        for g in range(NP):
            A.front(x_all, g)
            A.mla_kv(g)
            A.hgrn(False, A.vec[:, 69 + g // SLOT_ST:70 + g // SLOT_ST])
        for o in range(O_ST):
            g = NP + o
            A.front(x_all, g)
            A.mla_kv(g)
            A.hgrn(True, None)
            A.mla_q()
            A.attention(g, NP)
            A.out_proj(x_all, g, h_mid, o)
        K.finish()
        K.replay()
    return nc


def phaseA_inputs(inp, b, q):
    NP = NSLOT * SLOT_ST * T
    NO = O_ST * T
    x = inp["x"][b]
    pos = inp["positions"][b]
    x_all = np.empty((NP + NO, D), np.float32)
    pos_all = np.empty((1, NP + NO), np.int32)
    for i in range(NSLOT):
        src = max(q - NSLOT + i, 0)
        x_all[i * SEG:(i + 1) * SEG] = x[src * SEG:(src + 1) * SEG]
        pos_all[0, i * SEG:(i + 1) * SEG] = pos[src * SEG:(src + 1) * SEG]
    x_all[NP:] = x[q * SEG:(q + 1) * SEG]
    pos_all[0, NP:] = pos[q * SEG:(q + 1) * SEG]
    return {"x_all": x_all, "pos_all": pos_all, "w_in": inp["e_w_in"][0], "w_uq": inp["e_w_uq"][0],
            "w_ukv": inp["e_w_ukv"][0], "w_out": inp["e_w_out"][0], "vecsA": vecsA_host(inp, q)}


RH = 8
NRANK = 4
NVC = 32


def ret_tables():
    g = 1.0 - 2.0 ** (-5.0 - np.arange(8, dtype=np.float64))
    idx = np.arange(128, dtype=np.float64)
    DT = np.zeros((128, 8, 128), np.float32)
    for h in range(8):
        diff = idx[None, :] - idx[:, None]
        DT[:, h, :] = np.where(diff >= 0, g[h] ** np.maximum(diff, 0), 0.0) / 16.0
    decq = np.zeros((128, 8, 128), np.float32)
    for h in range(8):
        decq[:, h, :] = (g[h] ** (idx + 1.0))[None, :]
    kdec = np.zeros((128, 8), np.float32)
    for h in range(8):
        kdec[:, h] = g[h] ** (127.0 - idx) / 16.0
    cdec = [float(g[h] ** 128.0) for h in range(8)]
    return DT, decq, kdec, cdec, g


def vecsC_host(inp, q):
    v = np.zeros((128, 16 + 1 + NVC), np.float32)
    v[:, 0:16] = inp["norm_mix"][1].reshape(16, 128).T
    v[:, 16] = (10000.0 ** (-np.arange(128, dtype=np.float32) / np.float32(128))).astype(np.float32)
    g = 1.0 - 2.0 ** (-5.0 - np.arange(8, dtype=np.float64))
    for r in range(NRANK):
        for h in range(8):
            v[:, 17 + r * 8 + h] = float(g[h] ** (float(SEG) * (q - 1 - r))) if r < q else 0.0
    return v


class PhaseC:
    def __init__(self, K, C, nc, full, kvs=None):
        self.K, self.C, self.nc, self.full, self.kvs = K, C, nc, full, kvs

    def prologue(self, w_in, w_out, scr0=None):
        K = self.K
        with contextlib.ExitStack() as st2:
            old = K.stack
            K.stack = st2
            WC = WCast(K, nslots=3)
            scr = dict(scr0) if scr0 else {}
            if "k" not in scr and not (self.full and self.kvs is not None):
                scr["k"] = WC.cast(w_in[:, 2048:4096], D, 2048, "sC_k", cw=128)
                scr["v"] = WC.cast(w_in[:, 4096:8192], D, 4096, "sC_v", cw=256)
            if self.full:
                scr["q"] = WC.cast(w_in[:, 0:2048], D, 2048, "sC_q", cw=128)
                scr["g"] = WC.cast(w_in[:, 8192:12288], D, 4096, "sC_g", cw=128)
                scr["wout"] = WC.cast(w_out, 4096, D, "sC_wout", cw=128)
            K.barrier()
            K.stack = old
        self.scr = scr

    def alloc(self, vecs, pos_own):
        K, C, nc, full = self.K, self.C, self.nc, self.full
        self.vec = K.sb("vecC", [128, 17 + NVC], F32)
        self.t_vec = K.trk("vecC")
        K.dma("sp", "const", self.vec[:], vecs[:, :], writes=[self.t_vec])
        DT, decq, kdec, cdec, _ = ret_tables()
        self.cdec = cdec
        self.t_cst = K.trk("cstC")
        self.kdec = K.sb("kdec", [128, 8], F32)
        K.dma("sp", "const", self.kdec[:], nc.inline_tensor(kdec, name="kdec_d%d" % int(full)).ap()[:, :], writes=[self.t_cst])
        if full:
            self.DT = K.sb("DT", [128, 8, 128], F32)
            self.decq = K.sb("decq", [128, 8, 128], F32)
            K.dma("sp", "const", self.DT[:], nc.inline_tensor(DT, name="DT_d%d" % int(full)).ap()[:, :, :], writes=[self.t_cst])
            K.dma("sp", "const", self.decq[:], nc.inline_tensor(decq, name="decq_d%d" % int(full)).ap()[:, :, :], writes=[self.t_cst])
        self.pos = pos_own
        self.xT = K.sb("hTc", [128, 16, T], F32)
        self.t_x = [K.trk("x") for _ in range(16)]
        self.wk_i = 0
        self.uT = K.sb("uTc", [128, 16, T], BF16)
        self.t_u = [K.trk("u") for _ in range(16)]
        self.rstd = K.sb("rstdC", [128, T], F32)
        self.t_rstd = K.trk("rstd")
        self.stage = Stage(K, "stgC", D, 2 if not full else 1)
        self.W = WStream(K, "wC", 16, 128, 2 if full else 3)
        self.use_kvs = self.kvs is not None
        nb_ = 2 if (full and self.use_kvs) else 1
        if not (full and self.use_kvs):
            self.WV = WStream(K, "wCv", 16, 256, 2)
        self.posi = K.sb("posiC", [128, T], I32)
        self.cos = K.sb("cosC", [128, T], F32)
        self.sin = K.sb("sinC", [128, T], F32)
        self.t_rope = K.trk("rope")
        self.t_cs = K.trk("cossin")
        self.wb = [K.sb("wbC%d" % i, [128, T], BF16) for i in range(4)]
        self.t_wb = [K.trk("wb") for _ in range(4)]
        self.wb_i = 0
        self.R = K.sb("R", [128, 8, 2, 512], F32)
        self.t_R = [[K.trk("R") for _ in range(2)] for _ in range(8)]
        self.vtm_b = [K.sb("vtmC%d" % i, [128, 4, 512], BF16) for i in range(nb_)]
        self.t_vtm_b = [[K.trk("vtm") for _ in range(4)] for _ in range(nb_)]
        self.kdtm_b = [K.sb("kdtm%d" % i, [128, 4, 256], BF16) for i in range(nb_)]
        self.t_kdtm_b = [[K.trk("kdtm") for _ in range(2)] for _ in range(nb_)]
        self.kr_b = [K.sb("krC%d" % i, [128, 2, T], BF16) for i in range(nb_)] if (full or self.use_kvs) else None
        self.t_kr_b = [K.trk("kr") for _ in range(nb_)]
        self.hb_i = 0
        if full:
            self.Rb = K.sb("Rb", [128, 2, 2, 512], BF16)
            self.t_Rb = [[K.trk("Rb") for _ in range(2)] for _ in range(2)]
            self.onT = K.sb("onT", [128, 32, T], BF16)
            self.t_on = [K.trk("on") for _ in range(32)]
            self.sq = self.onT
            self.t_sq = self.t_on
            self.WO = WStream(K, "wCo", 32, 128, 2)
            self.qr = K.sb("qrC", [128, 2, T], BF16)
            self.qd = K.sb("qdC", [128, 2, T], BF16)
            self.t_qr = K.trk("qr")
            self.t_qd = K.trk("qd")
            self.AD = [K.sb("AD%d" % i, [128, 128], BF16) for i in range(2)]
            self.t_AD = [K.trk("AD") for _ in range(2)]
        else:
            self.sq = K.sb("sqC", [128, 16, T], BF16)
            self.t_sq = [K.trk("sq") for _ in range(16)]

    def work(self):
        i = self.wk_i % 16
        self.wk_i += 1
        return self.xT[:, i, :], self.t_x[i]

    def workb(self):
        i = self.wb_i % 4
        self.wb_i += 1
        return self.wb[i], self.t_wb[i]

    def init_state(self, Rprev, t_Rprev=None):
        K = self.K
        self.t_Rprev = t_Rprev if t_Rprev is not None else K.trk("none")
        for h in range(8):
            for dc in range(2):
                if Rprev is None:
                    K.op("pool", lambda e, h=h, dc=dc: e.memset(self.R[:, h, dc, :], 0.0), writes=[self.t_R[h][dc]])
                    continue
                for i in range(NRANK - 1):
                    tmp, t_tmp = self.work()
                    src = Rprev[h // 2][i, h % 2, dc, :, :] if isinstance(Rprev, list) else Rprev[i, h, dc, :, :]
                    K.dma("sp", "rin%d" % (i % 2), tmp[:, :], src, reads=[self.t_Rprev], writes=[t_tmp])
                    c = self.vec[:, 17 + i * 8 + h:18 + i * 8 + h]
                    if i == 0:
                        K.ts("dve", self.R[:, h, dc, :], tmp[:, :], c, None, ALU.mult, None, [t_tmp, self.t_vec],
                             [self.t_R[h][dc]])
                    else:
                        K.stt(self.R[:, h, dc, :], tmp[:, :], c, self.R[:, h, dc, :], ALU.mult, ALU.add,
                              [t_tmp, self.t_vec, self.t_R[h][dc]], [self.t_R[h][dc]])

    def rope_tables(self, o):
        K, v = self.K, self.vec
        K.dma("sp", "posi", self.posi[:, :], self.pos[0:1, o * T:(o + 1) * T].broadcast_to([128, T]),
              writes=[self.t_rope])
        posf, t_pf = self.work()
        ra, t_ra = self.work()
        rn, t_rn = self.work()
        K.copy("dve", posf, self.posi[:, :], [self.t_rope], [t_pf])
        for off, out in ((0.0, self.sin), (float(np.pi / 2), self.cos)):
            K.ts("dve", ra, posf, v[:, 16:17], off, ALU.mult, ALU.add, [t_pf, self.t_vec], [t_ra])
            K.ts("dve", rn, ra, float(1.0 / TWO_PI), MAGIC, ALU.mult, ALU.add, [t_ra], [t_rn])
            K.ts("dve", rn, rn, -MAGIC, None, ALU.add, None, [t_rn], [t_rn])
            K.stt(ra, rn, -C1, ra, ALU.mult, ALU.add, [t_rn, t_ra], [t_ra])
            K.stt(ra, rn, -C2, ra, ALU.mult, ALU.add, [t_rn, t_ra], [t_ra])
            K.ts("dve", ra, ra, -PI_LO, PI_LO, ALU.max, ALU.min, [t_ra], [t_ra])
            K.act(out[:, :], ra, AF.Sin, [t_ra], [self.t_cs])

    def proj_fm(self, w, t_w, rhsT, t_rhs, nkc=16):
        K, C = self.K, self.C
        bk, t_bk = C.bank()
        for kc in range(nkc):
            K.mm(bk[:, :], w[:, kc, :], rhsT[:, kc, :], kc == 0, kc == nkc - 1, [t_w, t_rhs[kc]], [t_bk])
        return bk, t_bk

    def rope_pair(self, b1, t_b1, b2, t_b2, out1, out2, t_outs, f32=False):
        K = self.K
        a, t_a = self.work()
        b, t_b = self.work()
        K.tt("dve", a, b1[:, :], self.cos[:, :], ALU.mult, [t_b1, self.t_cs], [t_a])
        K.tt("dve", b, b2[:, :], self.sin[:, :], ALU.mult, [t_b2, self.t_cs], [t_b])
        K.tt("pool", out1, a, b, ALU.subtract, [t_a, t_b], t_outs)
        c, t_c = self.work()
        d, t_d = self.work()
        K.tt("dve", c, b2[:, :], self.cos[:, :], ALU.mult, [t_b2, self.t_cs], [t_c])
        K.tt("dve", d, b1[:, :], self.sin[:, :], ALU.mult, [t_b1, self.t_cs], [t_d])
        K.tt("pool", out2, c, d, ALU.add, [t_c, t_d], t_outs)

    def supertile(self, h_in, o, h_out=None):
        K, C, full = self.K, self.C, self.full
        NOT = K.trk("none")
        load_tokmajor_T(C, h_in[o * T:(o + 1) * T, :], D, self.xT, self.t_x, self.stage)
        rmsnorm_fm(C, self.xT, self.t_x, 16, self.vec[:, 0:16], self.t_vec, self.uT, self.t_u,
                   self.sq, self.t_sq, self.rstd, self.t_rstd, D)
        self.rope_tables(o)
        for h in range(8):
            bi = self.hb_i % len(self.vtm_b)
            self.hb_i += 1
            self.vtm, self.t_vtm = self.vtm_b[bi], self.t_vtm_b[bi]
            self.kdtm, self.t_kdtm = self.kdtm_b[bi], self.t_kdtm_b[bi]
            if self.kr_b is not None:
                self.kr, self.t_kr = self.kr_b[bi], self.t_kr_b[bi]
            load_kv = full and self.use_kvs
            if load_kv:
                K.dma("sp", "kvl%d" % bi, self.kdtm[:, :, :], self.kvs["kd"][o, h], writes=self.t_kdtm)
                K.dma("sp", "kvv%d" % bi, self.vtm[:, :, :], self.kvs["v"][o, h], writes=self.t_vtm)
                K.dma("sp", "kvr%d" % bi, self.kr[:, :, :], self.kvs["kr"][o, h], writes=[self.t_kr])
            else:
                wk1, t_wk1 = self.W.load(self.scr["k"], NOT, 2 * h)
                b1, t_b1 = self.proj_fm(wk1, t_wk1, self.uT, self.t_u)
                wk2, t_wk2 = self.W.load(self.scr["k"], NOT, 2 * h + 1)
                b2, t_b2 = self.proj_fm(wk2, t_wk2, self.uT, self.t_u)
                k1, t_k1 = self.work()
                k2, t_k2 = self.work()
                self.rope_pair(b1, t_b1, b2, t_b2, k1, k2, [t_k1, t_k2])
                for half in range(2):
                    wv, t_wv = self.WV.load(self.scr["v"], NOT, 2 * h + half)
                    for tb in range(4):
                        bk, t_bk = C.bank()
                        for kc in range(16):
                            K.mm(bk[:, 0:256], self.uT[:, kc, tb * 128:(tb + 1) * 128], wv[:, kc, :], kc == 0, kc == 15,
                                 [self.t_u[kc], t_wv], [t_bk])
                        K.copy("act", self.vtm[:, tb, half * 256:(half + 1) * 256], bk[:, 0:256], [t_bk], [self.t_vtm[tb]])
            if full:
                wq1, t_wq1 = self.W.load(self.scr["q"], NOT, 2 * h)
                q1, t_q1 = self.proj_fm(wq1, t_wq1, self.uT, self.t_u)
                wq2, t_wq2 = self.W.load(self.scr["q"], NOT, 2 * h + 1)
                q2, t_q2 = self.proj_fm(wq2, t_wq2, self.uT, self.t_u)
                self.rope_pair(q1, t_q1, q2, t_q2, self.qr[:, 0, :], self.qr[:, 1, :], [self.t_qr])
                dq = self.decq[:, h:h + 1, :].broadcast_to([128, 4, 128])
                for dc in range(2):
                    K.tt("dve", self.qd[:, dc, :].rearrange("p (c t) -> p c t", c=4),
                         self.qr[:, dc, :].rearrange("p (c t) -> p c t", c=4), dq, ALU.mult,
                         [self.t_qr, self.t_cst], [self.t_qd])
            for dc, (kf, t_kf) in (enumerate(((k1, t_k1), (k2, t_k2))) if not load_kv else ()):
                bt, t_bt = C.bank()
                for c in range(4):
                    K.tr(bt[:, c * 128:(c + 1) * 128], kf[:, c * 128:(c + 1) * 128], C.ident[:], [t_kf, C.t_ident],
                         [t_bt], inc=(c == 3))
                K.ts("dve", self.kdtm[:, :, dc * 128:(dc + 1) * 128], bt[:, :].rearrange("p (c d) -> p c d", c=4),
                     self.kdec[:, h:h + 1], None, ALU.mult, None, [t_bt, self.t_cst], [self.t_kdtm[dc]])
                if self.kr_b is not None:
                    K.copy("act", self.kr[:, dc, :], kf, [t_kf], [self.t_kr])
            if self.use_kvs and not full:
                K.dma("pool", "kvs0", self.kvs["kd"][o, h], self.kdtm[:, :, :], reads=self.t_kdtm, writes=[])
                K.dma("pool", "kvs1", self.kvs["v"][o, h], self.vtm[:, :, :], reads=self.t_vtm, writes=[])
                K.dma("pool", "kvs2", self.kvs["kr"][o, h], self.kr[:, :, :], reads=[self.t_kr], writes=[])
            if full:
                bo = [C.bank(hold=True) for _ in range(4)]
                for dc in range(2):
                    K.copy("pool", self.Rb[:, h % 2, dc, :], self.R[:, h, dc, :], [self.t_R[h][dc]], [self.t_Rb[h % 2][dc]])
            for c in range(4):
                cs = slice(c * 128, (c + 1) * 128)
                if full:
                    ba, t_ba = C.bank()
                    for dc in range(2):
                        K.mm(ba[:, 0:128], self.kr[:, dc, cs], self.qr[:, dc, cs], dc == 0, dc == 1,
                             [self.t_kr, self.t_qr], [t_ba])
                    ad, t_ad = self.AD[c % 2], self.t_AD[c % 2]
                    K.tt("dve", ad[:, :], ba[:, 0:128], self.DT[:, h, :], ALU.mult, [t_ba, self.t_cst], [t_ad])
                    for vc in range(4):
                        bov, t_bov = bo[vc]
                        K.mm(bov[:, cs], self.vtm[:, c, vc * 128:(vc + 1) * 128], ad[:, :], True, False,
                             [self.t_vtm[c], t_ad], [t_bov], inc=False)
                        for dc in range(2):
                            K.mm(bov[:, cs], self.Rb[:, h % 2, dc, vc * 128:(vc + 1) * 128], self.qd[:, dc, cs], False, dc == 1,
                                 [self.t_Rb[h % 2][dc], self.t_qd], [t_bov], inc=(dc == 1))
                for dc in range(2):
                    bs, t_bs = C.bank()
                    K.mm(bs[:, :], self.kdtm[:, c, dc * 128:(dc + 1) * 128], self.vtm[:, c, :], True, True,
                         [self.t_kdtm[dc], self.t_vtm[c]], [t_bs])
                    K.stt(self.R[:, h, dc, :], self.R[:, h, dc, :], self.cdec[h], bs[:, :], ALU.mult, ALU.add,
                          [self.t_R[h][dc], t_bs], [self.t_R[h][dc]])
                    if full and c < 3:
                        K.copy("pool", self.Rb[:, h % 2, dc, :], self.R[:, h, dc, :], [self.t_R[h][dc]], [self.t_Rb[h % 2][dc]])
            if full:
                ofs = []
                sqs = []
                for vc in range(4):
                    bov, t_bov = bo[vc]
                    of, t_of = self.work()
                    K.copy("act", of, bov[:, :], [t_bov], [t_of])
                    sqb, t_sqb = self.workb()
                    K.act(sqb[:, :], bov[:, :], AF.Square, [t_bov], [t_sqb])
                    C.release(t_bov)
                    ofs.append((of, t_of))
                    sqs.append((sqb, t_sqb))
                bk, t_bk = C.bank()
                for vc in range(4):
                    K.mm(bk[:, :], C.ones[:, :], sqs[vc][0][:, :], vc == 0, vc == 3, [C.t_ones, sqs[vc][1]], [t_bk])
                ro, t_ro = self.work()
                K.act(ro, bk[:, :], AF.Ln, [t_bk], [t_ro], bias=EPS, scale=1.0 / 512)
                K.act(ro, ro, AF.Exp, [t_ro], [t_ro], scale=-0.5)
                for vc in range(4):
                    wg_, t_wg = self.W.load(self.scr["g"], NOT, 4 * h + vc)
                    bg, t_bg = self.proj_fm(wg_, t_wg, self.uT, self.t_u)
                    sg, t_sg = self.work()
                    K.act(sg, bg[:, :], AF.Silu, [t_bg], [t_sg])
                    of, t_of = ofs[vc]
                    K.tt("dve", of, of, ro, ALU.mult, [t_of, t_ro], [t_of])
                    K.tt("pool", self.onT[:, 4 * h + vc, :], of, sg, ALU.mult, [t_of, t_sg], [self.t_on[4 * h + vc]])
        if full:
            for n in range(16):
                w, t_w = self.WO.load(self.scr["wout"], NOT, n)
                bk, t_bk = self.proj_fm(w, t_w, self.onT, self.t_on, nkc=32)
                K.copy("act", self.xT[:, n, :], bk[:, :], [t_bk], [self.t_x[n]])
            for tb in range(4):
                sb_, t_sb, key = self.stage.next()
                K.dma("sp", key, sb_[:, :], h_in[o * T + tb * 128:o * T + (tb + 1) * 128, :], writes=[t_sb])
                for c4 in range(4):
                    bk, t_bk = C.bank()
                    for cc in range(4):
                        kc = c4 * 4 + cc
                        K.tr(bk[:, cc * 128:(cc + 1) * 128], self.xT[:, kc, tb * 128:(tb + 1) * 128], C.ident[:],
                             [self.t_x[kc], C.t_ident], [t_bk], inc=(cc == 3))
                    K.tt("dve", sb_[:, c4 * 512:(c4 + 1) * 512], bk[:, :], sb_[:, c4 * 512:(c4 + 1) * 512], ALU.add,
                         [t_bk, t_sb], [t_sb])
                K.dma("pool", key + "s", h_out[o * T + tb * 128:o * T + (tb + 1) * 128, :], sb_[:, :], reads=[t_sb],
                      writes=[])

    def store_state(self, R_out):
        K = self.K
        for h in range(8):
            for dc in range(2):
                dst = R_out[h // 2][h % 2, dc, :, :] if isinstance(R_out, list) else R_out[h, dc, :, :]
                K.dma("pool", "rout%d" % dc, dst, self.R[:, h, dc, :], reads=[self.t_R[h][dc]], writes=[])


def build_phaseC(nc, full):
    stack = contextlib.ExitStack()
    with stack:
        K = Kern(nc, stack)
        h_in = dram_in(nc, "h_in", [O_ST * T, D])
        pos = dram_in(nc, "pos", [1, O_ST * T], I32)
        w_in = dram_in(nc, "w_in", [D, 12288])
        vecs = dram_in(nc, "vecsC", [128, 17 + NVC])
        if full:
            w_out = dram_in(nc, "w_out", [4096, D])
            Rprev = dram_in(nc, "Rprev", [NRANK, 8, 2, 128, 512])
            h_out = dram_out(nc, "h_mid", [O_ST * T, D])
        else:
            w_out = None
            R_out = dram_out(nc, "R_out", [8, 2, 128, 512])
        C = Ctx(K)
        P = PhaseC(K, C, nc, full)
        P.prologue(w_in, w_out)
        P.alloc(vecs, pos)
        P.init_state(Rprev if full else None)
        for o in range(O_ST):
            P.supertile(h_in, o, h_out if full else None)
        if not full:
            P.store_state(R_out)
        K.finish()
        K.replay()
    return nc


def _launch(build, maps):
    nc = bass.Bass("TRN2", target_bir_lowering=False)
    build(nc)
    res = run_bass_kernel_spmd(nc, maps, core_ids=list(range(NCORE)))
    return res.results


def _ffn_maps(inp, li, hs):
    maps = []
    vec = ffn_vecs(inp["norm_ffn"][li], inp["norm_ple"][li], inp["ffn_conv_w"][li], inp["ffn_conv_b"][li])
    for c in range(NCORE):
        b, q = c // 4, c % 4
        hprev = np.ascontiguousarray(hs[c - 1][-2:, :]) if q > 0 else np.zeros((2, D), np.float32)
        maps.append({"h_in": hs[c], "hprev": hprev,
                     "p_in": np.ascontiguousarray(inp["p"][li, b, q * SEG:(q + 1) * SEG]),
                     "w_gate": inp["ffn_w_gate"][li], "w_up": inp["ffn_w_up"][li], "w_down": inp["ffn_w_down"][li],
                     "w_pg": inp["ple_w_gate"][li], "w_pp": inp["ple_w_proj"][li], "vecs": vec})
    return maps


def kernel_unfused(**inputs):
    inp = {k: np.asarray(v) for k, v in inputs.items()}
    r = _launch(build_phaseA, [phaseA_inputs(inp, c // 4, c % 4) for c in range(NCORE)])
    hmid0 = [np.ascontiguousarray(x["h_mid"]) for x in r]
    r = _launch(build_ffn_phase, _ffn_maps(inp, 0, hmid0))
    h1 = [np.ascontiguousarray(x["h_out"]) for x in r]
    posm = [np.ascontiguousarray(inp["positions"][c // 4][None, (c % 4) * SEG:(c % 4 + 1) * SEG]) for c in range(NCORE)]
    w_in1 = inp["o_w_in"][0]
    r = _launch(lambda nc: build_phaseC(nc, False),
                [{"h_in": h1[c], "pos": posm[c], "w_in": w_in1, "vecsC": vecsC_host(inp, c % 4)} for c in range(NCORE)])
    Rl = [x["R_out"] for x in r]
    maps = []
    for c in range(NCORE):
        q = c % 4
        Rprev = np.zeros((NRANK, 8, 2, 128, 512), np.float32)
        for r_ in range(q):
            Rprev[r_] = Rl[c - q + r_]
        maps.append({"h_in": h1[c], "pos": posm[c], "w_in": w_in1, "w_out": inp["o_w_out"][0], "Rprev": Rprev,
                     "vecsC": vecsC_host(inp, q)})
    r = _launch(lambda nc: build_phaseC(nc, True), maps)
    hmid1 = [np.ascontiguousarray(x["h_mid"]) for x in r]
    r = _launch(build_ffn_phase, _ffn_maps(inp, 1, hmid1))
    out = np.empty((2, 4 * SEG, D), np.float32)
    for c in range(NCORE):
        out[c // 4, (c % 4) * SEG:(c % 4 + 1) * SEG] = r[c]["h_out"]
    return out


GROUPS = [[0, 1, 2, 3], [4, 5, 6, 7]]
DEBUG_OUT = False


def build_fused(nc):
    stack = contextlib.ExitStack()
    with stack:
        K = Kern(nc, stack)
        NP = NSLOT * SLOT_ST
        ntok = (NP + O_ST) * T
        x_all = dram_in(nc, "x_all", [ntok, D])
        pos_all = dram_in(nc, "pos_all", [1, ntok], I32)
        w_in = dram_in(nc, "w_in", [D, 5184])
        w_uq = dram_in(nc, "w_uq", [512, 1536])
        w_ukv = dram_in(nc, "w_ukv", [512, 2048])
        w_out = dram_in(nc, "w_out", [D, D])
        vecsA = dram_in(nc, "vecsA", [128, NVA])
        ffw = []
        for li in range(2):
            ffw.append(dict(
                p=dram_in(nc, "p%d" % li, [SEG, PLE]),
                gate=dram_in(nc, "w_gate%d" % li, [D, DFF]), up=dram_in(nc, "w_up%d" % li, [D, DFF]),
                down=dram_in(nc, "w_down%d" % li, [DFF, D]), pg=dram_in(nc, "w_pg%d" % li, [D, D]),
                pp=dram_in(nc, "w_pp%d" % li, [PLE, D]), vecs=dram_in(nc, "vecsF%d" % li, [128, NVF])))
        o_w_in = dram_in(nc, "o_w_in", [D, 12288])
        o_w_out = dram_in(nc, "o_w_out", [4096, D])
        vecsC = dram_in(nc, "vecsC", [128, 17 + NVC])
        out = dram_out(nc, "out", [SEG, D])
        internal = lambda name, shape: nc.dram_tensor(name, list(shape), F32, kind="Internal").ap()
        mk = (lambda name, shape: dram_out(nc, name, shape)) if DEBUG_OUT else internal
        hA = mk("hA", [SEG, D])
        hB = mk("hB", [SEG, D])
        hC = mk("hC", [SEG, D])
        hl = [internal("hl%d" % i, [2, D]) for i in range(2)]
        hg = [internal("hg%d" % i, [8, D]) for i in range(2)]
        Rloc = [internal("Rloc%d" % i, [512, 512]) for i in range(4)]
        Rall = [internal("Rall%d" % i, [NRANK * 512, 512]) for i in range(4)]
        C = Ctx(K)
        BYP = mybir.AluOpType.bypass

        with K.phase("A_"):
            A = PhaseA(K, C, nc)
            A.prologue(w_in, w_uq, w_ukv, w_out)
            A.alloc(vecsA, pos_all, ntok)
            for g in range(NP):
                A.front(x_all, g)
                A.mla_kv(g)
                A.hgrn(False, A.vec[:, 69 + g // SLOT_ST:70 + g // SLOT_ST])
            for o in range(O_ST):
                g = NP + o
                A.front(x_all, g)
                A.mla_kv(g)
                A.hgrn(True, None)
                A.mla_q()
                A.attention(g, NP)
                A.out_proj(x_all, g, hA, o)

        def exchange(i, hsrc):
            t_hl = K.trk("hl")
            K.dma("sp", "xch", hl[i][:, :], hsrc[SEG - 2:SEG, :], writes=[t_hl])
            t_g = K.trk("hg")
            K.barrier()
            K.collective("AllGather", BYP, GROUPS, hl[i], hg[i], reads=[t_hl], writes=[t_g])
            K.barrier()
            return t_g

        t_g0 = exchange(0, hA)
        f = ffw[0]
        with K.phase("B_"):
            scr = cast_ffn_weights(K, f["gate"], f["up"], f["down"], f["pg"], f["pp"], pfx="B")
            ffn_phase_body(K, C, hA, None, f["p"], f["vecs"], scr, hB, gath=hg[0], t_gath=t_g0)
        pos_own = pos_all[:, NP * T:NP * T + SEG]
        ibf = lambda name, shape: nc.dram_tensor(name, list(shape), BF16, kind="Internal").ap()
        kvs = {"kd": ibf("kvs_kd", [O_ST, 8, 128, 4, 256]), "v": ibf("kvs_v", [O_ST, 8, 128, 4, 512]),
               "kr": ibf("kvs_kr", [O_ST, 8, 128, 2, T])}
        with K.phase("C1_"):
            P1 = PhaseC(K, C, nc, False, kvs)
            P1.prologue(o_w_in, None)
            P1.alloc(vecsC, pos_own)
            P1.init_state(None)
            for o in range(O_ST):
                P1.supertile(hB, o)
            P1.store_state([Rloc[i].rearrange("(h dc p) n -> h dc p n", h=2, dc=2) for i in range(4)])
        t_R = K.trk("Rall")
        for i in range(4):
            K.collective("AllGather", BYP, GROUPS, Rloc[i], Rall[i], reads=[], writes=[t_R])
            K.barrier()
        with K.phase("C2_"):
            P2 = PhaseC(K, C, nc, True, kvs)
            P2.prologue(o_w_in, o_w_out, scr0=P1.scr)
            P2.alloc(vecsC, pos_own)
            P2.init_state([Rall[i].rearrange("(r h dc p) n -> r h dc p n", r=NRANK, h=2, dc=2) for i in range(4)], t_R)
            for o in range(O_ST):
                P2.supertile(hB, o, hC)
        t_g1 = exchange(1, hC)
        f = ffw[1]
        with K.phase("D_"):
            scr = cast_ffn_weights(K, f["gate"], f["up"], f["down"], f["pg"], f["pp"], pfx="D")
            ffn_phase_body(K, C, hC, None, f["p"], f["vecs"], scr, out, gath=hg[1], t_gath=t_g1)
        K.finish()
        K.replay()
    return nc


def fused_inputs(inp, c):
    b, q = c // 4, c % 4
    m = phaseA_inputs(inp, b, q)
    for li in range(2):
        m["p%d" % li] = np.ascontiguousarray(inp["p"][li, b, q * SEG:(q + 1) * SEG])
        m["w_gate%d" % li] = inp["ffn_w_gate"][li]
        m["w_up%d" % li] = inp["ffn_w_up"][li]
        m["w_down%d" % li] = inp["ffn_w_down"][li]
        m["w_pg%d" % li] = inp["ple_w_gate"][li]
        m["w_pp%d" % li] = inp["ple_w_proj"][li]
        m["vecsF%d" % li] = ffn_vecs(inp["norm_ffn"][li], inp["norm_ple"][li], inp["ffn_conv_w"][li],
                                      inp["ffn_conv_b"][li], q)
    m["o_w_in"] = inp["o_w_in"][0]
    m["o_w_out"] = inp["o_w_out"][0]
    m["vecsC"] = vecsC_host(inp, q)
    return m


def kernel(**inputs):
    inp = {k: np.asarray(v) for k, v in inputs.items()}
    nc = bass.Bass("TRN2", target_bir_lowering=False)
    build_fused(nc)
    maps = [fused_inputs(inp, c) for c in range(NCORE)]
    res = run_bass_kernel_spmd(nc, maps, core_ids=list(range(NCORE)))
    out = np.empty((2, 4 * SEG, D), np.float32)
    for c in range(NCORE):
        out[c // 4, (c % 4) * SEG:(c % 4 + 1) * SEG] = res.results[c]["out"]
    return out
```

```python
import contextlib
import numpy as np
import concourse.bass as bass
import concourse.mybir as mybir
from concourse.bass_utils import run_bass_kernel_spmd

F32 = mybir.dt.float32
BF16 = mybir.dt.bfloat16
I32 = mybir.dt.int32
AF = mybir.ActivationFunctionType
ALU = mybir.AluOpType

D = 2048
NCORE = 8
SEG = 4096
T = 512
NST = SEG // T
DFF = 5632
NFF = DFF // 128
PLE = 256
EPS = 1e-6


class Trk:
    __slots__ = ("name", "w", "r")

    def __init__(self, name):
        self.name = name
        self.w = None
        self.r = {}


class Stream:
    __slots__ = ("name", "sem", "cnt")

    def __init__(self, name, sem):
        self.name, self.sem, self.cnt = name, sem, 0


class Kern:
    ENG = ("pe", "act", "dve", "pool", "sp")

    def __init__(self, nc, stack):
        self.nc = nc
        self.stack = stack
        self.root = stack
        self.pfx = ""
        self.ncc = 0
        self.ops = {e: [] for e in self.ENG}
        self.streams = {}
        for e in ("pe", "act", "dve", "pool"):
            self.streams[e] = Stream(e, stack.enter_context(nc.semaphore("s_" + e)))
        self.seen = {e: {} for e in self.ENG}
        self.dma_streams = {}
        self.nbank = 0
        self.uid = 0

    def sb(self, name, shape, dt):
        return self.stack.enter_context(self.nc.sbuf_tensor(self.pfx + name, list(shape), dt))

    @contextlib.contextmanager
    def phase(self, pfx):
        old, oldp = self.stack, self.pfx
        with contextlib.ExitStack() as st2:
            self.stack = st2
            self.pfx = pfx
            yield
            self.barrier()
            self.stack, self.pfx = old, oldp

    def collective(self, kind, op, groups, in_ap, out_ap, reads=(), writes=()):
        key = "cc%d" % self.ncc
        self.ncc += 1
        st = Stream(key, self.root.enter_context(self.nc.semaphore("c_" + key)))
        self.dma_streams[key] = st
        deps = self._deps(reads, writes)
        self._emit_waits("pool", deps)
        st.cnt += 1
        self.ops["pool"].append(("ins", lambda e: e.collective_compute(kind, op, replica_groups=groups,
                                                                       ins=[in_ap.opt()], outs=[out_ap.opt()]),
                                 st.sem, None))
        for t in reads:
            if t.r.get(st, 0) < 1:
                t.r[st] = 1
        for t in writes:
            t.w = (st, 1)
            t.r = {}

    def psum_bank(self):
        t = self.stack.enter_context(self.nc.psum_tensor("bank%d" % self.nbank, [128, 512], F32))
        self.nbank += 1
        return t

    def trk(self, name="t"):
        self.uid += 1
        return Trk("%s%d" % (name, self.uid))

    def _deps(self, reads, writes):
        deps = {}
        for t in reads:
            if t.w is not None:
                s, v = t.w
                if deps.get(s, 0) < v:
                    deps[s] = v
        for t in writes:
            if t.w is not None:
                s, v = t.w
                if deps.get(s, 0) < v:
                    deps[s] = v
            for s, v in t.r.items():
                if deps.get(s, 0) < v:
                    deps[s] = v
        return deps

    def _emit_waits(self, eng, deps, own=None):
        seen = self.seen[eng]
        for s, v in deps.items():
            if s is own and eng == "pe":
                continue
            if seen.get(s, 0) >= v:
                continue
            seen[s] = v
            self.ops[eng].append(("wait", s.sem, v))

    def op(self, eng, fn, reads=(), writes=(), inc=True):
        st = self.streams[eng]
        deps = self._deps(reads, writes)
        self._emit_waits(eng, deps, own=st)
        if inc:
            st.cnt += 1
            val = st.cnt
            self.ops[eng].append(("ins", fn, st.sem, 1))
        else:
            val = st.cnt + 1
            self.ops[eng].append(("ins", fn, None, 0))
        for t in reads:
            if t.r.get(st, 0) < val:
                t.r[st] = val
        for t in writes:
            t.w = (st, val)
            t.r = {}

    def dma(self, q, key, out, in_, reads=(), writes=(), **kw):
        st = self.dma_streams.get(key)
        if st is None:
            st = Stream("dma_" + key, self.root.enter_context(self.nc.semaphore("d_" + key)))
            self.dma_streams[key] = st
        deps = self._deps(reads, writes)
        if st.cnt > 0:
            deps[st] = max(deps.get(st, 0), st.cnt)
        self._emit_waits(q, deps)
        st.cnt += 16
        val = st.cnt
        self.ops[q].append(("ins", lambda e, o=out, i=in_: e.dma_start(out=o, in_=i, **kw), st.sem, 16))
        for t in reads:
            if t.r.get(st, 0) < val:
                t.r[st] = val
        for t in writes:
            t.w = (st, val)
            t.r = {}

    def barrier(self):
        allst = list(self.streams.values()) + list(self.dma_streams.values())
        for e in self.ENG:
            for st in allst:
                if st.cnt > 0 and self.seen[e].get(st, 0) < st.cnt:
                    if e == "pe" and st is self.streams["pe"]:
                        continue
                    self.seen[e][st] = st.cnt
                    self.ops[e].append(("wait", st.sem, st.cnt))

    def finish(self):
        for st in self.dma_streams.values():
            if st.cnt > 0 and self.seen["sp"].get(st, 0) < st.cnt:
                self.ops["sp"].append(("wait", st.sem, st.cnt))
        for e in ("pe", "act", "dve", "pool"):
            st = self.streams[e]
            if st.cnt > 0:
                self.ops["sp"].append(("wait", st.sem, st.cnt))

    def replay(self):
        nc = self.nc
        with nc.Block() as block:
            def run(eng_obj, lst):
                for o in lst:
                    if o[0] == "wait":
                        eng_obj.wait_ge(o[1], o[2])
                    else:
                        ins = o[1](eng_obj)
                        if o[2] is not None:
                            if o[3] is None:
                                ins.then_inc(o[2])
                            else:
                                ins.then_inc(o[2], o[3])

            @block.tensor
            def _(e):
                run(e, self.ops["pe"])

            @block.scalar
            def _(e):
                run(e, self.ops["act"])

            @block.vector
            def _(e):
                run(e, self.ops["dve"])

            @block.gpsimd
            def _(e):
                run(e, self.ops["pool"])

            @block.sync
            def _(e):
                run(e, self.ops["sp"])

    def mm(self, out, lhsT, rhs, start, stop, reads, writes, inc=None):
        if inc is None:
            inc = stop
        self.op("pe", lambda e: e.matmul(out, lhsT, rhs, start=start, stop=stop), reads, writes, inc=inc)

    def tr(self, out, in_, ident, reads, writes, inc=True):
        self.op("pe", lambda e: e.transpose(out, in_, ident), reads, writes, inc=inc)

    def act(self, out, in_, func, reads, writes, bias=None, scale=None, eng="act"):
        kw = {}
        if bias is not None:
            kw["bias"] = bias
        if scale is not None:
            kw["scale"] = scale
        self.op(eng, lambda e: e.activation(out, in_, func, **kw), reads, writes)

    def tt(self, eng, out, in0, in1, op, reads, writes):
        self.op(eng, lambda e: e.tensor_tensor(out, in0, in1, op), reads, writes)

    def ts(self, eng, out, in0, s1, s2, op0, op1, reads, writes):
        if op1 is None:
            self.op(eng, lambda e: e.tensor_scalar(out, in0, s1, None, op0), reads, writes)
        else:
            self.op(eng, lambda e: e.tensor_scalar(out, in0, s1, s2, op0, op1), reads, writes)

    def stt(self, out, in0, scalar, in1, op0, op1, reads, writes):
        self.op("dve", lambda e: e.scalar_tensor_tensor(out, in0, scalar, in1, op0, op1), reads, writes)

    def copy(self, eng, out, in_, reads, writes):
        if eng == "act":
            self.op(eng, lambda e: e.copy(out, in_), reads, writes)
        else:
            self.op(eng, lambda e: e.tensor_copy(out, in_), reads, writes)


class Ctx:
    def __init__(self, K):
        self.K = K
        nc = K.nc
        self.banks = [K.psum_bank() for _ in range(8)]
        self.bank_trk = [K.trk("bank") for _ in range(8)]
        self.bank_i = 0
        self.held = set()
        self.ident = K.sb("ident", [128, 128], F32)
        self.t_ident = K.trk("ident")
        idd = nc.inline_tensor(np.eye(128, dtype=np.float32), name="ident_d").ap()
        K.dma("sp", "const", self.ident[:], idd[:, :], writes=[self.t_ident])
        self.ones_f = K.sb("ones_f", [128, 128], F32)
        self.ones = K.sb("ones_b", [128, 128], BF16)
        self.t_ones = K.trk("ones")
        K.op("pool", lambda e: e.memset(self.ones_f[:], 1.0), writes=[self.t_ones])
        K.copy("pool", self.ones[:], self.ones_f[:], [self.t_ones], [self.t_ones])

    def bank(self, hold=False):
        for _ in range(8):
            i = self.bank_i
            self.bank_i = (i + 1) % 8
            if i not in self.held:
                break
        else:
            raise RuntimeError("all PSUM banks held")
        if hold:
            self.held.add(i)
        return self.banks[i], self.bank_trk[i]

    def release(self, t_bk):
        self.held.discard(self.bank_trk.index(t_bk))


class WCast:
    def __init__(self, K, nslots=3, rows=8, defer=False, tag=""):
        self.K = K
        self.n = nslots
        self.f = [K.sb("wc%s_f%d" % (tag, i), [128, rows, 512], F32) for i in range(nslots)]
        self.b = [K.sb("wc%s_b%d" % (tag, i), [128, rows, 512], BF16) for i in range(nslots)]
        self.cap = rows * 512
        self.defer = defer
        self.steps = []
        self.tf = [K.trk("wcf") for _ in range(nslots)]
        self.tb = [K.trk("wcb") for _ in range(nslots)]
        self.i = 0
        self.engs = ("pool", "dve", "act")

    def pump(self, n):
        for _ in range(min(n, len(self.steps))):
            self.steps.pop(0)()

    def cast(self, w, Kd, N, name, cw=512):
        K = self.K
        nkc = Kd // 128
        nb = N // cw
        scr = K.nc.dram_tensor(name, [nb, 128, nkc, cw], BF16, kind="Internal").ap()
        wv = w.rearrange("(kc p) n -> p kc n", p=128)
        kstep = max(1, min(nkc, self.cap // cw, 8))
        for b in range(nb):
            for k0 in range(0, nkc, kstep):
                kn = min(kstep, nkc - k0)
                def step(b=b, k0=k0, kn=kn):
                    s = self.i % self.n
                    self.i += 1
                    fv = self.f[s][:].rearrange("p a b -> p (a b)")[:, 0:kn * cw].rearrange("p (a b) -> p a b", a=kn)
                    bv = self.b[s][:].rearrange("p a b -> p (a b)")[:, 0:kn * cw].rearrange("p (a b) -> p a b", a=kn)
                    K.dma("sp", "wcl%d" % s, fv, wv[:, k0:k0 + kn, b * cw:(b + 1) * cw], writes=[self.tf[s]])
                    eng = self.engs[self.i % 3]
                    K.copy(eng, bv, fv, [self.tf[s]], [self.tb[s]])
                    K.dma("pool", "wcs%d" % s, scr[b, :, k0:k0 + kn, :], bv, reads=[self.tb[s]], writes=[])
                if self.defer:
                    self.steps.append(step)
                else:
                    step()
        return scr


class Stage:
    def __init__(self, K, name, ncol, nslots):
        self.buf = [K.sb("%s%d" % (name, i), [128, ncol], F32) for i in range(nslots)]
        self.t = [K.trk(name) for _ in range(nslots)]
        self.n = nslots
        self.i = 0
        self.name = name

    def next(self):
        s = self.i % self.n
        self.i += 1
        return self.buf[s], self.t[s], "%s%d" % (self.name, s)


def load_tokmajor_T(C, src, ncol, dst, t_dst, stage):
    K = C.K
    nch = ncol // 128
    for tb in range(T // 128):
        sb_, t_sb, key = stage.next()
        K.dma("sp", key, sb_[:, 0:ncol], src[tb * 128:(tb + 1) * 128, :], writes=[t_sb])
        for c0 in range(0, nch, 4):
            cn = min(4, nch - c0)
            bk, tb_ = C.bank()
            for cc in range(cn):
                kc = c0 + cc
                K.tr(bk[:, cc * 128:(cc + 1) * 128], sb_[:, kc * 128:(kc + 1) * 128], C.ident[:],
                     [t_sb, C.t_ident], [tb_], inc=(cc == cn - 1))
            K.copy(("act", "dve")[(c0 // 4) % 2], dst[:, c0:c0 + cn, tb * 128:(tb + 1) * 128],
                   bk[:, 0:cn * 128].rearrange("p (c t) -> p c t", c=cn), [tb_], t_dst[c0:c0 + cn])


def rmsnorm_fm(C, hT, t_h, nch, g_sb, t_g, uT, t_u, sq, t_sq, rstd, t_rstd, dim, W=T):
    K = C.K
    for kc in range(nch):
        K.act(sq[:, kc, :], hT[:, kc, :], AF.Square, [t_h[kc]], [t_sq[kc]])
    bk, tbk = C.bank()
    for kc in range(nch):
        K.mm(bk[:, 0:W], C.ones[:], sq[:, kc, :], kc == 0, kc == nch - 1, [C.t_ones, t_sq[kc]], [tbk])
    K.act(rstd[:, :], bk[:, 0:W], AF.Ln, [tbk], [t_rstd], bias=EPS, scale=1.0 / dim)
    K.act(rstd[:, :], rstd[:, :], AF.Exp, [t_rstd], [t_rstd], scale=-0.5)
    for kc in range(nch):
        K.stt(uT[:, kc, :], hT[:, kc, :], g_sb[:, kc:kc + 1], rstd[:, :], ALU.mult, ALU.mult,
              [t_h[kc], t_g, t_rstd], [t_u[kc]])


class WStream:
    def __init__(self, K, name, nkc, cw, nslots):
        self.K, self.name, self.n = K, name, nslots
        self.buf = [K.sb("%s_%d" % (name, i), [128, nkc, cw], BF16) for i in range(nslots)]
        self.t = [K.trk(name) for _ in range(nslots)]
        self.i = 0

    def load(self, scr, t_scr, b, k0=0, kn=None):
        s = self.i % self.n
        self.i += 1
        nk = scr.shape[2] if kn is None else kn
        self.K.dma("sp", "%s%d" % (self.name, s), self.buf[s][:, 0:nk, :], scr[b, :, k0:k0 + nk, :],
                   reads=[t_scr], writes=[self.t[s]])
        return self.buf[s], self.t[s]


def dram_in(nc, name, shape, dt=F32):
    return nc.dram_tensor(name, list(shape), dt, kind="ExternalInput").ap()


def dram_out(nc, name, shape, dt=F32):
    return nc.dram_tensor(name, list(shape), dt, kind="ExternalOutput").ap()


def store_tokmajor(C, hT, t_h, dst, stage, q="pool"):
    K = C.K
    for tb in range(T // 128):
        sb_, t_sb, key = stage.next()
        for c4 in range(4):
            bk, t_bk = C.bank()
            for cc in range(4):
                kc = c4 * 4 + cc
                K.tr(bk[:, cc * 128:(cc + 1) * 128], hT[:, kc, tb * 128:(tb + 1) * 128], C.ident[:],
                     [t_h[kc], C.t_ident], [t_bk], inc=(cc == 3))
            K.copy(("act", "dve")[c4 % 2], sb_[:, c4 * 512:(c4 + 1) * 512], bk[:, :], [t_bk], [t_sb])
        K.dma(q, key + "s", dst[tb * 128:(tb + 1) * 128, :], sb_[:, :], reads=[t_sb], writes=[])


def ffn_phase_body(K, C, h_in, hprev, p_in, vecs, scr, h_out, gath=None, t_gath=None):
    vec = K.sb("vec", [128, NVF], F32)
    t_vec = K.trk("vec")
    K.dma("sp", "const", vec[:], vecs[:, :], writes=[t_vec])
    g_ffn = vec[:, 0:16]
    g_ple = vec[:, 16:32]
    cw = [vec[:, 32 + i * NFF:32 + (i + 1) * NFF] for i in range(3)]
    cb = vec[:, 32 + 3 * NFF:32 + 4 * NFF]
    NOT = K.trk("none")

    hT = K.sb("hT", [128, 16, T], F32)
    t_h = [K.trk("h") for _ in range(16)]
    uT = K.sb("uT", [128, 16, T], BF16)
    t_u = [K.trk("u") for _ in range(16)]
    hid = K.sb("hid", [128, NFF, T], BF16)
    t_hid = [K.trk("hid") for _ in range(NFF)]
    rstd = K.sb("rstd", [128, T], F32)
    t_rstd = K.trk("rstd")
    stage = Stage(K, "stg", D, 2)
    pstage = Stage(K, "pstg", PLE, 2)
    pT = K.sb("pT", [128, 2, T], BF16)
    t_p = [K.trk("p") for _ in range(2)]
    halo_sb = K.sb("halo_sb", [128, NFF, 2], F32)
    t_halo = [K.trk("halo") for _ in range(NFF)]
    abuf = [K.sb("abuf%d" % i, [128, T + 2], F32) for i in range(2)]
    t_abuf = [K.trk("abuf") for _ in range(2)]
    cbuf = [K.sb("cbuf%d" % i, [128, T], F32) for i in range(2)]
    t_cbuf = [K.trk("cbuf") for _ in range(2)]
    sig = [K.sb("sig%d" % i, [128, T], F32) for i in range(2)]
    t_sig = [K.trk("sig") for _ in range(2)]
    WG = WStream(K, "wg", 16, 128, 3)
    WU = WStream(K, "wu", 16, 128, 3)
    WD = WStream(K, "wd", NFF, 128, 2)
    WP = WStream(K, "wp", 16, 256, 2)
    wpp_sb = K.sb("wpp", [128, 2, D], BF16)
    t_wpp = K.trk("wpp")
    for b in range(4):
        K.dma("sp", "const", wpp_sb[:, :, b * 512:(b + 1) * 512], scr["pp"][b], writes=[t_wpp])
    sq = hid
    t_sq = t_hid
    hpT = K.sb("hpT", [128, 16, 2], F32)
    hpu = K.sb("hpu", [128, 16, 2], BF16)
    hpsq = K.sb("hpsq", [128, 16, 2], BF16)
    hprs = K.sb("hprs", [128, 2], F32)
    t_hp = [K.trk("hp") for _ in range(16)]
    t_hpu = [K.trk("hpu") for _ in range(16)]
    t_hpsq = [K.trk("hpsq") for _ in range(16)]
    t_hprs = K.trk("hprs")
    if gath is None:
        for t_ in range(2):
            K.dma("sp", "const", hpT[:, :, t_], hprev[t_, :].rearrange("(c p) -> p c", p=128), writes=t_hp,
                  allow_slow_non_contiguous=True)
    else:
        hpa = K.sb("hpa", [128, 16, 8], F32)
        t_hpa = K.trk("hpa")
        hrows = K.sb("hrows", [8, D], F32)
        t_hrows = K.trk("hrows")
        K.dma("sp", "const", hrows[:, :], gath[:, :], reads=[t_gath], writes=[t_hrows])
        bkh, t_bkh = C.bank()
        for c_ in range(16):
            K.tr(bkh[:, c_ * 8:(c_ + 1) * 8], hrows[0:8, c_ * 128:(c_ + 1) * 128], C.ident[0:8, 0:8],
                 [t_hrows, C.t_ident], [t_bkh], inc=(c_ == 15))
        K.copy("act", hpa[:, :, :].rearrange("p c r -> p (c r)"), bkh[:, 0:128], [t_bkh], [t_hpa])
        sel = vec[:, 32 + 4 * NFF:32 + 4 * NFF + 4]
        for t_ in range(2):
            K.ts("dve", hpT[:, :, t_], hpa[:, :, t_], sel[:, 0:1], None, ALU.mult, None, [t_hpa, t_vec], t_hp)
            for r_ in range(1, 4):
                K.stt(hpT[:, :, t_], hpa[:, :, 2 * r_ + t_], sel[:, r_:r_ + 1], hpT[:, :, t_], ALU.mult, ALU.add,
                      [t_hpa, t_vec] + t_hp, t_hp)
    rmsnorm_fm(C, hpT, t_hp, 16, g_ffn, t_vec, hpu, t_hpu, hpsq, t_hpsq, hprs, t_hprs, D, W=2)

    for st in range(NST):
        tok = slice(st * T, (st + 1) * T)
        load_tokmajor_T(C, h_in[tok, :], D, hT, t_h, stage)
        load_tokmajor_T(C, p_in[tok, :], PLE, pT, t_p, pstage)
        rmsnorm_fm(C, hT, t_h, 16, g_ffn, t_vec, uT, t_u, sq, t_sq, rstd, t_rstd, D)
        for j in range(NFF):
            wg, t_wg = WG.load(scr["gate"], NOT, j)
            wu, t_wu = WU.load(scr["up"], NOT, j)
            if st == 0:
                bh, t_bh = C.bank()
                for kc in range(16):
                    K.mm(bh[:, 0:2], wg[:, kc, :], hpu[:, kc, :], kc == 0, kc == 15, [t_wg, t_hpu[kc]], [t_bh])
                K.copy("act", halo_sb[:, j, :], bh[:, 0:2], [t_bh], [t_halo[j]])
            bg, t_bg = C.bank()
            for kc in range(16):
                K.mm(bg[:, :], wg[:, kc, :], uT[:, kc, :], kc == 0, kc == 15, [t_wg, t_u[kc]], [t_bg])
            bu, t_bu = C.bank()
            for kc in range(16):
                K.mm(bu[:, :], wu[:, kc, :], uT[:, kc, :], kc == 0, kc == 15, [t_wu, t_u[kc]], [t_bu])
            ab, t_ab = abuf[j % 2], t_abuf[j % 2]
            cbf, t_cb = cbuf[j % 2], t_cbuf[j % 2]
            K.copy("pool", ab[:, 0:2], halo_sb[:, j, :], [t_halo[j]], [t_ab])
            K.copy("act", ab[:, 2:T + 2], bg[:, :], [t_bg], [t_ab])
            K.copy("pool", halo_sb[:, j, :], ab[:, T:T + 2], [t_ab], [t_halo[j]])
            K.ts("dve", cbf[:, :], ab[:, 2:T + 2], cw[2][:, j:j + 1], cb[:, j:j + 1], ALU.mult, ALU.add,
                 [t_ab, t_vec], [t_cb])
            K.stt(cbf[:, :], ab[:, 1:T + 1], cw[1][:, j:j + 1], cbf[:, :], ALU.mult, ALU.add,
                  [t_ab, t_vec, t_cb], [t_cb])
            K.stt(cbf[:, :], ab[:, 0:T], cw[0][:, j:j + 1], cbf[:, :], ALU.mult, ALU.add,
                  [t_ab, t_vec, t_cb], [t_cb])
            K.act(cbf[:, :], cbf[:, :], AF.Silu, [t_cb], [t_cb])
            K.tt("dve", hid[:, j, :], cbf[:, :], bu[:, :], ALU.mult, [t_cb, t_bu], [t_hid[j]])
        for n in range(16):
            wd, t_wd = WD.load(scr["down"], NOT, n)
            bd, t_bd = C.bank()
            for j in range(NFF):
                K.mm(bd[:, :], wd[:, j, :], hid[:, j, :], j == 0, j == NFF - 1, [t_wd, t_hid[j]], [t_bd])
            K.tt("dve", hT[:, n, :], hT[:, n, :], bd[:, :], ALU.add, [t_h[n], t_bd], [t_h[n]])
        rmsnorm_fm(C, hT, t_h, 16, g_ple, t_vec, uT, t_u, sq, t_sq, rstd, t_rstd, D)
        for n2 in range(8):
            wp, t_wp = WP.load(scr["pg"], NOT, n2)
            for nn in range(2):
                n = n2 * 2 + nn
                bg, t_bg = C.bank()
                for kc in range(16):
                    K.mm(bg[:, :], wp[:, kc, nn * 128:(nn + 1) * 128], uT[:, kc, :], kc == 0, kc == 15,
                         [t_wp, t_u[kc]], [t_bg])
                bp, t_bp = C.bank()
                for kc in range(2):
                    K.mm(bp[:, :], wpp_sb[:, kc, n * 128:(n + 1) * 128], pT[:, kc, :], kc == 0, kc == 1,
                         [t_wpp, t_p[kc]], [t_bp])
                sg, t_sg = sig[n % 2], t_sig[n % 2]
                K.act(sg[:, :], bg[:, :], AF.Sigmoid, [t_bg], [t_sg])
                K.tt("dve", sg[:, :], sg[:, :], bp[:, :], ALU.mult, [t_sg, t_bp], [t_sg])
                K.tt("pool", hT[:, n, :], hT[:, n, :], sg[:, :], ALU.add, [t_h[n], t_sg], [t_h[n]])
        store_tokmajor(C, hT, t_h, h_out[tok, :], stage)


def cast_ffn_weights(K, w_gate, w_up, w_down, w_pg, w_pp, pfx=""):
    with contextlib.ExitStack() as st2:
        K2 = K
        old = K.stack
        K.stack = st2
        WC = WCast(K, nslots=3)
        scr = {}
        scr["gate"] = WC.cast(w_gate, D, DFF, pfx + "s_gate", cw=128)
        scr["up"] = WC.cast(w_up, D, DFF, pfx + "s_up", cw=128)
        scr["down"] = WC.cast(w_down, DFF, D, pfx + "s_down", cw=128)
        scr["pg"] = WC.cast(w_pg, D, D, pfx + "s_pg", cw=256)
        scr["pp"] = WC.cast(w_pp, PLE, D, pfx + "s_pp", cw=512)
        K.barrier()
        K.stack = old
    return scr


def build_ffn_phase(nc):
    stack = contextlib.ExitStack()
    with stack:
        K = Kern(nc, stack)
        h_in = dram_in(nc, "h_in", [SEG, D])
        hprev = dram_in(nc, "hprev", [2, D])
        p_in = dram_in(nc, "p_in", [SEG, PLE])
        w_gate = dram_in(nc, "w_gate", [D, DFF])
        w_up = dram_in(nc, "w_up", [D, DFF])
        w_down = dram_in(nc, "w_down", [DFF, D])
        w_pg = dram_in(nc, "w_pg", [D, D])
        w_pp = dram_in(nc, "w_pp", [PLE, D])
        vecs = dram_in(nc, "vecs", [128, NVF])
        h_out = dram_out(nc, "h_out", [SEG, D])
        C = Ctx(K)
        scr = cast_ffn_weights(K, w_gate, w_up, w_down, w_pg, w_pp)
        ffn_phase_body(K, C, h_in, hprev, p_in, vecs, scr, h_out)
        K.finish()
        K.replay()
    return nc


NVF = 32 + NFF * 4 + 4


def ffn_vecs(norm_ffn, norm_ple, conv_w, conv_b, q=0):
    v = np.zeros((128, NVF), np.float32)
    if q > 0:
        v[:, 32 + 4 * NFF + q - 1] = 1.0
    v[:, 0:16] = norm_ffn.reshape(16, 128).T
    v[:, 16:32] = norm_ple.reshape(16, 128).T
    for i in range(3):
        v[:, 32 + i * NFF:32 + (i + 1) * NFF] = conv_w[i].reshape(NFF, 128).T
    v[:, 32 + 3 * NFF:32 + 4 * NFF] = conv_b.reshape(NFF, 128).T
    return v


NSLOT = 3
SLOT_ST = 8
O_ST = 8
TWO_PI = 2.0 * np.pi
MAGIC = 12582912.0
C1 = 6.28125
C2 = float(TWO_PI - 6.28125)
PI_LO = 3.1415925
NVA = 80


def vecsA_host(inp, q):
    v = np.zeros((128, NVA), np.float32)
    v[:, 0:16] = inp["norm_mix"][0].reshape(16, 128).T
    v[:, 16:24] = inp["e_lb_logits"][0].reshape(8, 128).T
    v[:, 24:32] = inp["e_lb_logits"][1].reshape(8, 128).T
    v[:, 32:36] = inp["e_q_a_norm"][0].reshape(4, 128).T
    v[:, 36:40] = inp["e_kv_a_norm"][0].reshape(4, 128).T
    qn, kn = inp["e_q_norm"][0], inp["e_k_norm"][0]
    for base, g in ((40, qn), (43, kn)):
        v[:, base] = g[0:128]
        v[0:64, base + 1] = g[128:192]
        v[0:32, base + 2] = g[160:192]
        v[32:64, base + 2] = g[128:160]
    v[:, 46] = inp["e_hg_onorm"][0]
    invf = (10000.0 ** (-np.arange(32, dtype=np.float32) / np.float32(32))).astype(np.float32)
    v[0:64, 47] = np.concatenate([invf, invf])
    v[0:32, 48] = -1.0
    v[32:64, 48] = 1.0
    for i in range(NSLOT):
        valid = (q - NSLOT + i) >= 0
        v[:, 65 + i] = 0.0 if valid else -30000.0
        v[:, 69 + i] = 1.0 if valid else 0.0
    v[:, 68] = 0.0
    return v


class PhaseA:
    def __init__(self, K, C, nc):
        self.K, self.C, self.nc = K, C, nc

    def prologue(self, w_in, w_uq, w_ukv, w_out):
        K = self.K
        with contextlib.ExitStack() as st2:
            old = K.stack
            K.stack = st2
            WC = WCast(K, nslots=3)
            scr = {}
            scr["win"] = WC.cast(w_in[:, 0:5120], D, 5120, "sA_win", cw=128)
            scr["hi"] = WC.cast(w_in[:, 2048:3072], D, 1024, "sA_hi", cw=256)
            scr["wout"] = WC.cast(w_out, D, D, "sA_wout", cw=128)
            scr["small"] = K.nc.dram_tensor("sA_small", [4, 128, 4, 1024], BF16, kind="Internal").ap()
            scr["kpe"] = K.nc.dram_tensor("sA_kpe", [1, 128, 16, 128], BF16, kind="Internal").ap()

            def piece(dst, src, n, nkc):
                srcv = src.rearrange("(kc p) n -> p kc n", p=128)
                for k0 in range(0, nkc, 8):
                    kn = min(8, nkc - k0)
                    s = WC.i % WC.n
                    WC.i += 1
                    fv = WC.f[s][:].rearrange("p a b -> p (a b)")[:, 0:kn * n].rearrange("p (a b) -> p a b", a=kn)
                    bv = WC.b[s][:].rearrange("p a b -> p (a b)")[:, 0:kn * n].rearrange("p (a b) -> p a b", a=kn)
                    K.dma("sp", "wcl%d" % s, fv, srcv[:, k0:k0 + kn, :], writes=[WC.tf[s]])
                    K.copy(WC.engs[WC.i % 3], bv, fv, [WC.tf[s]], [WC.tb[s]])
                    K.dma("pool", "wcs%d" % s, dst[:, k0:k0 + kn, :], bv, reads=[WC.tb[s]], writes=[])

            sm = scr["small"]
            for h in range(8):
                piece(sm[0, :, :, h * 128:(h + 1) * 128], w_ukv[:, h * 256:h * 256 + 128], 128, 4)
                piece(sm[1, :, :, h * 128:(h + 1) * 128], w_ukv[:, h * 256 + 128:h * 256 + 256], 128, 4)
                piece(sm[2, :, :, h * 128:(h + 1) * 128], w_uq[:, h * 192:h * 192 + 128], 128, 4)
                piece(sm[3, :, :, h * 64:(h + 1) * 64], w_uq[:, h * 192 + 128:h * 192 + 192], 64, 4)
                piece(sm[3, :, :, 512 + h * 64:512 + h * 64 + 32], w_uq[:, h * 192 + 160:h * 192 + 192], 32, 4)
                piece(sm[3, :, :, 512 + h * 64 + 32:512 + (h + 1) * 64], w_uq[:, h * 192 + 128:h * 192 + 160], 32, 4)
            piece(scr["kpe"][0, :, :, 0:64], w_in[:, 5120:5184], 64, 16)
            piece(scr["kpe"][0, :, :, 64:96], w_in[:, 5152:5184], 32, 16)
            piece(scr["kpe"][0, :, :, 96:128], w_in[:, 5120:5152], 32, 16)
            K.barrier()
            K.stack = old
        self.scr = scr

    def alloc(self, vecs, pos_all, ntok_all):
        K, C, nc = self.K, self.C, self.nc
        self.vec = K.sb("vecA", [128, NVA], F32)
        self.t_vec = K.trk("vecA")
        K.dma("sp", "const", self.vec[:], vecs[:, :], writes=[self.t_vec])
        v = self.vec
        self.lb = K.sb("lb", [128, 8], F32)
        self.oml = K.sb("oml", [128, 8], F32)
        K.tt("dve", self.lb[:], v[:, 16:24], v[:, 24:32], ALU.subtract, [self.t_vec], [self.t_vec])
        K.act(self.lb[:], self.lb[:], AF.Sigmoid, [self.t_vec], [self.t_vec])
        K.ts("dve", self.oml[:], self.lb[:], -1.0, 1.0, ALU.mult, ALU.add, [self.t_vec], [self.t_vec])
        tri = np.triu(np.ones((128, 128), np.float32))
        self.tri = K.sb("tri", [128, 128], F32)
        rm = np.ones((128, T), np.float32)
        rm[:, 0::128] = 0.0
        self.rmask = K.sb("rmask", [128, T], F32)
        self.t_cst = K.trk("cstA")
        K.dma("sp", "const", self.tri[:], nc.inline_tensor(tri, name="tri_d").ap()[:, :], writes=[self.t_cst])
        K.dma("sp", "const", self.rmask[:], nc.inline_tensor(rm, name="rm_d").ap()[:, :], writes=[self.t_cst])
        self.trib = K.sb("trib", [128, 128], BF16)
        K.copy("dve", self.trib[:], self.tri[:], [self.t_cst], [self.t_cst])
        self.pos_all = pos_all
        self.ntok = ntok_all
        self.Kn = nc.dram_tensor("Kn_scr", [8, 128, ntok_all], BF16, kind="Internal").ap()
        self.Kr = nc.dram_tensor("Kr_scr", [8, 64, ntok_all], BF16, kind="Internal").ap()
        self.Vs = nc.dram_tensor("V_scr", [ntok_all, 1024], BF16, kind="Internal").ap()
        ng = ntok_all // T
        self.t_kn = [[K.trk("kn") for _ in range(8)] for _ in range(ng)]
        self.t_kr = [[K.trk("kr") for _ in range(8)] for _ in range(ng)]
        self.t_vs = [[K.trk("vs") for _ in range(4)] for _ in range(ng)]
        self.xT = K.sb("xT", [128, 16, T], F32)
        self.t_x = [K.trk("x") for _ in range(16)]
        self.wk = [self.xT[:, i, :] for i in range(16)]
        self.t_wk = self.t_x
        self.wk_i = 0
        self.uT = K.sb("uTa", [128, 16, T], BF16)
        self.t_u = [K.trk("u") for _ in range(16)]
        self.mixT = K.sb("mixT", [128, 16, T], BF16)
        self.t_mix = [K.trk("mix") for _ in range(16)]
        self.sq = self.mixT
        self.t_sq = self.t_mix
        self.sq4 = K.sb("sq4", [128, 4, T], BF16)
        self.t_sq4 = [K.trk("sq4") for _ in range(4)]
        self.rstd = K.sb("rstdA", [128, T], F32)
        self.t_rstd = K.trk("rstd")
        self.stage = Stage(K, "stgA", D, 2)
        self.W = WStream(K, "wA", 16, 128, 3)
        self.WH = WStream(K, "wAh", 16, 256, 2)
        self.WS = WStream(K, "wAs", 4, 1024, 2)
        self.posi = K.sb("posi", [64, T], I32)
        self.cos = K.sb("cosT", [64, T], F32)
        self.sin = K.sb("sinT", [64, T], F32)
        self.t_rope = K.trk("rope")
        self.t_cs = K.trk("cossin")
        self.wb = [K.sb("wbA%d" % i, [128, T], BF16) for i in range(6)]
        self.t_wb = [K.trk("wb") for _ in range(6)]
        self.wb_i = 0
        self.kvn = K.sb("kvn", [128, 4, T], BF16)
        self.t_kvn = [K.trk("kvn") for _ in range(4)]
        self.kpe = K.sb("kpe", [64, 2, T], F32)
        self.t_kpe = K.trk("kpe")
        self.KR = K.sb("KR", [64, T], F32)
        self.t_KR = K.trk("KR")
        self.sqpe = K.sb("sqpe", [64, T], BF16)
        self.t_sqpe = K.trk("sqpe")
        self.vtm = K.sb("vtm", [128, 4, 1024], BF16)
        self.t_vtm = [K.trk("vtm") for _ in range(4)]
        self.hv = self.vtm
        self.t_hv = self.t_vtm
        self.Qn = K.sb("Qn", [128, 8, T], BF16)
        self.Qr = K.sb("Qr", [64, 8, T], BF16)
        self.t_Q = [K.trk("Q") for _ in range(8)]
        self.S = K.sb("S", [128, 8, 128], F32)
        self.Sb = K.sb("Sb", [128, 8, 128], BF16)
        self.t_S = [K.trk("S") for _ in range(8)]
        self.t_Sb = [K.trk("Sb") for _ in range(8)]
        for h in range(8):
            K.op("pool", lambda e, h=h: e.memset(self.S[:, h, :], 0.0), writes=[self.t_S[h]])
            K.op("pool", lambda e, h=h: e.memset(self.Sb[:, h, :], 0.0), writes=[self.t_Sb[h]])
        self.fbuf = [K.sb("fbuf%d" % i, [128, T], F32) for i in range(2)]
        self.t_fbuf = [K.trk("fbuf") for _ in range(2)]
        self.kktm = K.sb("kktm", [128, 2, 4, 128], BF16)
        self.t_kktm = [K.trk("kktm") for _ in range(2)]
        self.ebl = K.sb("ebl", [128, 2, 4], F32)
        self.aK = [K.sb("aK%d" % i, [128, T], BF16) for i in range(3)]
        self.aKr = [K.sb("aKr%d" % i, [64, T], BF16) for i in range(3)]
        self.aV = [K.sb("aV%d" % i, [128, 4, 128], BF16) for i in range(3)]
        self.t_aK = [K.trk("aK") for _ in range(3)]
        self.t_aKr = [K.trk("aKr") for _ in range(3)]
        self.t_aV = [K.trk("aV") for _ in range(3)]
        self.aP = [K.sb("aP%d" % i, [128, T], BF16) for i in range(4)]
        self.t_aP = [K.trk("aP") for _ in range(4)]
        self.a_i = 0
        self.p_i = 0

    def work(self):
        i = self.wk_i % 16
        self.wk_i += 1
        return self.wk[i], self.t_wk[i]

    def workb(self):
        i = self.wb_i % 6
        self.wb_i += 1
        return self.wb[i], self.t_wb[i]

    def rope_tables(self, g):
        K = self.K
        v = self.vec
        K.dma("sp", "posi", self.posi[:, :], self.pos_all[0:1, g * T:(g + 1) * T].broadcast_to([64, T]),
              writes=[self.t_rope])
        pf, t_pf = self.work()
        ra, t_ra = self.work()
        rn, t_rn = self.work()
        posf, ra, rn = pf[0:64, :], ra[0:64, :], rn[0:64, :]
        K.copy("dve", posf, self.posi[:, :], [self.t_rope], [t_pf])
        for off, out in ((0.0, self.sin), (float(np.pi / 2), self.cos)):
            K.ts("dve", ra, posf, v[0:64, 47:48], off, ALU.mult, ALU.add, [t_pf, self.t_vec], [t_ra])
            K.ts("dve", rn, ra, float(1.0 / TWO_PI), MAGIC, ALU.mult, ALU.add, [t_ra], [t_rn])
            K.ts("dve", rn, rn, -MAGIC, None, ALU.add, None, [t_rn], [t_rn])
            K.stt(ra, rn, -C1, ra, ALU.mult, ALU.add, [t_rn, t_ra], [t_ra])
            K.stt(ra, rn, -C2, ra, ALU.mult, ALU.add, [t_rn, t_ra], [t_ra])
            K.ts("dve", ra, ra, -PI_LO, PI_LO, ALU.max, ALU.min, [t_ra], [t_ra])
            K.act(out[:, :], ra, AF.Sin, [t_ra], [self.t_cs])
        K.ts("dve", self.sin[:, :], self.sin[:, :], v[0:64, 48:49], None, ALU.mult, None,
             [self.t_cs, self.t_vec], [self.t_cs])

    def proj_fm(self, lhs_fn, lhs_trk, nkc, rhsT, t_rhs, np_out=128):
        K, C = self.K, self.C
        bk, t_bk = C.bank()
        for kc in range(nkc):
            K.mm(bk[0:np_out, :], lhs_fn(kc), rhsT[:, kc, :], kc == 0, kc == nkc - 1, [lhs_trk, t_rhs[kc]], [t_bk])
        return bk, t_bk

    def rstd_from(self, parts, dim, out, t_out, np_out=128):
        K, C = self.K, self.C
        bk, t_bk = C.bank()
        n = len(parts)
        for i, (ap, t, kp) in enumerate(parts):
            K.mm(bk[:, :], C.ones[0:kp, :], ap, i == 0, i == n - 1, [C.t_ones, t], [t_bk])
        K.act(out, bk[0:np_out, :], AF.Ln, [t_bk], [t_out], bias=EPS, scale=1.0 / dim)
        K.act(out, out, AF.Exp, [t_out], [t_out], scale=-0.5)

    def front(self, x_all, g):
        C = self.C
        load_tokmajor_T(C, x_all[g * T:(g + 1) * T, :], D, self.xT, self.t_x, self.stage)
        rmsnorm_fm(C, self.xT, self.t_x, 16, self.vec[:, 0:16], self.t_vec, self.uT, self.t_u,
                   self.sq, self.t_sq, self.rstd, self.t_rstd, D)

    def mla_kv(self, g):
        K, C, v = self.K, self.C, self.vec
        NOT = K.trk("none")
        self.rope_tables(g)
        sqs = []
        ckv = []
        for c in range(4):
            w, t_w = self.W.load(self.scr["win"], NOT, 36 + c)
            bk, t_bk = self.proj_fm(lambda kc, w=w: w[:, kc, :], t_w, 16, self.uT, self.t_u)
            cf, t_cf = self.work()
            ckv.append((cf, t_cf))
            K.copy("act", cf[:, :], bk[:, :], [t_bk], [t_cf])
            K.act(self.sq4[:, c, :], bk[:, :], AF.Square, [t_bk], [self.t_sq4[c]])
            sqs.append((self.sq4[:, c, :], self.t_sq4[c], 128))
        self.rstd_from(sqs, 512, self.rstd[:, :], self.t_rstd)
        for c in range(4):
            K.stt(self.kvn[:, c, :], ckv[c][0][:, :], v[:, 36 + c:37 + c], self.rstd[:, :], ALU.mult, ALU.mult,
                  [ckv[c][1], self.t_vec, self.t_rstd], [self.t_kvn[c]])
        wk_, t_wk_ = self.W.load(self.scr["kpe"], NOT, 0)
        for i in range(2):
            bk, t_bk = self.proj_fm(lambda kc, i=i: wk_[:, kc, i * 64:(i + 1) * 64], t_wk_, 16,
                                    self.uT, self.t_u, np_out=64)
            K.copy("act", self.kpe[:, i, :], bk[0:64, :], [t_bk], [self.t_kpe])
        K.act(self.sqpe[:, :], self.kpe[:, 0, :], AF.Square, [self.t_kpe], [self.t_sqpe])
        tmp, t_tmp = self.work()
        K.stt(self.KR[:, :], self.kpe[:, 0, :], v[0:64, 44:45], self.cos[:, :], ALU.mult, ALU.mult,
              [self.t_kpe, self.t_vec, self.t_cs], [self.t_KR])
        K.stt(tmp[0:64, :], self.kpe[:, 1, :], v[0:64, 45:46], self.sin[:, :], ALU.mult, ALU.mult,
              [self.t_kpe, self.t_vec, self.t_cs], [t_tmp])
        K.tt("pool", self.KR[:, :], self.KR[:, :], tmp[0:64, :], ALU.add, [self.t_KR, t_tmp], [self.t_KR])
        ukn, t_ukn = self.WS.load(self.scr["small"], NOT, 0)
        for h in range(8):
            bk, t_bk = self.proj_fm(lambda kc, h=h: ukn[:, kc, h * 128:(h + 1) * 128], t_ukn, 4,
                                    self.kvn, self.t_kvn)
            kf, t_kf = self.work()
            K.copy("act", kf[:, :], bk[:, :], [t_bk], [t_kf])
            sqn, t_sqn = self.workb()
            K.act(sqn[:, :], bk[:, :], AF.Square, [t_bk], [t_sqn])
            rk, t_rk = self.work()
            self.rstd_from([(sqn[:, :], t_sqn, 128), (self.sqpe[:, :], self.t_sqpe, 64)], 192, rk[:, :], t_rk)
            kb, t_kb = self.workb()
            K.stt(kb[:, :], kf[:, :], v[:, 43:44], rk[:, :], ALU.mult, ALU.mult, [t_kf, self.t_vec, t_rk], [t_kb])
            K.dma("pool", "kns%d" % (h % 4), self.Kn[h, :, g * T:(g + 1) * T], kb[:, :], reads=[t_kb],
                  writes=[self.t_kn[g][h]])
            krb, t_krb = self.workb()
            K.tt("pool", krb[0:64, :], self.KR[:, :], rk[0:64, :], ALU.mult, [self.t_KR, t_rk], [t_krb])
            K.dma("pool", "krs%d" % (h % 4), self.Kr[h, :, g * T:(g + 1) * T], krb[0:64, :], reads=[t_krb],
                  writes=[self.t_kr[g][h]])
        uv, t_uv = self.WS.load(self.scr["small"], NOT, 1)
        for tb in range(4):
            for half in range(2):
                bk, t_bk = C.bank()
                for kc in range(4):
                    K.mm(bk[:, :], self.kvn[:, kc, tb * 128:(tb + 1) * 128], uv[:, kc, half * 512:(half + 1) * 512],
                         kc == 0, kc == 3, [self.t_kvn[kc], t_uv], [t_bk])
                K.copy(("act", "dve")[half], self.vtm[:, tb, half * 512:(half + 1) * 512], bk[:, :], [t_bk],
                       [self.t_vtm[tb]])
            K.dma("pool", "vs%d" % tb, self.Vs[g * T + tb * 128:g * T + (tb + 1) * 128, :], self.vtm[:, tb, :],
                  reads=[self.t_vtm[tb]], writes=[self.t_vs[g][tb]])

    def hgrn(self, own, vflag_col):
        K, C, v = self.K, self.C, self.vec
        NOT = K.trk("none")
        for qd_ in range(4):
            w, t_w = self.WH.load(self.scr["hi"], NOT, qd_)
            for tb in range(4):
                bk, t_bk = C.bank()
                for kc in range(16):
                    K.mm(bk[:, 0:256], self.uT[:, kc, tb * 128:(tb + 1) * 128], w[:, kc, :], kc == 0, kc == 15,
                         [self.t_u[kc], t_w], [t_bk])
                if vflag_col is None:
                    K.copy("act", self.hv[:, tb, qd_ * 256:(qd_ + 1) * 256], bk[:, 0:256], [t_bk], [self.t_hv[tb]])
                else:
                    K.ts("dve", self.hv[:, tb, qd_ * 256:(qd_ + 1) * 256], bk[:, 0:256], vflag_col, None, ALU.mult, None,
                         [t_bk, self.t_vec], [self.t_hv[tb]])
        def hg_proj(h):
            w, t_w = self.W.load(self.scr["win"], NOT, 8 + h)
            bk, t_bk = self.proj_fm(lambda kc, w=w: w[:, kc, :], t_w, 16, self.uT, self.t_u)
            f, t_f = self.fbuf[h % 2], self.t_fbuf[h % 2]
            K.act(f[:, :], bk[:, :], AF.Sigmoid, [t_bk], [t_f])
            return f, t_f

        nxt = hg_proj(0)
        for h in range(8):
            par = h % 2
            f, t_f = nxt
            if h < 7:
                nxt = hg_proj(h + 1)
            K.ts("dve", f[:, :], f[:, :], self.oml[:, h:h + 1], self.lb[:, h:h + 1], ALU.mult, ALU.add,
                 [t_f, self.t_vec], [t_f])
            b, t_b = self.work()
            K.act(b[:, :], f[:, :], AF.Ln, [t_f], [t_b])
            K.op("dve", lambda e, b=b: e.tensor_tensor_scan(b[:, :], self.rmask[:, :], b[:, :], 0.0, ALU.mult, ALU.add),
                 [t_b, self.t_cst], [t_b])
            K.ts("pool", f[:, :], f[:, :], -1.0, 1.0, ALU.mult, ALU.add, [t_f], [t_f])
            b3 = b[:, :].rearrange("p (c t) -> p c t", c=4)
            e4, t_e4 = self.work()
            for c in range(4):
                K.act(e4[:, c * 128:(c + 1) * 128], b[:, c * 128:(c + 1) * 128], AF.Exp, [t_b], [t_e4],
                      bias=b[:, c * 128 + 127:c * 128 + 128], scale=-1.0)
            K.tt("pool", e4[:, :], e4[:, :], f[:, :], ALU.mult, [t_e4, t_f], [t_e4])
            bkt, t_bkt = C.bank()
            for c in range(4):
                K.tr(bkt[:, c * 128:(c + 1) * 128], e4[:, c * 128:(c + 1) * 128], C.ident[:], [t_e4, C.t_ident],
                     [t_bkt], inc=(c == 3))
            K.copy("act", self.kktm[:, par, :, :].rearrange("p c k -> p (c k)"), bkt[:, :], [t_bkt], [self.t_kktm[par]])
            K.act(self.ebl[:, par, :], b3[:, :, 127], AF.Exp, [t_b], [self.t_kktm[par]])
            if own:
                wq, t_wq = self.W.load(self.scr["win"], NOT, h)
                bq, t_bq = self.proj_fm(lambda kc, wq=wq: wq[:, kc, :], t_wq, 16, self.uT, self.t_u)
                qf, t_qf = self.work()
                K.copy("act", qf[:, :], bq[:, :], [t_bq], [t_qf])
                e3, t_e3 = self.work()
                K.act(e3[:, :], b[:, :], AF.Exp, [t_b], [t_e3])
                qd, t_qd = self.workb()
                K.tt("dve", qd[:, :], qf[:, :], e3[:, :], ALU.mult, [t_qf, t_e3], [t_qd])
                e1, t_e1 = self.work()
                e2, t_e2 = self.work()
                nref, t_nref = self.work()
                K.ts("dve", nref[:, 0:4], b3[:, :, 63], -1.0, None, ALU.mult, None, [t_b], [t_nref])
                for c in range(4):
                    K.act(e1[:, c * 128:(c + 1) * 128], b[:, c * 128:(c + 1) * 128], AF.Exp, [t_b, t_nref], [t_e1],
                          bias=nref[:, c:c + 1], scale=1.0)
                    K.act(e2[:, c * 128:(c + 1) * 128], b[:, c * 128:(c + 1) * 128], AF.Exp, [t_b], [t_e2],
                          bias=b[:, c * 128 + 63:c * 128 + 64], scale=-1.0)
                qr, t_qr = self.workb()
                kr, t_kr = self.workb()
                K.tt("dve", qr[:, :], qf[:, :], e1[:, :], ALU.mult, [t_qf, t_e1], [t_qr])
                K.tt("pool", kr[:, :], f[:, :], e2[:, :], ALU.mult, [t_f, t_e2], [t_kr])
                bo, t_bo = C.bank(hold=True)
            for c in range(4):
                cs = slice(c * 128, (c + 1) * 128)
                if own:
                    ba, t_ba = C.bank()
                    K.mm(ba[:, 0:128], kr[:, cs], qr[:, cs], True, True, [t_kr, t_qr], [t_ba])
                    am, t_am = self.workb()
                    K.tt("dve", am[:, 0:128], ba[:, 0:128], self.tri[:, :], ALU.mult, [t_ba, self.t_cst], [t_am])
                    K.mm(bo[:, cs], self.hv[:, c, h * 128:(h + 1) * 128], am[:, 0:128], True, False,
                         [self.t_hv[c], t_am], [t_bo], inc=False)
                    K.mm(bo[:, cs], self.Sb[:, h, :], qd[:, cs], False, True, [self.t_Sb[h], t_qd], [t_bo], inc=True)
                bs, t_bs = C.bank()
                K.mm(bs[:, 0:128], self.kktm[:, par, c, :], self.hv[:, c, h * 128:(h + 1) * 128], True, True,
                     [self.t_kktm[par], self.t_hv[c]], [t_bs])
                K.stt(self.S[:, h, :], self.S[:, h, :], self.ebl[:, par, c:c + 1], bs[:, 0:128], ALU.mult, ALU.add,
                      [self.t_S[h], self.t_kktm[par], t_bs], [self.t_S[h]])
                K.copy("pool", self.Sb[:, h, :], self.S[:, h, :], [self.t_S[h]], [self.t_Sb[h]])
            if own:
                of, t_of = self.work()
                K.copy("act", of[:, :], bo[:, :], [t_bo], [t_of])
                sqo, t_sqo = self.workb()
                K.act(sqo[:, :], bo[:, :], AF.Square, [t_bo], [t_sqo])
                C.release(t_bo)
                ro, t_ro = self.work()
                self.rstd_from([(sqo[:, :], t_sqo, 128)], 128, ro[:, :], t_ro)
                wg_, t_wg = self.W.load(self.scr["win"], NOT, 24 + h)
                bg, t_bg = self.proj_fm(lambda kc, wg_=wg_: wg_[:, kc, :], t_wg, 16, self.uT, self.t_u)
                sg, t_sg = self.work()
                K.act(sg[:, :], bg[:, :], AF.Silu, [t_bg], [t_sg])
                K.stt(of[:, :], of[:, :], v[:, 46:47], ro[:, :], ALU.mult, ALU.mult, [t_of, self.t_vec, t_ro], [t_of])
                K.tt("pool", self.mixT[:, h, :], of[:, :], sg[:, :], ALU.mult, [t_of, t_sg], [self.t_mix[h]])

    def mla_q(self):
        K, C, v = self.K, self.C, self.vec
        NOT = K.trk("none")
        sqs = []
        cq = []
        for c in range(4):
            w, t_w = self.W.load(self.scr["win"], NOT, 32 + c)
            bk, t_bk = self.proj_fm(lambda kc, w=w: w[:, kc, :], t_w, 16, self.uT, self.t_u)
            cf, t_cf = self.work()
            cq.append((cf, t_cf))
            K.copy("act", cf[:, :], bk[:, :], [t_bk], [t_cf])
            K.act(self.sq4[:, c, :], bk[:, :], AF.Square, [t_bk], [self.t_sq4[c]])
            sqs.append((self.sq4[:, c, :], self.t_sq4[c], 128))
        self.rstd_from(sqs, 512, self.rstd[:, :], self.t_rstd)
        qn = self.kvn
        t_qn = self.t_kvn
        for c in range(4):
            K.stt(qn[:, c, :], cq[c][0][:, :], v[:, 32 + c:33 + c], self.rstd[:, :], ALU.mult, ALU.mult,
                  [cq[c][1], self.t_vec, self.t_rstd], [t_qn[c]])
        sc = float(192 ** -0.5)
        uqn, t_uqn = self.WS.load(self.scr["small"], NOT, 2)
        uqr, t_uqr = self.WS.load(self.scr["small"], NOT, 3)
        for h in range(8):
            bk, t_bk = self.proj_fm(lambda kc, h=h: uqn[:, kc, h * 128:(h + 1) * 128], t_uqn, 4, qn, t_qn)
            qf, t_qf = self.work()
            K.copy("act", qf[:, :], bk[:, :], [t_bk], [t_qf])
            sqn, t_sqn = self.workb()
            K.act(sqn[:, :], bk[:, :], AF.Square, [t_bk], [t_sqn])
            b1, t_b1 = self.proj_fm(lambda kc, h=h: uqr[:, kc, h * 64:(h + 1) * 64], t_uqr, 4, qn, t_qn, 64)
            b2, t_b2 = self.proj_fm(lambda kc, h=h: uqr[:, kc, 512 + h * 64:512 + (h + 1) * 64], t_uqr, 4, qn, t_qn, 64)
            sqr, t_sqr = self.workb()
            K.act(sqr[0:64, :], b1[0:64, :], AF.Square, [t_b1], [t_sqr])
            rq, t_rq = self.work()
            self.rstd_from([(sqn[:, :], t_sqn, 128), (sqr[0:64, :], t_sqr, 64)], 192, rq[:, :], t_rq)
            K.ts("pool", rq[:, :], rq[:, :], sc, None, ALU.mult, None, [t_rq], [t_rq])
            K.stt(self.Qn[:, h, :], qf[:, :], v[:, 40:41], rq[:, :], ALU.mult, ALU.mult, [t_qf, self.t_vec, t_rq],
                  [self.t_Q[h]])
            r1, t_r1 = self.work()
            r2, t_r2 = self.work()
            K.stt(r1[0:64, :], b1[0:64, :], v[0:64, 41:42], self.cos[:, :], ALU.mult, ALU.mult,
                  [t_b1, self.t_vec, self.t_cs], [t_r1])
            K.stt(r2[0:64, :], b2[0:64, :], v[0:64, 42:43], self.sin[:, :], ALU.mult, ALU.mult,
                  [t_b2, self.t_vec, self.t_cs], [t_r2])
            K.tt("pool", r1[0:64, :], r1[0:64, :], r2[0:64, :], ALU.add, [t_r1, t_r2], [t_r1])
            K.tt("pool", self.Qr[:, h, :], r1[0:64, :], rq[0:64, :], ALU.mult, [t_r1, t_rq], [self.t_Q[h]])

    def attention(self, g, n_prior_groups):
        K, C, v = self.K, self.C, self.vec
        for h in range(8):
            bo, t_bo = C.bank(hold=True)
            bd, t_bd = C.bank(hold=True)
            groups = list(range(0, g + 1))
            pend = None
            first = True
            for gi in groups:
                diag = (gi == g)
                slot = gi // SLOT_ST if gi < n_prior_groups else NSLOT
                bias = v[:, 65 + slot:66 + slot] if gi < n_prior_groups else v[:, 68:69]
                s = self.a_i % 3
                self.a_i += 1
                K.dma("sp", "aK%d" % s, self.aK[s][:, :], self.Kn[h, :, gi * T:(gi + 1) * T],
                      reads=[self.t_kn[gi][h]], writes=[self.t_aK[s]])
                K.dma("sp", "aKr%d" % s, self.aKr[s][:, :], self.Kr[h, :, gi * T:(gi + 1) * T],
                      reads=[self.t_kr[gi][h]], writes=[self.t_aKr[s]])
                K.dma("sp", "aV%d" % s, self.aV[s][:, :, :],
                      self.Vs[gi * T:(gi + 1) * T, h * 128:(h + 1) * 128].rearrange("(tb p) v -> p tb v", p=128),
                      reads=self.t_vs[gi], writes=[self.t_aV[s]])
                for kb in range(4):
                    q0 = kb * 128 if diag else 0
                    bs, t_bs = C.bank()
                    K.mm(bs[:, q0:T], self.aK[s][:, kb * 128:(kb + 1) * 128], self.Qn[:, h, q0:T], True, False,
                         [self.t_aK[s], self.t_Q[h]], [t_bs], inc=False)
                    K.mm(bs[:, q0:T], self.aKr[s][:, kb * 128:(kb + 1) * 128], self.Qr[:, h, q0:T], False, True,
                         [self.t_aKr[s], self.t_Q[h]], [t_bs], inc=True)
                    pi = self.p_i % 4
                    self.p_i += 1
                    P, t_P = self.aP[pi], self.t_aP[pi]
                    K.act(P[:, q0:T], bs[:, q0:T], AF.Exp, [t_bs, self.t_vec], [t_P], bias=bias, scale=1.0)
                    if diag:
                        K.tt("dve", P[:, q0:q0 + 128], P[:, q0:q0 + 128], self.trib[:, :], ALU.mult,
                             [t_P, self.t_cst], [t_P])
                    if pend is not None:
                        self._pv(*pend)
                    pend = (bo, t_bo, bd, t_bd, P, t_P, s, kb, q0, first, False)
                    first = False
            lst = list(pend)
            lst[-1] = True
            self._pv(*lst)
            rd, t_rd = self.work()
            K.act(rd[:, :], bd[:, :], AF.Ln, [t_bd], [t_rd])
            K.act(rd[:, :], rd[:, :], AF.Exp, [t_rd], [t_rd], scale=-1.0)
            K.tt("dve", self.mixT[:, 8 + h, :], bo[:, :], rd[:, :], ALU.mult, [t_bo, t_rd], [self.t_mix[8 + h]])
            C.release(t_bo)
            C.release(t_bd)

    def _pv(self, bo, t_bo, bd, t_bd, P, t_P, s, kb, q0, first, last):
        K, C = self.K, self.C
        K.mm(bo[:, q0:T], self.aV[s][:, kb, :], P[:, q0:T], first, last, [self.t_aV[s], t_P], [t_bo], inc=last)
        K.mm(bd[:, q0:T], C.ones[:, :], P[:, q0:T], first, last, [C.t_ones, t_P], [t_bd], inc=True)

    def out_proj(self, x_all, g, h_mid, o):
        K, C = self.K, self.C
        NOT = K.trk("none")
        for n in range(16):
            w, t_w = self.W.load(self.scr["wout"], NOT, n)
            bk, t_bk = self.proj_fm(lambda kc, w=w: w[:, kc, :], t_w, 16, self.mixT, self.t_mix)
            K.copy("act", self.xT[:, n, :], bk[:, :], [t_bk], [self.t_x[n]])
        for tb in range(4):
            sb_, t_sb, key = self.stage.next()
            K.dma("sp", key, sb_[:, :], x_all[g * T + tb * 128:g * T + (tb + 1) * 128, :], writes=[t_sb])
            for c4 in range(4):
                bk, t_bk = C.bank()
                for cc in range(4):
                    kc = c4 * 4 + cc
                    K.tr(bk[:, cc * 128:(cc + 1) * 128], self.xT[:, kc, tb * 128:(tb + 1) * 128], C.ident[:],
                         [self.t_x[kc], C.t_ident], [t_bk], inc=(cc == 3))
                K.tt("dve", sb_[:, c4 * 512:(c4 + 1) * 512], bk[:, :], sb_[:, c4 * 512:(c4 + 1) * 512], ALU.add,
                     [t_bk, t_sb], [t_sb])
            K.dma("pool", key + "s", h_mid[o * T + tb * 128:o * T + (tb + 1) * 128, :], sb_[:, :], reads=[t_sb], writes=[])


def build_phaseA(nc):
    stack = contextlib.ExitStack()
    with stack:
        K = Kern(nc, stack)
        NP = NSLOT * SLOT_ST
        ntok = (NP + O_ST) * T
        x_all = dram_in(nc, "x_all", [ntok, D])
        pos_all = dram_in(nc, "pos_all", [1, ntok], I32)
        w_in = dram_in(nc, "w_in", [D, 5184])
        w_uq = dram_in(nc, "w_uq", [512, 1536])
        w_ukv = dram_in(nc, "w_ukv", [512, 2048])
        w_out = dram_in(nc, "w_out", [D, D])
        vecs = dram_in(nc, "vecsA", [128, NVA])
        h_mid = dram_out(nc, "h_mid", [O_ST * T, D])
        C = Ctx(K)
        A = PhaseA(K, C, nc)
        A.prologue(w_in, w_uq, w_ukv, w_out)
        A.alloc(vecs, pos_all, ntok)
        for g in range(NP):
            A.front(x_all, g)
            A.mla_kv(g)
            A.hgrn(False, A.vec[:, 69 + g // SLOT_ST:70 + g // SLOT_ST])
        for o in range(O_ST):
            g = NP + o
            A.front(x_all, g)
            A.mla_kv(g)
            A.hgrn(True, None)
            A.mla_q()
            A.attention(g, NP)
            A.out_proj(x_all, g, h_mid, o)
        K.finish()
        K.replay()
    return nc


def phaseA_inputs(inp, b, q):
    NP = NSLOT * SLOT_ST * T
    NO = O_ST * T
    x = inp["x"][b]
    pos = inp["positions"][b]
    x_all = np.empty((NP + NO, D), np.float32)
    pos_all = np.empty((1, NP + NO), np.int32)
    for i in range(NSLOT):
        src = max(q - NSLOT + i, 0)
        x_all[i * SEG:(i + 1) * SEG] = x[src * SEG:(src + 1) * SEG]
        pos_all[0, i * SEG:(i + 1) * SEG] = pos[src * SEG:(src + 1) * SEG]
    x_all[NP:] = x[q * SEG:(q + 1) * SEG]
    pos_all[0, NP:] = pos[q * SEG:(q + 1) * SEG]
    return {"x_all": x_all, "pos_all": pos_all, "w_in": inp["e_w_in"][0], "w_uq": inp["e_w_uq"][0],
            "w_ukv": inp["e_w_ukv"][0], "w_out": inp["e_w_out"][0], "vecsA": vecsA_host(inp, q)}


RH = 8
NRANK = 4
NVC = 32


def ret_tables():
    g = 1.0 - 2.0 ** (-5.0 - np.arange(8, dtype=np.float64))
    idx = np.arange(128, dtype=np.float64)
    DT = np.zeros((128, 8, 128), np.float32)
    for h in range(8):
        diff = idx[None, :] - idx[:, None]
        DT[:, h, :] = np.where(diff >= 0, g[h] ** np.maximum(diff, 0), 0.0) / 16.0
    decq = np.zeros((128, 8, 128), np.float32)
    for h in range(8):
        decq[:, h, :] = (g[h] ** (idx + 1.0))[None, :]
    kdec = np.zeros((128, 8), np.float32)
    for h in range(8):
        kdec[:, h] = g[h] ** (127.0 - idx) / 16.0
    cdec = [float(g[h] ** 128.0) for h in range(8)]
    return DT, decq, kdec, cdec, g


def vecsC_host(inp, q):
    v = np.zeros((128, 16 + 1 + NVC), np.float32)
    v[:, 0:16] = inp["norm_mix"][1].reshape(16, 128).T
    v[:, 16] = (10000.0 ** (-np.arange(128, dtype=np.float32) / np.float32(128))).astype(np.float32)
    g = 1.0 - 2.0 ** (-5.0 - np.arange(8, dtype=np.float64))
    for r in range(NRANK):
        for h in range(8):
            v[:, 17 + r * 8 + h] = float(g[h] ** (float(SEG) * (q - 1 - r))) if r < q else 0.0
    return v


class PhaseC:
    def __init__(self, K, C, nc, full, kvs=None):
        self.K, self.C, self.nc, self.full, self.kvs = K, C, nc, full, kvs

    def prologue(self, w_in, w_out, scr0=None):
        K = self.K
        with contextlib.ExitStack() as st2:
            old = K.stack
            K.stack = st2
            WC = WCast(K, nslots=3)
            scr = dict(scr0) if scr0 else {}
            if "k" not in scr and not (self.full and self.kvs is not None):
                scr["k"] = WC.cast(w_in[:, 2048:4096], D, 2048, "sC_k", cw=128)
                scr["v"] = WC.cast(w_in[:, 4096:8192], D, 4096, "sC_v", cw=256)
            if self.full and "q" not in scr:
                scr["q"] = WC.cast(w_in[:, 0:2048], D, 2048, "sC_q", cw=128)
                scr["g"] = WC.cast(w_in[:, 8192:12288], D, 4096, "sC_g", cw=128)
                scr["wout"] = WC.cast(w_out, 4096, D, "sC_wout", cw=128)
            K.barrier()
            K.stack = old
        self.scr = scr

    def alloc(self, vecs, pos_own):
        K, C, nc, full = self.K, self.C, self.nc, self.full
        self.vec = K.sb("vecC", [128, 17 + NVC], F32)
        self.t_vec = K.trk("vecC")
        K.dma("sp", "const", self.vec[:], vecs[:, :], writes=[self.t_vec])
        DT, decq, kdec, cdec, _ = ret_tables()
        self.cdec = cdec
        self.t_cst = K.trk("cstC")
        self.kdec = K.sb("kdec", [128, 8], F32)
        K.dma("sp", "const", self.kdec[:], nc.inline_tensor(kdec, name="kdec_d%d" % int(full)).ap()[:, :], writes=[self.t_cst])
        if full:
            self.DT = K.sb("DT", [128, 8, 128], F32)
            self.decq = K.sb("decq", [128, 8, 128], F32)
            K.dma("sp", "const", self.DT[:], nc.inline_tensor(DT, name="DT_d%d" % int(full)).ap()[:, :, :], writes=[self.t_cst])
            K.dma("sp", "const", self.decq[:], nc.inline_tensor(decq, name="decq_d%d" % int(full)).ap()[:, :, :], writes=[self.t_cst])
        self.pos = pos_own
        self.xT = K.sb("hTc", [128, 16, T], F32)
        self.t_x = [K.trk("x") for _ in range(16)]
        self.wk_i = 0
        self.uT = K.sb("uTc", [128, 16, T], BF16)
        self.t_u = [K.trk("u") for _ in range(16)]
        self.rstd = K.sb("rstdC", [128, T], F32)
        self.t_rstd = K.trk("rstd")
        self.stage = Stage(K, "stgC", D, 2 if not full else 1)
        self.W = WStream(K, "wC", 16, 128, 2 if full else 3)
        self.use_kvs = self.kvs is not None
        nb_ = 2 if (full and self.use_kvs) else 1
        if not (full and self.use_kvs):
            self.WV = WStream(K, "wCv", 16, 256, 2)
        self.posi = K.sb("posiC", [128, T], I32)
        self.cos = K.sb("cosC", [128, T], F32)
        self.sin = K.sb("sinC", [128, T], F32)
        self.t_rope = K.trk("rope")
        self.t_cs = K.trk("cossin")
        self.wb = [K.sb("wbC%d" % i, [128, T], BF16) for i in range(4)]
        self.t_wb = [K.trk("wb") for _ in range(4)]
        self.wb_i = 0
        self.R = K.sb("R", [128, 8, 2, 512], F32)
        self.t_R = [[K.trk("R") for _ in range(2)] for _ in range(8)]
        self.vtm_b = [K.sb("vtmC%d" % i, [128, 4, 512], BF16) for i in range(nb_)]
        self.t_vtm_b = [[K.trk("vtm") for _ in range(4)] for _ in range(nb_)]
        self.kdtm_b = [K.sb("kdtm%d" % i, [128, 4, 256], BF16) for i in range(nb_)]
        self.t_kdtm_b = [[K.trk("kdtm") for _ in range(2)] for _ in range(nb_)]
        self.kr_b = [K.sb("krC%d" % i, [128, 2, T], BF16) for i in range(nb_)] if (full or self.use_kvs) else None
        self.t_kr_b = [K.trk("kr") for _ in range(nb_)]
        self.hb_i = 0
        if full:
            self.Rb = K.sb("Rb", [128, 2, 2, 512], BF16)
            self.t_Rb = [[K.trk("Rb") for _ in range(2)] for _ in range(2)]
            self.onT = K.sb("onT", [128, 32, T], BF16)
            self.t_on = [K.trk("on") for _ in range(32)]
            self.sq = self.onT
            self.t_sq = self.t_on
            self.WO = WStream(K, "wCo", 32, 128, 2)
            self.qr = K.sb("qrC", [128, 2, T], BF16)
            self.qd = K.sb("qdC", [128, 2, T], BF16)
            self.t_qr = K.trk("qr")
            self.t_qd = K.trk("qd")
            self.AD = [K.sb("AD%d" % i, [128, 128], BF16) for i in range(2)]
            self.t_AD = [K.trk("AD") for _ in range(2)]
        else:
            self.sq = K.sb("sqC", [128, 16, T], BF16)
            self.t_sq = [K.trk("sq") for _ in range(16)]

    def work(self):
        i = self.wk_i % 16
        self.wk_i += 1
        return self.xT[:, i, :], self.t_x[i]

    def workb(self):
        i = self.wb_i % 4
        self.wb_i += 1
        return self.wb[i], self.t_wb[i]

    def init_state(self, Rprev, t_Rprev=None):
        K = self.K
        self.t_Rprev = t_Rprev if t_Rprev is not None else K.trk("none")
        for h in range(8):
            for dc in range(2):
                if Rprev is None:
                    K.op("pool", lambda e, h=h, dc=dc: e.memset(self.R[:, h, dc, :], 0.0), writes=[self.t_R[h][dc]])
                    continue
                for i in range(NRANK - 1):
                    tmp, t_tmp = self.work()
                    src = Rprev[h // 2][i, h % 2, dc, :, :] if isinstance(Rprev, list) else Rprev[i, h, dc, :, :]
                    K.dma("sp", "rin%d" % (i % 2), tmp[:, :], src, reads=[self.t_Rprev], writes=[t_tmp])
                    c = self.vec[:, 17 + i * 8 + h:18 + i * 8 + h]
                    if i == 0:
                        K.ts("dve", self.R[:, h, dc, :], tmp[:, :], c, None, ALU.mult, None, [t_tmp, self.t_vec],
                             [self.t_R[h][dc]])
                    else:
                        K.stt(self.R[:, h, dc, :], tmp[:, :], c, self.R[:, h, dc, :], ALU.mult, ALU.add,
                              [t_tmp, self.t_vec, self.t_R[h][dc]], [self.t_R[h][dc]])

    def rope_tables(self, o):
        K, v = self.K, self.vec
        K.dma("sp", "posi", self.posi[:, :], self.pos[0:1, o * T:(o + 1) * T].broadcast_to([128, T]),
              writes=[self.t_rope])
        posf, t_pf = self.work()
        ra, t_ra = self.work()
        rn, t_rn = self.work()
        K.copy("dve", posf, self.posi[:, :], [self.t_rope], [t_pf])
        for off, out in ((0.0, self.sin), (float(np.pi / 2), self.cos)):
            K.ts("dve", ra, posf, v[:, 16:17], off, ALU.mult, ALU.add, [t_pf, self.t_vec], [t_ra])
            K.ts("dve", rn, ra, float(1.0 / TWO_PI), MAGIC, ALU.mult, ALU.add, [t_ra], [t_rn])
            K.ts("dve", rn, rn, -MAGIC, None, ALU.add, None, [t_rn], [t_rn])
            K.stt(ra, rn, -C1, ra, ALU.mult, ALU.add, [t_rn, t_ra], [t_ra])
            K.stt(ra, rn, -C2, ra, ALU.mult, ALU.add, [t_rn, t_ra], [t_ra])
            K.ts("dve", ra, ra, -PI_LO, PI_LO, ALU.max, ALU.min, [t_ra], [t_ra])
            K.act(out[:, :], ra, AF.Sin, [t_ra], [self.t_cs])

    def proj_fm(self, w, t_w, rhsT, t_rhs, nkc=16):
        K, C = self.K, self.C
        bk, t_bk = C.bank()
        for kc in range(nkc):
            K.mm(bk[:, :], w[:, kc, :], rhsT[:, kc, :], kc == 0, kc == nkc - 1, [t_w, t_rhs[kc]], [t_bk])
        return bk, t_bk

    def rope_pair(self, b1, t_b1, b2, t_b2, out1, out2, t_outs, f32=False):
        K = self.K
        a, t_a = self.work()
        b, t_b = self.work()
        K.tt("dve", a, b1[:, :], self.cos[:, :], ALU.mult, [t_b1, self.t_cs], [t_a])
        K.tt("dve", b, b2[:, :], self.sin[:, :], ALU.mult, [t_b2, self.t_cs], [t_b])
        K.tt("pool", out1, a, b, ALU.subtract, [t_a, t_b], t_outs)
        c, t_c = self.work()
        d, t_d = self.work()
        K.tt("dve", c, b2[:, :], self.cos[:, :], ALU.mult, [t_b2, self.t_cs], [t_c])
        K.tt("dve", d, b1[:, :], self.sin[:, :], ALU.mult, [t_b1, self.t_cs], [t_d])
        K.tt("pool", out2, c, d, ALU.add, [t_c, t_d], t_outs)

    def supertile(self, h_in, o, h_out=None):
        K, C, full = self.K, self.C, self.full
        NOT = K.trk("none")
        load_tokmajor_T(C, h_in[o * T:(o + 1) * T, :], D, self.xT, self.t_x, self.stage)
        rmsnorm_fm(C, self.xT, self.t_x, 16, self.vec[:, 0:16], self.t_vec, self.uT, self.t_u,
                   self.sq, self.t_sq, self.rstd, self.t_rstd, D)
        self.rope_tables(o)
        for h in range(8):
            if getattr(self, "hook", None) is not None:
                self.hook()
            bi = self.hb_i % len(self.vtm_b)
            self.hb_i += 1
            self.vtm, self.t_vtm = self.vtm_b[bi], self.t_vtm_b[bi]
            self.kdtm, self.t_kdtm = self.kdtm_b[bi], self.t_kdtm_b[bi]
            if self.kr_b is not None:
                self.kr, self.t_kr = self.kr_b[bi], self.t_kr_b[bi]
            load_kv = full and self.use_kvs
            if load_kv:
                K.dma("sp", "kvl%d" % bi, self.kdtm[:, :, :], self.kvs["kd"][o, h], writes=self.t_kdtm)
                K.dma("sp", "kvv%d" % bi, self.vtm[:, :, :], self.kvs["v"][o, h], writes=self.t_vtm)
                K.dma("sp", "kvr%d" % bi, self.kr[:, :, :], self.kvs["kr"][o, h], writes=[self.t_kr])
            else:
                wk1, t_wk1 = self.W.load(self.scr["k"], NOT, 2 * h)
                b1, t_b1 = self.proj_fm(wk1, t_wk1, self.uT, self.t_u)
                wk2, t_wk2 = self.W.load(self.scr["k"], NOT, 2 * h + 1)
                b2, t_b2 = self.proj_fm(wk2, t_wk2, self.uT, self.t_u)
                k1, t_k1 = self.work()
                k2, t_k2 = self.work()
                self.rope_pair(b1, t_b1, b2, t_b2, k1, k2, [t_k1, t_k2])
                for half in range(2):
                    wv, t_wv = self.WV.load(self.scr["v"], NOT, 2 * h + half)
                    for tb in range(4):
                        bk, t_bk = C.bank()
                        for kc in range(16):
                            K.mm(bk[:, 0:256], self.uT[:, kc, tb * 128:(tb + 1) * 128], wv[:, kc, :], kc == 0, kc == 15,
                                 [self.t_u[kc], t_wv], [t_bk])
                        K.copy("act", self.vtm[:, tb, half * 256:(half + 1) * 256], bk[:, 0:256], [t_bk], [self.t_vtm[tb]])
            if full:
                wq1, t_wq1 = self.W.load(self.scr["q"], NOT, 2 * h)
                q1, t_q1 = self.proj_fm(wq1, t_wq1, self.uT, self.t_u)
                wq2, t_wq2 = self.W.load(self.scr["q"], NOT, 2 * h + 1)
                q2, t_q2 = self.proj_fm(wq2, t_wq2, self.uT, self.t_u)
                self.rope_pair(q1, t_q1, q2, t_q2, self.qr[:, 0, :], self.qr[:, 1, :], [self.t_qr])
                dq = self.decq[:, h:h + 1, :].broadcast_to([128, 4, 128])
                for dc in range(2):
                    K.tt("dve", self.qd[:, dc, :].rearrange("p (c t) -> p c t", c=4),
                         self.qr[:, dc, :].rearrange("p (c t) -> p c t", c=4), dq, ALU.mult,
                         [self.t_qr, self.t_cst], [self.t_qd])
            for dc, (kf, t_kf) in (enumerate(((k1, t_k1), (k2, t_k2))) if not load_kv else ()):
                bt, t_bt = C.bank()
                for c in range(4):
                    K.tr(bt[:, c * 128:(c + 1) * 128], kf[:, c * 128:(c + 1) * 128], C.ident[:], [t_kf, C.t_ident],
                         [t_bt], inc=(c == 3))
                K.ts("dve", self.kdtm[:, :, dc * 128:(dc + 1) * 128], bt[:, :].rearrange("p (c d) -> p c d", c=4),
                     self.kdec[:, h:h + 1], None, ALU.mult, None, [t_bt, self.t_cst], [self.t_kdtm[dc]])
                if self.kr_b is not None:
                    K.copy("act", self.kr[:, dc, :], kf, [t_kf], [self.t_kr])
            if self.use_kvs and not full:
                K.dma("pool", "kvs0", self.kvs["kd"][o, h], self.kdtm[:, :, :], reads=self.t_kdtm, writes=[])
                K.dma("pool", "kvs1", self.kvs["v"][o, h], self.vtm[:, :, :], reads=self.t_vtm, writes=[])
                K.dma("pool", "kvs2", self.kvs["kr"][o, h], self.kr[:, :, :], reads=[self.t_kr], writes=[])
            if full:
                bo = [C.bank(hold=True) for _ in range(4)]
                for dc in range(2):
                    K.copy("pool", self.Rb[:, h % 2, dc, :], self.R[:, h, dc, :], [self.t_R[h][dc]], [self.t_Rb[h % 2][dc]])
            for c in range(4):
                cs = slice(c * 128, (c + 1) * 128)
                if full:
                    ba, t_ba = C.bank()
                    for dc in range(2):
                        K.mm(ba[:, 0:128], self.kr[:, dc, cs], self.qr[:, dc, cs], dc == 0, dc == 1,
                             [self.t_kr, self.t_qr], [t_ba])
                    ad, t_ad = self.AD[c % 2], self.t_AD[c % 2]
                    K.tt("dve", ad[:, :], ba[:, 0:128], self.DT[:, h, :], ALU.mult, [t_ba, self.t_cst], [t_ad])
                    for vc in range(4):
                        bov, t_bov = bo[vc]
                        K.mm(bov[:, cs], self.vtm[:, c, vc * 128:(vc + 1) * 128], ad[:, :], True, False,
                             [self.t_vtm[c], t_ad], [t_bov], inc=False)
                        for dc in range(2):
                            K.mm(bov[:, cs], self.Rb[:, h % 2, dc, vc * 128:(vc + 1) * 128], self.qd[:, dc, cs], False, dc == 1,
                                 [self.t_Rb[h % 2][dc], self.t_qd], [t_bov], inc=(dc == 1))
                for dc in range(2):
                    bs, t_bs = C.bank()
                    K.mm(bs[:, :], self.kdtm[:, c, dc * 128:(dc + 1) * 128], self.vtm[:, c, :], True, True,
                         [self.t_kdtm[dc], self.t_vtm[c]], [t_bs])
                    K.stt(self.R[:, h, dc, :], self.R[:, h, dc, :], self.cdec[h], bs[:, :], ALU.mult, ALU.add,
                          [self.t_R[h][dc], t_bs], [self.t_R[h][dc]])
                    if full and c < 3:
                        K.copy("pool", self.Rb[:, h % 2, dc, :], self.R[:, h, dc, :], [self.t_R[h][dc]], [self.t_Rb[h % 2][dc]])
            if full:
                ofs = []
                sqs = []
                for vc in range(4):
                    bov, t_bov = bo[vc]
                    of, t_of = self.work()
                    K.copy("act", of, bov[:, :], [t_bov], [t_of])
                    sqb, t_sqb = self.workb()
                    K.act(sqb[:, :], bov[:, :], AF.Square, [t_bov], [t_sqb])
                    C.release(t_bov)
                    ofs.append((of, t_of))
                    sqs.append((sqb, t_sqb))
                bk, t_bk = C.bank()
                for vc in range(4):
                    K.mm(bk[:, :], C.ones[:, :], sqs[vc][0][:, :], vc == 0, vc == 3, [C.t_ones, sqs[vc][1]], [t_bk])
                ro, t_ro = self.work()
                K.act(ro, bk[:, :], AF.Ln, [t_bk], [t_ro], bias=EPS, scale=1.0 / 512)
                K.act(ro, ro, AF.Exp, [t_ro], [t_ro], scale=-0.5)
                for vc in range(4):
                    wg_, t_wg = self.W.load(self.scr["g"], NOT, 4 * h + vc)
                    bg, t_bg = self.proj_fm(wg_, t_wg, self.uT, self.t_u)
                    sg, t_sg = self.work()
                    K.act(sg, bg[:, :], AF.Silu, [t_bg], [t_sg])
                    of, t_of = ofs[vc]
                    K.tt("dve", of, of, ro, ALU.mult, [t_of, t_ro], [t_of])
                    K.tt("pool", self.onT[:, 4 * h + vc, :], of, sg, ALU.mult, [t_of, t_sg], [self.t_on[4 * h + vc]])
        if full:
            for n in range(16):
                w, t_w = self.WO.load(self.scr["wout"], NOT, n)
                bk, t_bk = self.proj_fm(w, t_w, self.onT, self.t_on, nkc=32)
                K.copy("act", self.xT[:, n, :], bk[:, :], [t_bk], [self.t_x[n]])
            for tb in range(4):
                sb_, t_sb, key = self.stage.next()
                K.dma("sp", key, sb_[:, :], h_in[o * T + tb * 128:o * T + (tb + 1) * 128, :], writes=[t_sb])
                for c4 in range(4):
                    bk, t_bk = C.bank()
                    for cc in range(4):
                        kc = c4 * 4 + cc
                        K.tr(bk[:, cc * 128:(cc + 1) * 128], self.xT[:, kc, tb * 128:(tb + 1) * 128], C.ident[:],
                             [self.t_x[kc], C.t_ident], [t_bk], inc=(cc == 3))
                    K.tt("dve", sb_[:, c4 * 512:(c4 + 1) * 512], bk[:, :], sb_[:, c4 * 512:(c4 + 1) * 512], ALU.add,
                         [t_bk, t_sb], [t_sb])
                K.dma("pool", key + "s", h_out[o * T + tb * 128:o * T + (tb + 1) * 128, :], sb_[:, :], reads=[t_sb],
                      writes=[])

    def store_state(self, R_out):
        K = self.K
        for h in range(8):
            for dc in range(2):
                dst = R_out[h // 2][h % 2, dc, :, :] if isinstance(R_out, list) else R_out[h, dc, :, :]
                K.dma("pool", "rout%d" % dc, dst, self.R[:, h, dc, :], reads=[self.t_R[h][dc]], writes=[])


def build_phaseC(nc, full):
    stack = contextlib.ExitStack()
    with stack:
        K = Kern(nc, stack)
        h_in = dram_in(nc, "h_in", [O_ST * T, D])
        pos = dram_in(nc, "pos", [1, O_ST * T], I32)
        w_in = dram_in(nc, "w_in", [D, 12288])
        vecs = dram_in(nc, "vecsC", [128, 17 + NVC])
        if full:
            w_out = dram_in(nc, "w_out", [4096, D])
            Rprev = dram_in(nc, "Rprev", [NRANK, 8, 2, 128, 512])
            h_out = dram_out(nc, "h_mid", [O_ST * T, D])
        else:
            w_out = None
            R_out = dram_out(nc, "R_out", [8, 2, 128, 512])
        C = Ctx(K)
        P = PhaseC(K, C, nc, full)
        P.prologue(w_in, w_out)
        P.alloc(vecs, pos)
        P.init_state(Rprev if full else None)
        for o in range(O_ST):
            P.supertile(h_in, o, h_out if full else None)
        if not full:
            P.store_state(R_out)
        K.finish()
        K.replay()
    return nc


def _launch(build, maps):
    nc = bass.Bass("TRN2", target_bir_lowering=False)
    build(nc)
    res = run_bass_kernel_spmd(nc, maps, core_ids=list(range(NCORE)))
    return res.results


def _ffn_maps(inp, li, hs):
    maps = []
    vec = ffn_vecs(inp["norm_ffn"][li], inp["norm_ple"][li], inp["ffn_conv_w"][li], inp["ffn_conv_b"][li])
    for c in range(NCORE):
        b, q = c // 4, c % 4
        hprev = np.ascontiguousarray(hs[c - 1][-2:, :]) if q > 0 else np.zeros((2, D), np.float32)
        maps.append({"h_in": hs[c], "hprev": hprev,
                     "p_in": np.ascontiguousarray(inp["p"][li, b, q * SEG:(q + 1) * SEG]),
                     "w_gate": inp["ffn_w_gate"][li], "w_up": inp["ffn_w_up"][li], "w_down": inp["ffn_w_down"][li],
                     "w_pg": inp["ple_w_gate"][li], "w_pp": inp["ple_w_proj"][li], "vecs": vec})
    return maps


def kernel_unfused(**inputs):
    inp = {k: np.asarray(v) for k, v in inputs.items()}
    r = _launch(build_phaseA, [phaseA_inputs(inp, c // 4, c % 4) for c in range(NCORE)])
    hmid0 = [np.ascontiguousarray(x["h_mid"]) for x in r]
    r = _launch(build_ffn_phase, _ffn_maps(inp, 0, hmid0))
    h1 = [np.ascontiguousarray(x["h_out"]) for x in r]
    posm = [np.ascontiguousarray(inp["positions"][c // 4][None, (c % 4) * SEG:(c % 4 + 1) * SEG]) for c in range(NCORE)]
    w_in1 = inp["o_w_in"][0]
    r = _launch(lambda nc: build_phaseC(nc, False),
                [{"h_in": h1[c], "pos": posm[c], "w_in": w_in1, "vecsC": vecsC_host(inp, c % 4)} for c in range(NCORE)])
    Rl = [x["R_out"] for x in r]
    maps = []
    for c in range(NCORE):
        q = c % 4
        Rprev = np.zeros((NRANK, 8, 2, 128, 512), np.float32)
        for r_ in range(q):
            Rprev[r_] = Rl[c - q + r_]
        maps.append({"h_in": h1[c], "pos": posm[c], "w_in": w_in1, "w_out": inp["o_w_out"][0], "Rprev": Rprev,
                     "vecsC": vecsC_host(inp, q)})
    r = _launch(lambda nc: build_phaseC(nc, True), maps)
    hmid1 = [np.ascontiguousarray(x["h_mid"]) for x in r]
    r = _launch(build_ffn_phase, _ffn_maps(inp, 1, hmid1))
    out = np.empty((2, 4 * SEG, D), np.float32)
    for c in range(NCORE):
        out[c // 4, (c % 4) * SEG:(c % 4 + 1) * SEG] = r[c]["h_out"]
    return out


GROUPS = [[0, 1, 2, 3], [4, 5, 6, 7]]
DEBUG_OUT = False


def build_fused(nc):
    stack = contextlib.ExitStack()
    with stack:
        K = Kern(nc, stack)
        NP = NSLOT * SLOT_ST
        ntok = (NP + O_ST) * T
        x_all = dram_in(nc, "x_all", [ntok, D])
        pos_all = dram_in(nc, "pos_all", [1, ntok], I32)
        w_in = dram_in(nc, "w_in", [D, 5184])
        w_uq = dram_in(nc, "w_uq", [512, 1536])
        w_ukv = dram_in(nc, "w_ukv", [512, 2048])
        w_out = dram_in(nc, "w_out", [D, D])
        vecsA = dram_in(nc, "vecsA", [128, NVA])
        ffw = []
        for li in range(2):
            ffw.append(dict(
                p=dram_in(nc, "p%d" % li, [SEG, PLE]),
                gate=dram_in(nc, "w_gate%d" % li, [D, DFF]), up=dram_in(nc, "w_up%d" % li, [D, DFF]),
                down=dram_in(nc, "w_down%d" % li, [DFF, D]), pg=dram_in(nc, "w_pg%d" % li, [D, D]),
                pp=dram_in(nc, "w_pp%d" % li, [PLE, D]), vecs=dram_in(nc, "vecsF%d" % li, [128, NVF])))
        o_w_in = dram_in(nc, "o_w_in", [D, 12288])
        o_w_out = dram_in(nc, "o_w_out", [4096, D])
        vecsC = dram_in(nc, "vecsC", [128, 17 + NVC])
        out = dram_out(nc, "out", [SEG, D])
        internal = lambda name, shape: nc.dram_tensor(name, list(shape), F32, kind="Internal").ap()
        mk = (lambda name, shape: dram_out(nc, name, shape)) if DEBUG_OUT else internal
        hA = mk("hA", [SEG, D])
        hB = mk("hB", [SEG, D])
        hC = mk("hC", [SEG, D])
        hl = [internal("hl%d" % i, [2, D]) for i in range(2)]
        hg = [internal("hg%d" % i, [8, D]) for i in range(2)]
        Rloc = [internal("Rloc%d" % i, [512, 512]) for i in range(4)]
        Rall = [internal("Rall%d" % i, [NRANK * 512, 512]) for i in range(4)]
        C = Ctx(K)
        BYP = mybir.AluOpType.bypass

        with K.phase("A_"):
            A = PhaseA(K, C, nc)
            A.prologue(w_in, w_uq, w_ukv, w_out)
            A.alloc(vecsA, pos_all, ntok)
            for g in range(NP):
                A.front(x_all, g)
                A.mla_kv(g)
                A.hgrn(False, A.vec[:, 69 + g // SLOT_ST:70 + g // SLOT_ST])
            for o in range(O_ST):
                g = NP + o
                A.front(x_all, g)
                A.mla_kv(g)
                A.hgrn(True, None)
                A.mla_q()
                A.attention(g, NP)
                A.out_proj(x_all, g, hA, o)

        def exchange(i, hsrc):
            t_hl = K.trk("hl")
            K.dma("sp", "xch", hl[i][:, :], hsrc[SEG - 2:SEG, :], writes=[t_hl])
            t_g = K.trk("hg")
            K.barrier()
            K.collective("AllGather", BYP, GROUPS, hl[i], hg[i], reads=[t_hl], writes=[t_g])
            K.barrier()
            return t_g

        t_g0 = exchange(0, hA)
        f = ffw[0]
        with K.phase("B_"):
            scr = cast_ffn_weights(K, f["gate"], f["up"], f["down"], f["pg"], f["pp"], pfx="B")
            ffn_phase_body(K, C, hA, None, f["p"], f["vecs"], scr, hB, gath=hg[0], t_gath=t_g0)
        pos_own = pos_all[:, NP * T:NP * T + SEG]
        ibf = lambda name, shape: nc.dram_tensor(name, list(shape), BF16, kind="Internal").ap()
        kvs = {"kd": ibf("kvs_kd", [O_ST, 8, 128, 4, 256]), "v": ibf("kvs_v", [O_ST, 8, 128, 4, 512]),
               "kr": ibf("kvs_kr", [O_ST, 8, 128, 2, T])}
        with K.phase("C1_"):
            P1 = PhaseC(K, C, nc, False, kvs)
            P1.prologue(o_w_in, None)
            P1.alloc(vecsC, pos_own)
            P1.init_state(None)
            WCd = WCast(K, nslots=3, rows=4, defer=True, tag="d")
            scr_c2 = dict(P1.scr)
            scr_c2["q"] = WCd.cast(o_w_in[:, 0:2048], D, 2048, "sC_q", cw=128)
            scr_c2["g"] = WCd.cast(o_w_in[:, 8192:12288], D, 4096, "sC_g", cw=128)
            scr_c2["wout"] = WCd.cast(o_w_out, 4096, D, "sC_wout", cw=128)
            fD = ffw[1]
            scr_d = {"gate": WCd.cast(fD["gate"], D, DFF, "Ds_gate", cw=128),
                     "up": WCd.cast(fD["up"], D, DFF, "Ds_up", cw=128),
                     "down": WCd.cast(fD["down"], DFF, D, "Ds_down", cw=128),
                     "pg": WCd.cast(fD["pg"], D, D, "Ds_pg", cw=256),
                     "pp": WCd.cast(fD["pp"], PLE, D, "Ds_pp", cw=512)}
            per_head = -(-len(WCd.steps) // (8 * O_ST))
            P1.hook = lambda: WCd.pump(per_head)
            for o in range(O_ST):
                P1.supertile(hB, o)
            WCd.pump(len(WCd.steps))
            P1.store_state([Rloc[i].rearrange("(h dc p) n -> h dc p n", h=2, dc=2) for i in range(4)])
        t_R = K.trk("Rall")
        for i in range(4):
            K.collective("AllGather", BYP, GROUPS, Rloc[i], Rall[i], reads=[], writes=[t_R])
            K.barrier()
        with K.phase("C2_"):
            P2 = PhaseC(K, C, nc, True, kvs)
            P2.prologue(o_w_in, o_w_out, scr0=scr_c2)
            P2.alloc(vecsC, pos_own)
            P2.init_state([Rall[i].rearrange("(r h dc p) n -> r h dc p n", r=NRANK, h=2, dc=2) for i in range(4)], t_R)
            for o in range(O_ST):
                P2.supertile(hB, o, hC)
        t_g1 = exchange(1, hC)
        f = ffw[1]
        with K.phase("D_"):
            ffn_phase_body(K, C, hC, None, f["p"], f["vecs"], scr_d, out, gath=hg[1], t_gath=t_g1)
        K.finish()
        K.replay()
    return nc


def fused_inputs(inp, c):
    b, q = c // 4, c % 4
    m = phaseA_inputs(inp, b, q)
    for li in range(2):
        m["p%d" % li] = np.ascontiguousarray(inp["p"][li, b, q * SEG:(q + 1) * SEG])
        m["w_gate%d" % li] = inp["ffn_w_gate"][li]
        m["w_up%d" % li] = inp["ffn_w_up"][li]
        m["w_down%d" % li] = inp["ffn_w_down"][li]
        m["w_pg%d" % li] = inp["ple_w_gate"][li]
        m["w_pp%d" % li] = inp["ple_w_proj"][li]
        m["vecsF%d" % li] = ffn_vecs(inp["norm_ffn"][li], inp["norm_ple"][li], inp["ffn_conv_w"][li],
                                      inp["ffn_conv_b"][li], q)
    m["o_w_in"] = inp["o_w_in"][0]
    m["o_w_out"] = inp["o_w_out"][0]
    m["vecsC"] = vecsC_host(inp, q)
    return m


def kernel(**inputs):
    inp = {k: np.asarray(v) for k, v in inputs.items()}
    nc = bass.Bass("TRN2", target_bir_lowering=False)
    build_fused(nc)
    maps = [fused_inputs(inp, c) for c in range(NCORE)]
    res = run_bass_kernel_spmd(nc, maps, core_ids=list(range(NCORE)))
    out = np.empty((2, 4 * SEG, D), np.float32)
    for c in range(NCORE):
        out[c // 4, (c % 4) * SEG:(c % 4 + 1) * SEG] = res.results[c]["out"]
    return out
```

```python
import contextlib
import numpy as np
import concourse.bass as bass
import concourse.mybir as mybir
from concourse.bass_utils import run_bass_kernel_spmd

F32 = mybir.dt.float32
BF16 = mybir.dt.bfloat16
I32 = mybir.dt.int32
AF = mybir.ActivationFunctionType
ALU = mybir.AluOpType

D = 2048
NCORE = 8
SEG = 4096
T = 512
NST = SEG // T
DFF = 5632
NFF = DFF // 128
PLE = 256
EPS = 1e-6


class Trk:
    __slots__ = ("name", "w", "r")

    def __init__(self, name):
        self.name = name
        self.w = None
        self.r = {}


class Stream:
    __slots__ = ("name", "sem", "cnt")

    def __init__(self, name, sem):
        self.name, self.sem, self.cnt = name, sem, 0


class Kern:
    ENG = ("pe", "act", "dve", "pool", "sp")

    def __init__(self, nc, stack):
        self.nc = nc
        self.stack = stack
        self.root = stack
        self.pfx = ""
        self.ncc = 0
        self.ops = {e: [] for e in self.ENG}
        self.streams = {}
        for e in ("pe", "act", "dve", "pool"):
            self.streams[e] = Stream(e, stack.enter_context(nc.semaphore("s_" + e)))
        self.seen = {e: {} for e in self.ENG}
        self.dma_streams = {}
        self.nbank = 0
        self.uid = 0

    def sb(self, name, shape, dt):
        return self.stack.enter_context(self.nc.sbuf_tensor(self.pfx + name, list(shape), dt))

    @contextlib.contextmanager
    def phase(self, pfx):
        old, oldp = self.stack, self.pfx
        with contextlib.ExitStack() as st2:
            self.stack = st2
            self.pfx = pfx
            yield
            self.barrier()
            self.stack, self.pfx = old, oldp

    def collective(self, kind, op, groups, in_ap, out_ap, reads=(), writes=()):
        key = "cc%d" % self.ncc
        self.ncc += 1
        st = Stream(key, self.root.enter_context(self.nc.semaphore("c_" + key)))
        self.dma_streams[key] = st
        deps = self._deps(reads, writes)
        self._emit_waits("pool", deps)
        st.cnt += 1
        self.ops["pool"].append(("ins", lambda e: e.collective_compute(kind, op, replica_groups=groups,
                                                                       ins=[in_ap.opt()], outs=[out_ap.opt()]),
                                 st.sem, None))
        for t in reads:
            if t.r.get(st, 0) < 1:
                t.r[st] = 1
        for t in writes:
            t.w = (st, 1)
            t.r = {}

    def psum_bank(self):
        t = self.stack.enter_context(self.nc.psum_tensor("bank%d" % self.nbank, [128, 512], F32))
        self.nbank += 1
        return t

    def trk(self, name="t"):
        self.uid += 1
        return Trk("%s%d" % (name, self.uid))

    def _deps(self, reads, writes):
        deps = {}
        for t in reads:
            if t.w is not None:
                s, v = t.w
                if deps.get(s, 0) < v:
                    deps[s] = v
        for t in writes:
            if t.w is not None:
                s, v = t.w
                if deps.get(s, 0) < v:
                    deps[s] = v
            for s, v in t.r.items():
                if deps.get(s, 0) < v:
                    deps[s] = v
        return deps

    def _emit_waits(self, eng, deps, own=None):
        seen = self.seen[eng]
        for s, v in deps.items():
            if s is own and eng == "pe":
                continue
            if seen.get(s, 0) >= v:
                continue
            seen[s] = v
            self.ops[eng].append(("wait", s.sem, v))

    def op(self, eng, fn, reads=(), writes=(), inc=True):
        st = self.streams[eng]
        deps = self._deps(reads, writes)
        self._emit_waits(eng, deps, own=st)
        if inc:
            st.cnt += 1
            val = st.cnt
            self.ops[eng].append(("ins", fn, st.sem, 1))
        else:
            val = st.cnt + 1
            self.ops[eng].append(("ins", fn, None, 0))
        for t in reads:
            if t.r.get(st, 0) < val:
                t.r[st] = val
        for t in writes:
            t.w = (st, val)
            t.r = {}

    def dma(self, q, key, out, in_, reads=(), writes=(), **kw):
        st = self.dma_streams.get(key)
        if st is None:
            st = Stream("dma_" + key, self.root.enter_context(self.nc.semaphore("d_" + key)))
            self.dma_streams[key] = st
        deps = self._deps(reads, writes)
        if st.cnt > 0:
            deps[st] = max(deps.get(st, 0), st.cnt)
        self._emit_waits(q, deps)
        st.cnt += 16
        val = st.cnt
        self.ops[q].append(("ins", lambda e, o=out, i=in_: e.dma_start(out=o, in_=i, **kw), st.sem, 16))
        for t in reads:
            if t.r.get(st, 0) < val:
                t.r[st] = val
        for t in writes:
            t.w = (st, val)
            t.r = {}

    def barrier(self):
        allst = list(self.streams.values()) + list(self.dma_streams.values())
        for e in self.ENG:
            for st in allst:
                if st.cnt > 0 and self.seen[e].get(st, 0) < st.cnt:
                    if e == "pe" and st is self.streams["pe"]:
                        continue
                    self.seen[e][st] = st.cnt
                    self.ops[e].append(("wait", st.sem, st.cnt))

    def finish(self):
        for st in self.dma_streams.values():
            if st.cnt > 0 and self.seen["sp"].get(st, 0) < st.cnt:
                self.ops["sp"].append(("wait", st.sem, st.cnt))
        for e in ("pe", "act", "dve", "pool"):
            st = self.streams[e]
            if st.cnt > 0:
                self.ops["sp"].append(("wait", st.sem, st.cnt))

    def replay(self):
        nc = self.nc
        with nc.Block() as block:
            def run(eng_obj, lst):
                for o in lst:
                    if o[0] == "wait":
                        eng_obj.wait_ge(o[1], o[2])
                    else:
                        ins = o[1](eng_obj)
                        if o[2] is not None:
                            if o[3] is None:
                                ins.then_inc(o[2])
                            else:
                                ins.then_inc(o[2], o[3])

            @block.tensor
            def _(e):
                run(e, self.ops["pe"])

            @block.scalar
            def _(e):
                run(e, self.ops["act"])

            @block.vector
            def _(e):
                run(e, self.ops["dve"])

            @block.gpsimd
            def _(e):
                run(e, self.ops["pool"])

            @block.sync
            def _(e):
                run(e, self.ops["sp"])

    def mm(self, out, lhsT, rhs, start, stop, reads, writes, inc=None):
        if inc is None:
            inc = stop
        self.op("pe", lambda e: e.matmul(out, lhsT, rhs, start=start, stop=stop), reads, writes, inc=inc)

    def tr(self, out, in_, ident, reads, writes, inc=True):
        self.op("pe", lambda e: e.transpose(out, in_, ident), reads, writes, inc=inc)

    def act(self, out, in_, func, reads, writes, bias=None, scale=None, eng="act"):
        kw = {}
        if bias is not None:
            kw["bias"] = bias
        if scale is not None:
            kw["scale"] = scale
        self.op(eng, lambda e: e.activation(out, in_, func, **kw), reads, writes)

    def tt(self, eng, out, in0, in1, op, reads, writes):
        self.op(eng, lambda e: e.tensor_tensor(out, in0, in1, op), reads, writes)

    def ts(self, eng, out, in0, s1, s2, op0, op1, reads, writes):
        if op1 is None:
            self.op(eng, lambda e: e.tensor_scalar(out, in0, s1, None, op0), reads, writes)
        else:
            self.op(eng, lambda e: e.tensor_scalar(out, in0, s1, s2, op0, op1), reads, writes)

    def stt(self, out, in0, scalar, in1, op0, op1, reads, writes):
        self.op("dve", lambda e: e.scalar_tensor_tensor(out, in0, scalar, in1, op0, op1), reads, writes)

    def copy(self, eng, out, in_, reads, writes):
        if eng == "act":
            self.op(eng, lambda e: e.copy(out, in_), reads, writes)
        else:
            self.op(eng, lambda e: e.tensor_copy(out, in_), reads, writes)


class Ctx:
    def __init__(self, K):
        self.K = K
        nc = K.nc
        self.banks = [K.psum_bank() for _ in range(8)]
        self.bank_trk = [K.trk("bank") for _ in range(8)]
        self.bank_i = 0
        self.held = set()
        self.ident = K.sb("ident", [128, 128], F32)
        self.t_ident = K.trk("ident")
        idd = nc.inline_tensor(np.eye(128, dtype=np.float32), name="ident_d").ap()
        K.dma("sp", "const", self.ident[:], idd[:, :], writes=[self.t_ident])
        self.ones_f = K.sb("ones_f", [128, 128], F32)
        self.ones = K.sb("ones_b", [128, 128], BF16)
        self.t_ones = K.trk("ones")
        K.op("pool", lambda e: e.memset(self.ones_f[:], 1.0), writes=[self.t_ones])
        K.copy("pool", self.ones[:], self.ones_f[:], [self.t_ones], [self.t_ones])

    def bank(self, hold=False):
        for _ in range(8):
            i = self.bank_i
            self.bank_i = (i + 1) % 8
            if i not in self.held:
                break
        else:
            raise RuntimeError("all PSUM banks held")
        if hold:
            self.held.add(i)
        return self.banks[i], self.bank_trk[i]

    def release(self, t_bk):
        self.held.discard(self.bank_trk.index(t_bk))


class WCast:
    def __init__(self, K, nslots=3, rows=8, defer=False, tag=""):
        self.K = K
        self.n = nslots
        self.f = [K.sb("wc%s_f%d" % (tag, i), [128, rows, 512], F32) for i in range(nslots)]
        self.b = [K.sb("wc%s_b%d" % (tag, i), [128, rows, 512], BF16) for i in range(nslots)]
        self.cap = rows * 512
        self.defer = defer
        self.steps = []
        self.tf = [K.trk("wcf") for _ in range(nslots)]
        self.tb = [K.trk("wcb") for _ in range(nslots)]
        self.i = 0
        self.engs = ("pool", "dve", "act")

    def pump(self, n):
        for _ in range(min(n, len(self.steps))):
            self.steps.pop(0)()

    def cast(self, w, Kd, N, name, cw=512):
        K = self.K
        nkc = Kd // 128
        nb = N // cw
        scr = K.nc.dram_tensor(name, [nb, 128, nkc, cw], BF16, kind="Internal").ap()
        wv = w.rearrange("(kc p) n -> p kc n", p=128)
        kstep = max(1, min(nkc, self.cap // cw, 8))
        for b in range(nb):
            for k0 in range(0, nkc, kstep):
                kn = min(kstep, nkc - k0)
                def step(b=b, k0=k0, kn=kn):
                    s = self.i % self.n
                    self.i += 1
                    fv = self.f[s][:].rearrange("p a b -> p (a b)")[:, 0:kn * cw].rearrange("p (a b) -> p a b", a=kn)
                    bv = self.b[s][:].rearrange("p a b -> p (a b)")[:, 0:kn * cw].rearrange("p (a b) -> p a b", a=kn)
                    K.dma("sp", "wcl%d" % s, fv, wv[:, k0:k0 + kn, b * cw:(b + 1) * cw], writes=[self.tf[s]])
                    eng = self.engs[self.i % 3]
                    K.copy(eng, bv, fv, [self.tf[s]], [self.tb[s]])
                    K.dma("pool", "wcs%d" % s, scr[b, :, k0:k0 + kn, :], bv, reads=[self.tb[s]], writes=[])
                if self.defer:
                    self.steps.append(step)
                else:
                    step()
        return scr


class Stage:
    def __init__(self, K, name, ncol, nslots):
        self.buf = [K.sb("%s%d" % (name, i), [128, ncol], F32) for i in range(nslots)]
        self.t = [K.trk(name) for _ in range(nslots)]
        self.n = nslots
        self.i = 0
        self.name = name

    def next(self):
        s = self.i % self.n
        self.i += 1
        return self.buf[s], self.t[s], "%s%d" % (self.name, s)


def load_tokmajor_T(C, src, ncol, dst, t_dst, stage):
    K = C.K
    nch = ncol // 128
    for tb in range(T // 128):
        sb_, t_sb, key = stage.next()
        K.dma("sp", key, sb_[:, 0:ncol], src[tb * 128:(tb + 1) * 128, :], writes=[t_sb])
        for c0 in range(0, nch, 4):
            cn = min(4, nch - c0)
            bk, tb_ = C.bank()
            for cc in range(cn):
                kc = c0 + cc
                K.tr(bk[:, cc * 128:(cc + 1) * 128], sb_[:, kc * 128:(kc + 1) * 128], C.ident[:],
                     [t_sb, C.t_ident], [tb_], inc=(cc == cn - 1))
            K.copy(("act", "dve")[(c0 // 4) % 2], dst[:, c0:c0 + cn, tb * 128:(tb + 1) * 128],
                   bk[:, 0:cn * 128].rearrange("p (c t) -> p c t", c=cn), [tb_], t_dst[c0:c0 + cn])


def rmsnorm_fm(C, hT, t_h, nch, g_sb, t_g, uT, t_u, sq, t_sq, rstd, t_rstd, dim, W=T):
    K = C.K
    for kc in range(nch):
        K.act(sq[:, kc, :], hT[:, kc, :], AF.Square, [t_h[kc]], [t_sq[kc]])
    bk, tbk = C.bank()
    for kc in range(nch):
        K.mm(bk[:, 0:W], C.ones[:], sq[:, kc, :], kc == 0, kc == nch - 1, [C.t_ones, t_sq[kc]], [tbk])
    K.act(rstd[:, :], bk[:, 0:W], AF.Ln, [tbk], [t_rstd], bias=EPS, scale=1.0 / dim)
    K.act(rstd[:, :], rstd[:, :], AF.Exp, [t_rstd], [t_rstd], scale=-0.5)
    for kc in range(nch):
        K.stt(uT[:, kc, :], hT[:, kc, :], g_sb[:, kc:kc + 1], rstd[:, :], ALU.mult, ALU.mult,
              [t_h[kc], t_g, t_rstd], [t_u[kc]])


class WStream:
    def __init__(self, K, name, nkc, cw, nslots):
        self.K, self.name, self.n = K, name, nslots
        self.buf = [K.sb("%s_%d" % (name, i), [128, nkc, cw], BF16) for i in range(nslots)]
        self.t = [K.trk(name) for _ in range(nslots)]
        self.i = 0

    def load(self, scr, t_scr, b, k0=0, kn=None):
        s = self.i % self.n
        self.i += 1
        nk = scr.shape[2] if kn is None else kn
        self.K.dma("sp", "%s%d" % (self.name, s), self.buf[s][:, 0:nk, :], scr[b, :, k0:k0 + nk, :],
                   reads=[t_scr], writes=[self.t[s]])
        return self.buf[s], self.t[s]


def dram_in(nc, name, shape, dt=F32):
    return nc.dram_tensor(name, list(shape), dt, kind="ExternalInput").ap()


def dram_out(nc, name, shape, dt=F32):
    return nc.dram_tensor(name, list(shape), dt, kind="ExternalOutput").ap()


def store_tokmajor(C, hT, t_h, dst, stage, q="pool"):
    K = C.K
    for tb in range(T // 128):
        sb_, t_sb, key = stage.next()
        for c4 in range(4):
            bk, t_bk = C.bank()
            for cc in range(4):
                kc = c4 * 4 + cc
                K.tr(bk[:, cc * 128:(cc + 1) * 128], hT[:, kc, tb * 128:(tb + 1) * 128], C.ident[:],
                     [t_h[kc], C.t_ident], [t_bk], inc=(cc == 3))
            K.copy(("act", "dve")[c4 % 2], sb_[:, c4 * 512:(c4 + 1) * 512], bk[:, :], [t_bk], [t_sb])
        K.dma(q, key + "s", dst[tb * 128:(tb + 1) * 128, :], sb_[:, :], reads=[t_sb], writes=[])


def ffn_phase_body(K, C, h_in, hprev, p_in, vecs, scr, h_out, gath=None, t_gath=None, hook=None, tail=None):
    vec = K.sb("vec", [128, NVF], F32)
    t_vec = K.trk("vec")
    K.dma("sp", "const", vec[:], vecs[:, :], writes=[t_vec])
    g_ffn = vec[:, 0:16]
    g_ple = vec[:, 16:32]
    cw = [vec[:, 32 + i * NFF:32 + (i + 1) * NFF] for i in range(3)]
    cb = vec[:, 32 + 3 * NFF:32 + 4 * NFF]
    NOT = K.trk("none")

    hT = K.sb("hT", [128, 16, T], F32)
    t_h = [K.trk("h") for _ in range(16)]
    uT = K.sb("uT", [128, 16, T], BF16)
    t_u = [K.trk("u") for _ in range(16)]
    hid = K.sb("hid", [128, NFF, T], BF16)
    t_hid = [K.trk("hid") for _ in range(NFF)]
    rstd = K.sb("rstd", [128, T], F32)
    t_rstd = K.trk("rstd")
    stage = Stage(K, "stg", D, 2)
    pstage = Stage(K, "pstg", PLE, 2)
    pT = K.sb("pT", [128, 2, T], BF16)
    t_p = [K.trk("p") for _ in range(2)]
    halo_sb = K.sb("halo_sb", [128, NFF, 2], F32)
    t_halo = [K.trk("halo") for _ in range(NFF)]
    abuf = [K.sb("abuf%d" % i, [128, T + 2], F32) for i in range(2)]
    t_abuf = [K.trk("abuf") for _ in range(2)]
    cbuf = [K.sb("cbuf%d" % i, [128, T], F32) for i in range(2)]
    t_cbuf = [K.trk("cbuf") for _ in range(2)]
    sig = [K.sb("sig%d" % i, [128, T], F32) for i in range(2)]
    t_sig = [K.trk("sig") for _ in range(2)]
    WG = WStream(K, "wg", 16, 128, 3)
    WU = WStream(K, "wu", 16, 128, 3)
    WD = WStream(K, "wd", NFF, 128, 2)
    WP = WStream(K, "wp", 16, 256, 2)
    wpp_sb = K.sb("wpp", [128, 2, D], BF16)
    t_wpp = K.trk("wpp")
    for b in range(4):
        K.dma("sp", "const", wpp_sb[:, :, b * 512:(b + 1) * 512], scr["pp"][b], writes=[t_wpp])
    sq = hid
    t_sq = t_hid
    hpT = K.sb("hpT", [128, 16, 2], F32)
    hpu = K.sb("hpu", [128, 16, 2], BF16)
    hpsq = K.sb("hpsq", [128, 16, 2], BF16)
    hprs = K.sb("hprs", [128, 2], F32)
    t_hp = [K.trk("hp") for _ in range(16)]
    t_hpu = [K.trk("hpu") for _ in range(16)]
    t_hpsq = [K.trk("hpsq") for _ in range(16)]
    t_hprs = K.trk("hprs")
    if gath is None:
        for t_ in range(2):
            K.dma("sp", "const", hpT[:, :, t_], hprev[t_, :].rearrange("(c p) -> p c", p=128), writes=t_hp,
                  allow_slow_non_contiguous=True)
    else:
        hpa = K.sb("hpa", [128, 16, 8], F32)
        t_hpa = K.trk("hpa")
        hrows = K.sb("hrows", [128, 128], F32)
        t_hrows = K.trk("hrows")
        K.dma("sp", "const", hrows[:, :], gath.rearrange("r (c p) -> (r c) p", p=128), reads=[t_gath],
              writes=[t_hrows])
        bkh, t_bkh = C.bank()
        K.tr(bkh[:, 0:128], hrows[:, :], C.ident[:], [t_hrows, C.t_ident], [t_bkh])
        K.copy("act", hpa[:, :, :].rearrange("p c r -> p r c"), bkh[:, 0:128].rearrange("p (r c) -> p r c", r=8),
               [t_bkh], [t_hpa])
        sel = vec[:, 32 + 4 * NFF:32 + 4 * NFF + 4]
        for t_ in range(2):
            K.ts("dve", hpT[:, :, t_], hpa[:, :, t_], sel[:, 0:1], None, ALU.mult, None, [t_hpa, t_vec], t_hp)
            for r_ in range(1, 4):
                K.stt(hpT[:, :, t_], hpa[:, :, 2 * r_ + t_], sel[:, r_:r_ + 1], hpT[:, :, t_], ALU.mult, ALU.add,
                      [t_hpa, t_vec] + t_hp, t_hp)
    rmsnorm_fm(C, hpT, t_hp, 16, g_ffn, t_vec, hpu, t_hpu, hpsq, t_hpsq, hprs, t_hprs, D, W=2)

    for st in range(NST):
        tok = slice(st * T, (st + 1) * T)
        load_tokmajor_T(C, h_in[tok, :], D, hT, t_h, stage)
        load_tokmajor_T(C, p_in[tok, :], PLE, pT, t_p, pstage)
        rmsnorm_fm(C, hT, t_h, 16, g_ffn, t_vec, uT, t_u, sq, t_sq, rstd, t_rstd, D)
        for j in range(NFF):
            if hook is not None:
                hook()
            wg, t_wg = WG.load(scr["gate"], NOT, j)
            wu, t_wu = WU.load(scr["up"], NOT, j)
            if st == 0:
                bh, t_bh = C.bank()
                for kc in range(16):
                    K.mm(bh[:, 0:2], wg[:, kc, :], hpu[:, kc, :], kc == 0, kc == 15, [t_wg, t_hpu[kc]], [t_bh])
                K.copy("act", halo_sb[:, j, :], bh[:, 0:2], [t_bh], [t_halo[j]])
            bg, t_bg = C.bank()
            for kc in range(16):
                K.mm(bg[:, :], wg[:, kc, :], uT[:, kc, :], kc == 0, kc == 15, [t_wg, t_u[kc]], [t_bg])
            bu, t_bu = C.bank()
            for kc in range(16):
                K.mm(bu[:, :], wu[:, kc, :], uT[:, kc, :], kc == 0, kc == 15, [t_wu, t_u[kc]], [t_bu])
            ab, t_ab = abuf[j % 2], t_abuf[j % 2]
            cbf, t_cb = cbuf[j % 2], t_cbuf[j % 2]
            K.copy("pool", ab[:, 0:2], halo_sb[:, j, :], [t_halo[j]], [t_ab])
            K.copy("act", ab[:, 2:T + 2], bg[:, :], [t_bg], [t_ab])
            K.copy("pool", halo_sb[:, j, :], ab[:, T:T + 2], [t_ab], [t_halo[j]])
            K.ts("dve", cbf[:, :], ab[:, 2:T + 2], cw[2][:, j:j + 1], cb[:, j:j + 1], ALU.mult, ALU.add,
                 [t_ab, t_vec], [t_cb])
            K.stt(cbf[:, :], ab[:, 1:T + 1], cw[1][:, j:j + 1], cbf[:, :], ALU.mult, ALU.add,
                  [t_ab, t_vec, t_cb], [t_cb])
            K.stt(cbf[:, :], ab[:, 0:T], cw[0][:, j:j + 1], cbf[:, :], ALU.mult, ALU.add,
                  [t_ab, t_vec, t_cb], [t_cb])
            K.act(cbf[:, :], cbf[:, :], AF.Silu, [t_cb], [t_cb])
            K.tt("dve", hid[:, j, :], cbf[:, :], bu[:, :], ALU.mult, [t_cb, t_bu], [t_hid[j]])
        for n in range(16):
            wd, t_wd = WD.load(scr["down"], NOT, n)
            bd, t_bd = C.bank()
            for j in range(NFF):
                K.mm(bd[:, :], wd[:, j, :], hid[:, j, :], j == 0, j == NFF - 1, [t_wd, t_hid[j]], [t_bd])
            K.tt("dve", hT[:, n, :], hT[:, n, :], bd[:, :], ALU.add, [t_h[n], t_bd], [t_h[n]])
        rmsnorm_fm(C, hT, t_h, 16, g_ple, t_vec, uT, t_u, sq, t_sq, rstd, t_rstd, D)
        for n2 in range(8):
            wp, t_wp = WP.load(scr["pg"], NOT, n2)
            for nn in range(2):
                n = n2 * 2 + nn
                bg, t_bg = C.bank()
                for kc in range(16):
                    K.mm(bg[:, :], wp[:, kc, nn * 128:(nn + 1) * 128], uT[:, kc, :], kc == 0, kc == 15,
                         [t_wp, t_u[kc]], [t_bg])
                bp, t_bp = C.bank()
                for kc in range(2):
                    K.mm(bp[:, :], wpp_sb[:, kc, n * 128:(n + 1) * 128], pT[:, kc, :], kc == 0, kc == 1,
                         [t_wpp, t_p[kc]], [t_bp])
                sg, t_sg = sig[n % 2], t_sig[n % 2]
                K.act(sg[:, :], bg[:, :], AF.Sigmoid, [t_bg], [t_sg])
                K.tt("dve", sg[:, :], sg[:, :], bp[:, :], ALU.mult, [t_sg, t_bp], [t_sg])
                K.tt("pool", hT[:, n, :], hT[:, n, :], sg[:, :], ALU.add, [t_h[n], t_sg], [t_h[n]])
        store_tokmajor(C, hT, t_h, h_out[tok, :], stage)
    if tail is not None:
        tail()


def cast_ffn_weights(K, w_gate, w_up, w_down, w_pg, w_pp, pfx=""):
    with contextlib.ExitStack() as st2:
        K2 = K
        old = K.stack
        K.stack = st2
        WC = WCast(K, nslots=3)
        scr = {}
        scr["gate"] = WC.cast(w_gate, D, DFF, pfx + "s_gate", cw=128)
        scr["up"] = WC.cast(w_up, D, DFF, pfx + "s_up", cw=128)
        scr["down"] = WC.cast(w_down, DFF, D, pfx + "s_down", cw=128)
        scr["pg"] = WC.cast(w_pg, D, D, pfx + "s_pg", cw=256)
        scr["pp"] = WC.cast(w_pp, PLE, D, pfx + "s_pp", cw=512)
        K.barrier()
        K.stack = old
    return scr


def build_ffn_phase(nc):
    stack = contextlib.ExitStack()
    with stack:
        K = Kern(nc, stack)
        h_in = dram_in(nc, "h_in", [SEG, D])
        hprev = dram_in(nc, "hprev", [2, D])
        p_in = dram_in(nc, "p_in", [SEG, PLE])
        w_gate = dram_in(nc, "w_gate", [D, DFF])
        w_up = dram_in(nc, "w_up", [D, DFF])
        w_down = dram_in(nc, "w_down", [DFF, D])
        w_pg = dram_in(nc, "w_pg", [D, D])
        w_pp = dram_in(nc, "w_pp", [PLE, D])
        vecs = dram_in(nc, "vecs", [128, NVF])
        h_out = dram_out(nc, "h_out", [SEG, D])
        C = Ctx(K)
        scr = cast_ffn_weights(K, w_gate, w_up, w_down, w_pg, w_pp)
        ffn_phase_body(K, C, h_in, hprev, p_in, vecs, scr, h_out)
        K.finish()
        K.replay()
    return nc


NVF = 32 + NFF * 4 + 4


def ffn_vecs(norm_ffn, norm_ple, conv_w, conv_b, q=0):
    v = np.zeros((128, NVF), np.float32)
    if q > 0:
        v[:, 32 + 4 * NFF + q - 1] = 1.0
    v[:, 0:16] = norm_ffn.reshape(16, 128).T
    v[:, 16:32] = norm_ple.reshape(16, 128).T
    for i in range(3):
        v[:, 32 + i * NFF:32 + (i + 1) * NFF] = conv_w[i].reshape(NFF, 128).T
    v[:, 32 + 3 * NFF:32 + 4 * NFF] = conv_b.reshape(NFF, 128).T
    return v


NSLOT = 3
SLOT_ST = 8
O_ST = 8
TWO_PI = 2.0 * np.pi
MAGIC = 12582912.0
C1 = 6.28125
C2 = float(TWO_PI - 6.28125)
PI_LO = 3.1415925
NVA = 80


def vecsA_host(inp, q):
    v = np.zeros((128, NVA), np.float32)
    v[:, 0:16] = inp["norm_mix"][0].reshape(16, 128).T
    v[:, 16:24] = inp["e_lb_logits"][0].reshape(8, 128).T
    v[:, 24:32] = inp["e_lb_logits"][1].reshape(8, 128).T
    v[:, 32:36] = inp["e_q_a_norm"][0].reshape(4, 128).T
    v[:, 36:40] = inp["e_kv_a_norm"][0].reshape(4, 128).T
    qn, kn = inp["e_q_norm"][0], inp["e_k_norm"][0]
    for base, g in ((40, qn), (43, kn)):
        v[:, base] = g[0:128]
        v[0:64, base + 1] = g[128:192]
        v[0:32, base + 2] = g[160:192]
        v[32:64, base + 2] = g[128:160]
    v[:, 46] = inp["e_hg_onorm"][0]
    invf = (10000.0 ** (-np.arange(32, dtype=np.float32) / np.float32(32))).astype(np.float32)
    v[0:64, 47] = np.concatenate([invf, invf])
    v[0:32, 48] = -1.0
    v[32:64, 48] = 1.0
    for i in range(NSLOT):
        valid = (q - NSLOT + i) >= 0
        v[:, 65 + i] = 0.0 if valid else -30000.0
        v[:, 69 + i] = 1.0 if valid else 0.0
    v[:, 68] = 0.0
    return v


class PhaseA:
    def __init__(self, K, C, nc):
        self.K, self.C, self.nc = K, C, nc

    def prologue(self, w_in, w_uq, w_ukv, w_out):
        K = self.K
        with contextlib.ExitStack() as st2:
            old = K.stack
            K.stack = st2
            WC = WCast(K, nslots=3)
            scr = {}
            scr["win"] = WC.cast(w_in[:, 0:5120], D, 5120, "sA_win", cw=128)
            scr["hi"] = WC.cast(w_in[:, 2048:3072], D, 1024, "sA_hi", cw=256)
            scr["wout"] = WC.cast(w_out, D, D, "sA_wout", cw=128)
            scr["small"] = K.nc.dram_tensor("sA_small", [4, 128, 4, 1024], BF16, kind="Internal").ap()
            scr["kpe"] = K.nc.dram_tensor("sA_kpe", [1, 128, 16, 128], BF16, kind="Internal").ap()

            def piece(dst, src, n, nkc):
                srcv = src.rearrange("(kc p) n -> p kc n", p=128)
                for k0 in range(0, nkc, 8):
                    kn = min(8, nkc - k0)
                    s = WC.i % WC.n
                    WC.i += 1
                    fv = WC.f[s][:].rearrange("p a b -> p (a b)")[:, 0:kn * n].rearrange("p (a b) -> p a b", a=kn)
                    bv = WC.b[s][:].rearrange("p a b -> p (a b)")[:, 0:kn * n].rearrange("p (a b) -> p a b", a=kn)
                    K.dma("sp", "wcl%d" % s, fv, srcv[:, k0:k0 + kn, :], writes=[WC.tf[s]])
                    K.copy(WC.engs[WC.i % 3], bv, fv, [WC.tf[s]], [WC.tb[s]])
                    K.dma("pool", "wcs%d" % s, dst[:, k0:k0 + kn, :], bv, reads=[WC.tb[s]], writes=[])

            sm = scr["small"]
            for h in range(8):
                piece(sm[0, :, :, h * 128:(h + 1) * 128], w_ukv[:, h * 256:h * 256 + 128], 128, 4)
                piece(sm[1, :, :, h * 128:(h + 1) * 128], w_ukv[:, h * 256 + 128:h * 256 + 256], 128, 4)
                piece(sm[2, :, :, h * 128:(h + 1) * 128], w_uq[:, h * 192:h * 192 + 128], 128, 4)
                piece(sm[3, :, :, h * 64:(h + 1) * 64], w_uq[:, h * 192 + 128:h * 192 + 192], 64, 4)
                piece(sm[3, :, :, 512 + h * 64:512 + h * 64 + 32], w_uq[:, h * 192 + 160:h * 192 + 192], 32, 4)
                piece(sm[3, :, :, 512 + h * 64 + 32:512 + (h + 1) * 64], w_uq[:, h * 192 + 128:h * 192 + 160], 32, 4)
            piece(scr["kpe"][0, :, :, 0:64], w_in[:, 5120:5184], 64, 16)
            piece(scr["kpe"][0, :, :, 64:96], w_in[:, 5152:5184], 32, 16)
            piece(scr["kpe"][0, :, :, 96:128], w_in[:, 5120:5152], 32, 16)
            K.barrier()
            K.stack = old
        self.scr = scr

    def alloc(self, vecs, pos_all, ntok_all):
        K, C, nc = self.K, self.C, self.nc
        self.vec = K.sb("vecA", [128, NVA], F32)
        self.t_vec = K.trk("vecA")
        K.dma("sp", "const", self.vec[:], vecs[:, :], writes=[self.t_vec])
        v = self.vec
        self.lb = K.sb("lb", [128, 8], F32)
        self.oml = K.sb("oml", [128, 8], F32)
        K.tt("dve", self.lb[:], v[:, 16:24], v[:, 24:32], ALU.subtract, [self.t_vec], [self.t_vec])
        K.act(self.lb[:], self.lb[:], AF.Sigmoid, [self.t_vec], [self.t_vec])
        K.ts("dve", self.oml[:], self.lb[:], -1.0, 1.0, ALU.mult, ALU.add, [self.t_vec], [self.t_vec])
        tri = np.triu(np.ones((128, 128), np.float32))
        self.tri = K.sb("tri", [128, 128], F32)
        rm = np.ones((128, T), np.float32)
        rm[:, 0::128] = 0.0
        self.rmask = K.sb("rmask", [128, T], F32)
        self.t_cst = K.trk("cstA")
        K.dma("sp", "const", self.tri[:], nc.inline_tensor(tri, name="tri_d").ap()[:, :], writes=[self.t_cst])
        K.dma("sp", "const", self.rmask[:], nc.inline_tensor(rm, name="rm_d").ap()[:, :], writes=[self.t_cst])
        self.trib = K.sb("trib", [128, 128], BF16)
        K.copy("dve", self.trib[:], self.tri[:], [self.t_cst], [self.t_cst])
        self.pos_all = pos_all
        self.ntok = ntok_all
        self.Kn = nc.dram_tensor("Kn_scr", [8, 128, ntok_all], BF16, kind="Internal").ap()
        self.Kr = nc.dram_tensor("Kr_scr", [8, 64, ntok_all], BF16, kind="Internal").ap()
        self.Vs = nc.dram_tensor("V_scr", [ntok_all, 1024], BF16, kind="Internal").ap()
        ng = ntok_all // T
        self.t_kn = [[K.trk("kn") for _ in range(8)] for _ in range(ng)]
        self.t_kr = [[K.trk("kr") for _ in range(8)] for _ in range(ng)]
        self.t_vs = [[K.trk("vs") for _ in range(4)] for _ in range(ng)]
        self.xT = K.sb("xT", [128, 16, T], F32)
        self.t_x = [K.trk("x") for _ in range(16)]
        self.wk = [self.xT[:, i, :] for i in range(16)]
        self.t_wk = self.t_x
        self.wk_i = 0
        self.uT = K.sb("uTa", [128, 16, T], BF16)
        self.t_u = [K.trk("u") for _ in range(16)]
        self.mixT = K.sb("mixT", [128, 16, T], BF16)
        self.t_mix = [K.trk("mix") for _ in range(16)]
        self.sq = self.mixT
        self.t_sq = self.t_mix
        self.sq4 = K.sb("sq4", [128, 4, T], BF16)
        self.t_sq4 = [K.trk("sq4") for _ in range(4)]
        self.rstd = K.sb("rstdA", [128, T], F32)
        self.t_rstd = K.trk("rstd")
        self.stage = Stage(K, "stgA", D, 2)
        self.W = WStream(K, "wA", 16, 128, 3)
        self.WH = WStream(K, "wAh", 16, 256, 2)
        self.WS = WStream(K, "wAs", 4, 1024, 2)
        self.posi = K.sb("posi", [64, T], I32)
        self.cos = K.sb("cosT", [64, T], F32)
        self.sin = K.sb("sinT", [64, T], F32)
        self.t_rope = K.trk("rope")
        self.t_cs = K.trk("cossin")
        self.wb = [K.sb("wbA%d" % i, [128, T], BF16) for i in range(6)]
        self.t_wb = [K.trk("wb") for _ in range(6)]
        self.wb_i = 0
        self.kvn = K.sb("kvn", [128, 4, T], BF16)
        self.t_kvn = [K.trk("kvn") for _ in range(4)]
        self.kpe = K.sb("kpe", [64, 2, T], F32)
        self.t_kpe = K.trk("kpe")
        self.KR = K.sb("KR", [64, T], F32)
        self.t_KR = K.trk("KR")
        self.sqpe = K.sb("sqpe", [64, T], BF16)
        self.t_sqpe = K.trk("sqpe")
        self.vtm = K.sb("vtm", [128, 4, 1024], BF16)
        self.t_vtm = [K.trk("vtm") for _ in range(4)]
        self.hv = self.vtm
        self.t_hv = self.t_vtm
        self.Qn = K.sb("Qn", [128, 8, T], BF16)
        self.Qr = K.sb("Qr", [64, 8, T], BF16)
        self.t_Q = [K.trk("Q") for _ in range(8)]
        self.S = K.sb("S", [128, 8, 128], F32)
        self.Sb = K.sb("Sb", [128, 8, 128], BF16)
        self.t_S = [K.trk("S") for _ in range(8)]
        self.t_Sb = [K.trk("Sb") for _ in range(8)]
        for h in range(8):
            K.op("pool", lambda e, h=h: e.memset(self.S[:, h, :], 0.0), writes=[self.t_S[h]])
            K.op("pool", lambda e, h=h: e.memset(self.Sb[:, h, :], 0.0), writes=[self.t_Sb[h]])
        self.fbuf = [K.sb("fbuf%d" % i, [128, T], F32) for i in range(2)]
        self.t_fbuf = [K.trk("fbuf") for _ in range(2)]
        self.kktm = K.sb("kktm", [128, 2, 4, 128], BF16)
        self.t_kktm = [K.trk("kktm") for _ in range(2)]
        self.ebl = K.sb("ebl", [128, 2, 4], F32)
        self.aK = [K.sb("aK%d" % i, [128, T], BF16) for i in range(3)]
        self.aKr = [K.sb("aKr%d" % i, [64, T], BF16) for i in range(3)]
        self.aV = [K.sb("aV%d" % i, [128, 4, 128], BF16) for i in range(3)]
        self.t_aK = [K.trk("aK") for _ in range(3)]
        self.t_aKr = [K.trk("aKr") for _ in range(3)]
        self.t_aV = [K.trk("aV") for _ in range(3)]
        self.aP = [K.sb("aP%d" % i, [128, T], BF16) for i in range(4)]
        self.t_aP = [K.trk("aP") for _ in range(4)]
        self.a_i = 0
        self.p_i = 0

    def work(self):
        i = self.wk_i % 16
        self.wk_i += 1
        return self.wk[i], self.t_wk[i]

    def workb(self):
        i = self.wb_i % 6
        self.wb_i += 1
        return self.wb[i], self.t_wb[i]

    def rope_tables(self, g):
        K = self.K
        v = self.vec
        K.dma("sp", "posi", self.posi[:, :], self.pos_all[0:1, g * T:(g + 1) * T].broadcast_to([64, T]),
              writes=[self.t_rope])
        pf, t_pf = self.work()
        ra, t_ra = self.work()
        rn, t_rn = self.work()
        posf, ra, rn = pf[0:64, :], ra[0:64, :], rn[0:64, :]
        K.copy("dve", posf, self.posi[:, :], [self.t_rope], [t_pf])
        for off, out in ((0.0, self.sin), (float(np.pi / 2), self.cos)):
            K.ts("dve", ra, posf, v[0:64, 47:48], off, ALU.mult, ALU.add, [t_pf, self.t_vec], [t_ra])
            K.ts("dve", rn, ra, float(1.0 / TWO_PI), MAGIC, ALU.mult, ALU.add, [t_ra], [t_rn])
            K.ts("dve", rn, rn, -MAGIC, None, ALU.add, None, [t_rn], [t_rn])
            K.stt(ra, rn, -C1, ra, ALU.mult, ALU.add, [t_rn, t_ra], [t_ra])
            K.stt(ra, rn, -C2, ra, ALU.mult, ALU.add, [t_rn, t_ra], [t_ra])
            K.ts("dve", ra, ra, -PI_LO, PI_LO, ALU.max, ALU.min, [t_ra], [t_ra])
            K.act(out[:, :], ra, AF.Sin, [t_ra], [self.t_cs])
        K.ts("dve", self.sin[:, :], self.sin[:, :], v[0:64, 48:49], None, ALU.mult, None,
             [self.t_cs, self.t_vec], [self.t_cs])

    def proj_fm(self, lhs_fn, lhs_trk, nkc, rhsT, t_rhs, np_out=128):
        K, C = self.K, self.C
        bk, t_bk = C.bank()
        for kc in range(nkc):
            K.mm(bk[0:np_out, :], lhs_fn(kc), rhsT[:, kc, :], kc == 0, kc == nkc - 1, [lhs_trk, t_rhs[kc]], [t_bk])
        return bk, t_bk

    def rstd_from(self, parts, dim, out, t_out, np_out=128):
        K, C = self.K, self.C
        bk, t_bk = C.bank()
        n = len(parts)
        for i, (ap, t, kp) in enumerate(parts):
            K.mm(bk[:, :], C.ones[0:kp, :], ap, i == 0, i == n - 1, [C.t_ones, t], [t_bk])
        K.act(out, bk[0:np_out, :], AF.Ln, [t_bk], [t_out], bias=EPS, scale=1.0 / dim)
        K.act(out, out, AF.Exp, [t_out], [t_out], scale=-0.5)

    def front(self, x_all, g):
        C = self.C
        load_tokmajor_T(C, x_all[g * T:(g + 1) * T, :], D, self.xT, self.t_x, self.stage)
        rmsnorm_fm(C, self.xT, self.t_x, 16, self.vec[:, 0:16], self.t_vec, self.uT, self.t_u,
                   self.sq, self.t_sq, self.rstd, self.t_rstd, D)

    def mla_kv(self, g):
        K, C, v = self.K, self.C, self.vec
        NOT = K.trk("none")
        self.rope_tables(g)
        sqs = []
        ckv = []
        for c in range(4):
            w, t_w = self.W.load(self.scr["win"], NOT, 36 + c)
            bk, t_bk = self.proj_fm(lambda kc, w=w: w[:, kc, :], t_w, 16, self.uT, self.t_u)
            cf, t_cf = self.work()
            ckv.append((cf, t_cf))
            K.copy("act", cf[:, :], bk[:, :], [t_bk], [t_cf])
            K.act(self.sq4[:, c, :], bk[:, :], AF.Square, [t_bk], [self.t_sq4[c]])
            sqs.append((self.sq4[:, c, :], self.t_sq4[c], 128))
        self.rstd_from(sqs, 512, self.rstd[:, :], self.t_rstd)
        for c in range(4):
            K.stt(self.kvn[:, c, :], ckv[c][0][:, :], v[:, 36 + c:37 + c], self.rstd[:, :], ALU.mult, ALU.mult,
                  [ckv[c][1], self.t_vec, self.t_rstd], [self.t_kvn[c]])
        wk_, t_wk_ = self.W.load(self.scr["kpe"], NOT, 0)
        for i in range(2):
            bk, t_bk = self.proj_fm(lambda kc, i=i: wk_[:, kc, i * 64:(i + 1) * 64], t_wk_, 16,
                                    self.uT, self.t_u, np_out=64)
            K.copy("act", self.kpe[:, i, :], bk[0:64, :], [t_bk], [self.t_kpe])
        K.act(self.sqpe[:, :], self.kpe[:, 0, :], AF.Square, [self.t_kpe], [self.t_sqpe])
        tmp, t_tmp = self.work()
        K.stt(self.KR[:, :], self.kpe[:, 0, :], v[0:64, 44:45], self.cos[:, :], ALU.mult, ALU.mult,
              [self.t_kpe, self.t_vec, self.t_cs], [self.t_KR])
        K.stt(tmp[0:64, :], self.kpe[:, 1, :], v[0:64, 45:46], self.sin[:, :], ALU.mult, ALU.mult,
              [self.t_kpe, self.t_vec, self.t_cs], [t_tmp])
        K.tt("pool", self.KR[:, :], self.KR[:, :], tmp[0:64, :], ALU.add, [self.t_KR, t_tmp], [self.t_KR])
        ukn, t_ukn = self.WS.load(self.scr["small"], NOT, 0)
        for h in range(8):
            bk, t_bk = self.proj_fm(lambda kc, h=h: ukn[:, kc, h * 128:(h + 1) * 128], t_ukn, 4,
                                    self.kvn, self.t_kvn)
            kf, t_kf = self.work()
            K.copy("act", kf[:, :], bk[:, :], [t_bk], [t_kf])
            sqn, t_sqn = self.workb()
            K.act(sqn[:, :], bk[:, :], AF.Square, [t_bk], [t_sqn])
            rk, t_rk = self.work()
            self.rstd_from([(sqn[:, :], t_sqn, 128), (self.sqpe[:, :], self.t_sqpe, 64)], 192, rk[:, :], t_rk)
            kb, t_kb = self.workb()
            K.stt(kb[:, :], kf[:, :], v[:, 43:44], rk[:, :], ALU.mult, ALU.mult, [t_kf, self.t_vec, t_rk], [t_kb])
            K.dma("pool", "kns%d" % (h % 4), self.Kn[h, :, g * T:(g + 1) * T], kb[:, :], reads=[t_kb],
                  writes=[self.t_kn[g][h]])
            krb, t_krb = self.workb()
            K.tt("pool", krb[0:64, :], self.KR[:, :], rk[0:64, :], ALU.mult, [self.t_KR, t_rk], [t_krb])
            K.dma("pool", "krs%d" % (h % 4), self.Kr[h, :, g * T:(g + 1) * T], krb[0:64, :], reads=[t_krb],
                  writes=[self.t_kr[g][h]])
        uv, t_uv = self.WS.load(self.scr["small"], NOT, 1)
        for tb in range(4):
            for half in range(2):
                bk, t_bk = C.bank()
                for kc in range(4):
                    K.mm(bk[:, :], self.kvn[:, kc, tb * 128:(tb + 1) * 128], uv[:, kc, half * 512:(half + 1) * 512],
                         kc == 0, kc == 3, [self.t_kvn[kc], t_uv], [t_bk])
                K.copy(("act", "dve")[half], self.vtm[:, tb, half * 512:(half + 1) * 512], bk[:, :], [t_bk],
                       [self.t_vtm[tb]])
            K.dma("pool", "vs%d" % tb, self.Vs[g * T + tb * 128:g * T + (tb + 1) * 128, :], self.vtm[:, tb, :],
                  reads=[self.t_vtm[tb]], writes=[self.t_vs[g][tb]])

    def hgrn(self, own, vflag_col):
        K, C, v = self.K, self.C, self.vec
        NOT = K.trk("none")
        for qd_ in range(4):
            w, t_w = self.WH.load(self.scr["hi"], NOT, qd_)
            for tb in range(4):
                bk, t_bk = C.bank()
                for kc in range(16):
                    K.mm(bk[:, 0:256], self.uT[:, kc, tb * 128:(tb + 1) * 128], w[:, kc, :], kc == 0, kc == 15,
                         [self.t_u[kc], t_w], [t_bk])
                if vflag_col is None:
                    K.copy("act", self.hv[:, tb, qd_ * 256:(qd_ + 1) * 256], bk[:, 0:256], [t_bk], [self.t_hv[tb]])
                else:
                    K.ts("dve", self.hv[:, tb, qd_ * 256:(qd_ + 1) * 256], bk[:, 0:256], vflag_col, None, ALU.mult, None,
                         [t_bk, self.t_vec], [self.t_hv[tb]])
        def hg_proj(h):
            w, t_w = self.W.load(self.scr["win"], NOT, 8 + h)
            bk, t_bk = self.proj_fm(lambda kc, w=w: w[:, kc, :], t_w, 16, self.uT, self.t_u)
            f, t_f = self.fbuf[h % 2], self.t_fbuf[h % 2]
            K.act(f[:, :], bk[:, :], AF.Sigmoid, [t_bk], [t_f])
            return f, t_f

        nxt = hg_proj(0)
        for h in range(8):
            par = h % 2
            f, t_f = nxt
            if h < 7:
                nxt = hg_proj(h + 1)
            K.ts("dve", f[:, :], f[:, :], self.oml[:, h:h + 1], self.lb[:, h:h + 1], ALU.mult, ALU.add,
                 [t_f, self.t_vec], [t_f])
            b, t_b = self.work()
            K.act(b[:, :], f[:, :], AF.Ln, [t_f], [t_b])
            K.op("dve", lambda e, b=b: e.tensor_tensor_scan(b[:, :], self.rmask[:, :], b[:, :], 0.0, ALU.mult, ALU.add),
                 [t_b, self.t_cst], [t_b])
            K.ts("pool", f[:, :], f[:, :], -1.0, 1.0, ALU.mult, ALU.add, [t_f], [t_f])
            b3 = b[:, :].rearrange("p (c t) -> p c t", c=4)
            e4, t_e4 = self.work()
            for c in range(4):
                K.act(e4[:, c * 128:(c + 1) * 128], b[:, c * 128:(c + 1) * 128], AF.Exp, [t_b], [t_e4],
                      bias=b[:, c * 128 + 127:c * 128 + 128], scale=-1.0)
            K.tt("pool", e4[:, :], e4[:, :], f[:, :], ALU.mult, [t_e4, t_f], [t_e4])
            bkt, t_bkt = C.bank()
            for c in range(4):
                K.tr(bkt[:, c * 128:(c + 1) * 128], e4[:, c * 128:(c + 1) * 128], C.ident[:], [t_e4, C.t_ident],
                     [t_bkt], inc=(c == 3))
            K.copy("act", self.kktm[:, par, :, :].rearrange("p c k -> p (c k)"), bkt[:, :], [t_bkt], [self.t_kktm[par]])
            K.act(self.ebl[:, par, :], b3[:, :, 127], AF.Exp, [t_b], [self.t_kktm[par]])
            if own:
                wq, t_wq = self.W.load(self.scr["win"], NOT, h)
                bq, t_bq = self.proj_fm(lambda kc, wq=wq: wq[:, kc, :], t_wq, 16, self.uT, self.t_u)
                qf, t_qf = self.work()
                K.copy("act", qf[:, :], bq[:, :], [t_bq], [t_qf])
                e3, t_e3 = self.work()
                K.act(e3[:, :], b[:, :], AF.Exp, [t_b], [t_e3])
                qd, t_qd = self.workb()
                K.tt("dve", qd[:, :], qf[:, :], e3[:, :], ALU.mult, [t_qf, t_e3], [t_qd])
                e1, t_e1 = self.work()
                e2, t_e2 = self.work()
                nref, t_nref = self.work()
                K.ts("dve", nref[:, 0:4], b3[:, :, 63], -1.0, None, ALU.mult, None, [t_b], [t_nref])
                for c in range(4):
                    K.act(e1[:, c * 128:(c + 1) * 128], b[:, c * 128:(c + 1) * 128], AF.Exp, [t_b, t_nref], [t_e1],
                          bias=nref[:, c:c + 1], scale=1.0)
                    K.act(e2[:, c * 128:(c + 1) * 128], b[:, c * 128:(c + 1) * 128], AF.Exp, [t_b], [t_e2],
                          bias=b[:, c * 128 + 63:c * 128 + 64], scale=-1.0)
                qr, t_qr = self.workb()
                kr, t_kr = self.workb()
                K.tt("dve", qr[:, :], qf[:, :], e1[:, :], ALU.mult, [t_qf, t_e1], [t_qr])
                K.tt("pool", kr[:, :], f[:, :], e2[:, :], ALU.mult, [t_f, t_e2], [t_kr])
                bo, t_bo = C.bank(hold=True)
            for c in range(4):
                cs = slice(c * 128, (c + 1) * 128)
                if own:
                    ba, t_ba = C.bank()
                    K.mm(ba[:, 0:128], kr[:, cs], qr[:, cs], True, True, [t_kr, t_qr], [t_ba])
                    am, t_am = self.workb()
                    K.tt("dve", am[:, 0:128], ba[:, 0:128], self.tri[:, :], ALU.mult, [t_ba, self.t_cst], [t_am])
                    K.mm(bo[:, cs], self.hv[:, c, h * 128:(h + 1) * 128], am[:, 0:128], True, False,
                         [self.t_hv[c], t_am], [t_bo], inc=False)
                    K.mm(bo[:, cs], self.Sb[:, h, :], qd[:, cs], False, True, [self.t_Sb[h], t_qd], [t_bo], inc=True)
                bs, t_bs = C.bank()
                K.mm(bs[:, 0:128], self.kktm[:, par, c, :], self.hv[:, c, h * 128:(h + 1) * 128], True, True,
                     [self.t_kktm[par], self.t_hv[c]], [t_bs])
                K.stt(self.S[:, h, :], self.S[:, h, :], self.ebl[:, par, c:c + 1], bs[:, 0:128], ALU.mult, ALU.add,
                      [self.t_S[h], self.t_kktm[par], t_bs], [self.t_S[h]])
                K.copy("pool", self.Sb[:, h, :], self.S[:, h, :], [self.t_S[h]], [self.t_Sb[h]])
            if own:
                of, t_of = self.work()
                K.copy("act", of[:, :], bo[:, :], [t_bo], [t_of])
                sqo, t_sqo = self.workb()
                K.act(sqo[:, :], bo[:, :], AF.Square, [t_bo], [t_sqo])
                C.release(t_bo)
                ro, t_ro = self.work()
                self.rstd_from([(sqo[:, :], t_sqo, 128)], 128, ro[:, :], t_ro)
                wg_, t_wg = self.W.load(self.scr["win"], NOT, 24 + h)
                bg, t_bg = self.proj_fm(lambda kc, wg_=wg_: wg_[:, kc, :], t_wg, 16, self.uT, self.t_u)
                sg, t_sg = self.work()
                K.act(sg[:, :], bg[:, :], AF.Silu, [t_bg], [t_sg])
                K.stt(of[:, :], of[:, :], v[:, 46:47], ro[:, :], ALU.mult, ALU.mult, [t_of, self.t_vec, t_ro], [t_of])
                K.tt("pool", self.mixT[:, h, :], of[:, :], sg[:, :], ALU.mult, [t_of, t_sg], [self.t_mix[h]])

    def mla_q(self):
        K, C, v = self.K, self.C, self.vec
        NOT = K.trk("none")
        sqs = []
        cq = []
        for c in range(4):
            w, t_w = self.W.load(self.scr["win"], NOT, 32 + c)
            bk, t_bk = self.proj_fm(lambda kc, w=w: w[:, kc, :], t_w, 16, self.uT, self.t_u)
            cf, t_cf = self.work()
            cq.append((cf, t_cf))
            K.copy("act", cf[:, :], bk[:, :], [t_bk], [t_cf])
            K.act(self.sq4[:, c, :], bk[:, :], AF.Square, [t_bk], [self.t_sq4[c]])
            sqs.append((self.sq4[:, c, :], self.t_sq4[c], 128))
        self.rstd_from(sqs, 512, self.rstd[:, :], self.t_rstd)
        qn = self.kvn
        t_qn = self.t_kvn
        for c in range(4):
            K.stt(qn[:, c, :], cq[c][0][:, :], v[:, 32 + c:33 + c], self.rstd[:, :], ALU.mult, ALU.mult,
                  [cq[c][1], self.t_vec, self.t_rstd], [t_qn[c]])
        sc = float(192 ** -0.5)
        uqn, t_uqn = self.WS.load(self.scr["small"], NOT, 2)
        uqr, t_uqr = self.WS.load(self.scr["small"], NOT, 3)
        for h in range(8):
            bk, t_bk = self.proj_fm(lambda kc, h=h: uqn[:, kc, h * 128:(h + 1) * 128], t_uqn, 4, qn, t_qn)
            qf, t_qf = self.work()
            K.copy("act", qf[:, :], bk[:, :], [t_bk], [t_qf])
            sqn, t_sqn = self.workb()
            K.act(sqn[:, :], bk[:, :], AF.Square, [t_bk], [t_sqn])
            b1, t_b1 = self.proj_fm(lambda kc, h=h: uqr[:, kc, h * 64:(h + 1) * 64], t_uqr, 4, qn, t_qn, 64)
            b2, t_b2 = self.proj_fm(lambda kc, h=h: uqr[:, kc, 512 + h * 64:512 + (h + 1) * 64], t_uqr, 4, qn, t_qn, 64)
            sqr, t_sqr = self.workb()
            K.act(sqr[0:64, :], b1[0:64, :], AF.Square, [t_b1], [t_sqr])
            rq, t_rq = self.work()
            self.rstd_from([(sqn[:, :], t_sqn, 128), (sqr[0:64, :], t_sqr, 64)], 192, rq[:, :], t_rq)
            K.ts("pool", rq[:, :], rq[:, :], sc, None, ALU.mult, None, [t_rq], [t_rq])
            K.stt(self.Qn[:, h, :], qf[:, :], v[:, 40:41], rq[:, :], ALU.mult, ALU.mult, [t_qf, self.t_vec, t_rq],
                  [self.t_Q[h]])
            r1, t_r1 = self.work()
            r2, t_r2 = self.work()
            K.stt(r1[0:64, :], b1[0:64, :], v[0:64, 41:42], self.cos[:, :], ALU.mult, ALU.mult,
                  [t_b1, self.t_vec, self.t_cs], [t_r1])
            K.stt(r2[0:64, :], b2[0:64, :], v[0:64, 42:43], self.sin[:, :], ALU.mult, ALU.mult,
                  [t_b2, self.t_vec, self.t_cs], [t_r2])
            K.tt("pool", r1[0:64, :], r1[0:64, :], r2[0:64, :], ALU.add, [t_r1, t_r2], [t_r1])
            K.tt("pool", self.Qr[:, h, :], r1[0:64, :], rq[0:64, :], ALU.mult, [t_r1, t_rq], [self.t_Q[h]])

    def attention(self, g, n_prior_groups):
        K, C, v = self.K, self.C, self.vec
        for h in range(8):
            bo, t_bo = C.bank(hold=True)
            bd, t_bd = C.bank(hold=True)
            groups = list(range(0, g + 1))
            pend = None
            first = True
            for gi in groups:
                diag = (gi == g)
                slot = gi // SLOT_ST if gi < n_prior_groups else NSLOT
                bias = v[:, 65 + slot:66 + slot] if gi < n_prior_groups else v[:, 68:69]
                s = self.a_i % 3
                self.a_i += 1
                K.dma("sp", "aK%d" % s, self.aK[s][:, :], self.Kn[h, :, gi * T:(gi + 1) * T],
                      reads=[self.t_kn[gi][h]], writes=[self.t_aK[s]])
                K.dma("sp", "aKr%d" % s, self.aKr[s][:, :], self.Kr[h, :, gi * T:(gi + 1) * T],
                      reads=[self.t_kr[gi][h]], writes=[self.t_aKr[s]])
                K.dma("sp", "aV%d" % s, self.aV[s][:, :, :],
                      self.Vs[gi * T:(gi + 1) * T, h * 128:(h + 1) * 128].rearrange("(tb p) v -> p tb v", p=128),
                      reads=self.t_vs[gi], writes=[self.t_aV[s]])
                for kb in range(4):
                    q0 = kb * 128 if diag else 0
                    bs, t_bs = C.bank()
                    K.mm(bs[:, q0:T], self.aK[s][:, kb * 128:(kb + 1) * 128], self.Qn[:, h, q0:T], True, False,
                         [self.t_aK[s], self.t_Q[h]], [t_bs], inc=False)
                    K.mm(bs[:, q0:T], self.aKr[s][:, kb * 128:(kb + 1) * 128], self.Qr[:, h, q0:T], False, True,
                         [self.t_aKr[s], self.t_Q[h]], [t_bs], inc=True)
                    pi = self.p_i % 4
                    self.p_i += 1
                    P, t_P = self.aP[pi], self.t_aP[pi]
                    K.act(P[:, q0:T], bs[:, q0:T], AF.Exp, [t_bs, self.t_vec], [t_P], bias=bias, scale=1.0)
                    if diag:
                        K.tt("dve", P[:, q0:q0 + 128], P[:, q0:q0 + 128], self.trib[:, :], ALU.mult,
                             [t_P, self.t_cst], [t_P])
                    if pend is not None:
                        self._pv(*pend)
                    pend = (bo, t_bo, bd, t_bd, P, t_P, s, kb, q0, first, False)
                    first = False
            lst = list(pend)
            lst[-1] = True
            self._pv(*lst)
            rd, t_rd = self.work()
            K.act(rd[:, :], bd[:, :], AF.Ln, [t_bd], [t_rd])
            K.act(rd[:, :], rd[:, :], AF.Exp, [t_rd], [t_rd], scale=-1.0)
            K.tt("dve", self.mixT[:, 8 + h, :], bo[:, :], rd[:, :], ALU.mult, [t_bo, t_rd], [self.t_mix[8 + h]])
            C.release(t_bo)
            C.release(t_bd)

    def _pv(self, bo, t_bo, bd, t_bd, P, t_P, s, kb, q0, first, last):
        K, C = self.K, self.C
        K.mm(bo[:, q0:T], self.aV[s][:, kb, :], P[:, q0:T], first, last, [self.t_aV[s], t_P], [t_bo], inc=last)
        K.mm(bd[:, q0:T], C.ones[:, :], P[:, q0:T], first, last, [C.t_ones, t_P], [t_bd], inc=True)

    def out_proj(self, x_all, g, h_mid, o):
        K, C = self.K, self.C
        NOT = K.trk("none")
        for n in range(16):
            w, t_w = self.W.load(self.scr["wout"], NOT, n)
            bk, t_bk = self.proj_fm(lambda kc, w=w: w[:, kc, :], t_w, 16, self.mixT, self.t_mix)
            K.copy("act", self.xT[:, n, :], bk[:, :], [t_bk], [self.t_x[n]])
        for tb in range(4):
            sb_, t_sb, key = self.stage.next()
            K.dma("sp", key, sb_[:, :], x_all[g * T + tb * 128:g * T + (tb + 1) * 128, :], writes=[t_sb])
            for c4 in range(4):
                bk, t_bk = C.bank()
                for cc in range(4):
                    kc = c4 * 4 + cc
                    K.tr(bk[:, cc * 128:(cc + 1) * 128], self.xT[:, kc, tb * 128:(tb + 1) * 128], C.ident[:],
                         [self.t_x[kc], C.t_ident], [t_bk], inc=(cc == 3))
                K.tt("dve", sb_[:, c4 * 512:(c4 + 1) * 512], bk[:, :], sb_[:, c4 * 512:(c4 + 1) * 512], ALU.add,
                     [t_bk, t_sb], [t_sb])
            K.dma("pool", key + "s", h_mid[o * T + tb * 128:o * T + (tb + 1) * 128, :], sb_[:, :], reads=[t_sb], writes=[])


def build_phaseA(nc):
    stack = contextlib.ExitStack()
    with stack:
        K = Kern(nc, stack)
        NP = NSLOT * SLOT_ST
        ntok = (NP + O_ST) * T
        x_all = dram_in(nc, "x_all", [ntok, D])
        pos_all = dram_in(nc, "pos_all", [1, ntok], I32)
        w_in = dram_in(nc, "w_in", [D, 5184])
        w_uq = dram_in(nc, "w_uq", [512, 1536])
        w_ukv = dram_in(nc, "w_ukv", [512, 2048])
        w_out = dram_in(nc, "w_out", [D, D])
        vecs = dram_in(nc, "vecsA", [128, NVA])
        h_mid = dram_out(nc, "h_mid", [O_ST * T, D])
        C = Ctx(K)
        A = PhaseA(K, C, nc)
        A.prologue(w_in, w_uq, w_ukv, w_out)
        A.alloc(vecs, pos_all, ntok)
        for g in range(NP):
            A.front(x_all, g)
            A.mla_kv(g)
            A.hgrn(False, A.vec[:, 69 + g // SLOT_ST:70 + g // SLOT_ST])
        for o in range(O_ST):
            g = NP + o
            A.front(x_all, g)
            A.mla_kv(g)
            A.hgrn(True, None)
            A.mla_q()
            A.attention(g, NP)
            A.out_proj(x_all, g, h_mid, o)
        K.finish()
        K.replay()
    return nc


def phaseA_inputs(inp, b, q):
    NP = NSLOT * SLOT_ST * T
    NO = O_ST * T
    x = inp["x"][b]
    pos = inp["positions"][b]
    x_all = np.empty((NP + NO, D), np.float32)
    pos_all = np.empty((1, NP + NO), np.int32)
    for i in range(NSLOT):
        src = max(q - NSLOT + i, 0)
        x_all[i * SEG:(i + 1) * SEG] = x[src * SEG:(src + 1) * SEG]
        pos_all[0, i * SEG:(i + 1) * SEG] = pos[src * SEG:(src + 1) * SEG]
    x_all[NP:] = x[q * SEG:(q + 1) * SEG]
    pos_all[0, NP:] = pos[q * SEG:(q + 1) * SEG]
    return {"x_all": x_all, "pos_all": pos_all, "w_in": inp["e_w_in"][0], "w_uq": inp["e_w_uq"][0],
            "w_ukv": inp["e_w_ukv"][0], "w_out": inp["e_w_out"][0], "vecsA": vecsA_host(inp, q)}


RH = 8
NRANK = 4
NVC = 32


def ret_tables():
    g = 1.0 - 2.0 ** (-5.0 - np.arange(8, dtype=np.float64))
    idx = np.arange(128, dtype=np.float64)
    DT = np.zeros((128, 8, 128), np.float32)
    for h in range(8):
        diff = idx[None, :] - idx[:, None]
        DT[:, h, :] = np.where(diff >= 0, g[h] ** np.maximum(diff, 0), 0.0) / 16.0
    decq = np.zeros((128, 8, 128), np.float32)
    for h in range(8):
        decq[:, h, :] = (g[h] ** (idx + 1.0))[None, :]
    kdec = np.zeros((128, 8), np.float32)
    for h in range(8):
        kdec[:, h] = g[h] ** (127.0 - idx) / 16.0
    cdec = [float(g[h] ** 128.0) for h in range(8)]
    return DT, decq, kdec, cdec, g


def vecsC_host(inp, q):
    v = np.zeros((128, 16 + 1 + NVC), np.float32)
    v[:, 0:16] = inp["norm_mix"][1].reshape(16, 128).T
    v[:, 16] = (10000.0 ** (-np.arange(128, dtype=np.float32) / np.float32(128))).astype(np.float32)
    g = 1.0 - 2.0 ** (-5.0 - np.arange(8, dtype=np.float64))
    for r in range(NRANK):
        for h in range(8):
            v[:, 17 + r * 8 + h] = float(g[h] ** (float(SEG) * (q - 1 - r))) if r < q else 0.0
    return v


class PhaseC:
    def __init__(self, K, C, nc, full, kvs=None):
        self.K, self.C, self.nc, self.full, self.kvs = K, C, nc, full, kvs

    def prologue(self, w_in, w_out, scr0=None):
        K = self.K
        with contextlib.ExitStack() as st2:
            old = K.stack
            K.stack = st2
            WC = WCast(K, nslots=3)
            scr = dict(scr0) if scr0 else {}
            if "k" not in scr and not (self.full and self.kvs is not None):
                scr["k"] = WC.cast(w_in[:, 2048:4096], D, 2048, "sC_k", cw=128)
                scr["v"] = WC.cast(w_in[:, 4096:8192], D, 4096, "sC_v", cw=256)
            if self.full and "q" not in scr:
                scr["q"] = WC.cast(w_in[:, 0:2048], D, 2048, "sC_q", cw=128)
                scr["g"] = WC.cast(w_in[:, 8192:12288], D, 4096, "sC_g", cw=128)
                scr["wout"] = WC.cast(w_out, 4096, D, "sC_wout", cw=128)
            K.barrier()
            K.stack = old
        self.scr = scr

    def alloc(self, vecs, pos_own):
        K, C, nc, full = self.K, self.C, self.nc, self.full
        self.vec = K.sb("vecC", [128, 17 + NVC], F32)
        self.t_vec = K.trk("vecC")
        K.dma("sp", "const", self.vec[:], vecs[:, :], writes=[self.t_vec])
        DT, decq, kdec, cdec, _ = ret_tables()
        self.cdec = cdec
        self.t_cst = K.trk("cstC")
        self.kdec = K.sb("kdec", [128, 8], F32)
        K.dma("sp", "const", self.kdec[:], nc.inline_tensor(kdec, name="kdec_d%d" % int(full)).ap()[:, :], writes=[self.t_cst])
        if full:
            self.DT = K.sb("DT", [128, 8, 128], F32)
            self.decq = K.sb("decq", [128, 8, 128], F32)
            K.dma("sp", "const", self.DT[:], nc.inline_tensor(DT, name="DT_d%d" % int(full)).ap()[:, :, :], writes=[self.t_cst])
            K.dma("sp", "const", self.decq[:], nc.inline_tensor(decq, name="decq_d%d" % int(full)).ap()[:, :, :], writes=[self.t_cst])
        self.pos = pos_own
        self.xT = K.sb("hTc", [128, 16, T], F32)
        self.t_x = [K.trk("x") for _ in range(16)]
        self.wk_i = 0
        self.uT = K.sb("uTc", [128, 16, T], BF16)
        self.t_u = [K.trk("u") for _ in range(16)]
        self.rstd = K.sb("rstdC", [128, T], F32)
        self.t_rstd = K.trk("rstd")
        self.stage = Stage(K, "stgC", D, 2 if not full else 1)
        self.W = WStream(K, "wC", 16, 128, 2 if full else 3)
        self.use_kvs = self.kvs is not None
        nb_ = 2 if (full and self.use_kvs) else 1
        if not (full and self.use_kvs):
            self.WV = WStream(K, "wCv", 16, 256, 2)
        self.posi = K.sb("posiC", [128, T], I32)
        self.cos = K.sb("cosC", [128, T], F32)
        self.sin = K.sb("sinC", [128, T], F32)
        self.t_rope = K.trk("rope")
        self.t_cs = K.trk("cossin")
        self.wb = [K.sb("wbC%d" % i, [128, T], BF16) for i in range(4)]
        self.t_wb = [K.trk("wb") for _ in range(4)]
        self.wb_i = 0
        self.R = K.sb("R", [128, 8, 2, 512], F32)
        self.t_R = [[K.trk("R") for _ in range(2)] for _ in range(8)]
        self.vtm_b = [K.sb("vtmC%d" % i, [128, 4, 512], BF16) for i in range(nb_)]
        self.t_vtm_b = [[K.trk("vtm") for _ in range(4)] for _ in range(nb_)]
        self.kdtm_b = [K.sb("kdtm%d" % i, [128, 4, 256], BF16) for i in range(nb_)]
        self.t_kdtm_b = [[K.trk("kdtm") for _ in range(2)] for _ in range(nb_)]
        self.kr_b = [K.sb("krC%d" % i, [128, 2, T], BF16) for i in range(nb_)] if (full or self.use_kvs) else None
        self.t_kr_b = [K.trk("kr") for _ in range(nb_)]
        self.hb_i = 0
        if full:
            self.Rb = K.sb("Rb", [128, 2, 2, 512], BF16)
            self.t_Rb = [[K.trk("Rb") for _ in range(2)] for _ in range(2)]
            self.onT = K.sb("onT", [128, 32, T], BF16)
            self.t_on = [K.trk("on") for _ in range(32)]
            self.sq = self.onT
            self.t_sq = self.t_on
            self.WO = WStream(K, "wCo", 32, 128, 2)
            self.qr = K.sb("qrC", [128, 2, T], BF16)
            self.qd = K.sb("qdC", [128, 2, T], BF16)
            self.t_qr = K.trk("qr")
            self.t_qd = K.trk("qd")
            self.AD = [K.sb("AD%d" % i, [128, 128], BF16) for i in range(2)]
            self.t_AD = [K.trk("AD") for _ in range(2)]
        else:
            self.sq = K.sb("sqC", [128, 16, T], BF16)
            self.t_sq = [K.trk("sq") for _ in range(16)]

    def work(self):
        i = self.wk_i % 16
        self.wk_i += 1
        return self.xT[:, i, :], self.t_x[i]

    def workb(self):
        i = self.wb_i % 4
        self.wb_i += 1
        return self.wb[i], self.t_wb[i]

    def init_state(self, Rprev, t_Rprev=None):
        K = self.K
        self.t_Rprev = t_Rprev if t_Rprev is not None else K.trk("none")
        for h in range(8):
            for dc in range(2):
                if Rprev is None:
                    K.op("pool", lambda e, h=h, dc=dc: e.memset(self.R[:, h, dc, :], 0.0), writes=[self.t_R[h][dc]])
                    continue
                for i in range(NRANK - 1):
                    tmp, t_tmp = self.work()
                    src = Rprev[h // 2][i, h % 2, dc, :, :] if isinstance(Rprev, list) else Rprev[i, h, dc, :, :]
                    K.dma("sp", "rin%d" % (i % 2), tmp[:, :], src, reads=[self.t_Rprev], writes=[t_tmp])
                    c = self.vec[:, 17 + i * 8 + h:18 + i * 8 + h]
                    if i == 0:
                        K.ts("dve", self.R[:, h, dc, :], tmp[:, :], c, None, ALU.mult, None, [t_tmp, self.t_vec],
                             [self.t_R[h][dc]])
                    else:
                        K.stt(self.R[:, h, dc, :], tmp[:, :], c, self.R[:, h, dc, :], ALU.mult, ALU.add,
                              [t_tmp, self.t_vec, self.t_R[h][dc]], [self.t_R[h][dc]])

    def rope_tables(self, o):
        K, v = self.K, self.vec
        K.dma("sp", "posi", self.posi[:, :], self.pos[0:1, o * T:(o + 1) * T].broadcast_to([128, T]),
              writes=[self.t_rope])
        posf, t_pf = self.work()
        ra, t_ra = self.work()
        rn, t_rn = self.work()
        K.copy("dve", posf, self.posi[:, :], [self.t_rope], [t_pf])
        for off, out in ((0.0, self.sin), (float(np.pi / 2), self.cos)):
            K.ts("dve", ra, posf, v[:, 16:17], off, ALU.mult, ALU.add, [t_pf, self.t_vec], [t_ra])
            K.ts("dve", rn, ra, float(1.0 / TWO_PI), MAGIC, ALU.mult, ALU.add, [t_ra], [t_rn])
            K.ts("dve", rn, rn, -MAGIC, None, ALU.add, None, [t_rn], [t_rn])
            K.stt(ra, rn, -C1, ra, ALU.mult, ALU.add, [t_rn, t_ra], [t_ra])
            K.stt(ra, rn, -C2, ra, ALU.mult, ALU.add, [t_rn, t_ra], [t_ra])
            K.ts("dve", ra, ra, -PI_LO, PI_LO, ALU.max, ALU.min, [t_ra], [t_ra])
            K.act(out[:, :], ra, AF.Sin, [t_ra], [self.t_cs])

    def proj_fm(self, w, t_w, rhsT, t_rhs, nkc=16):
        K, C = self.K, self.C
        bk, t_bk = C.bank()
        for kc in range(nkc):
            K.mm(bk[:, :], w[:, kc, :], rhsT[:, kc, :], kc == 0, kc == nkc - 1, [t_w, t_rhs[kc]], [t_bk])
        return bk, t_bk

    def rope_pair(self, b1, t_b1, b2, t_b2, out1, out2, t_outs, f32=False):
        K = self.K
        a, t_a = self.work()
        b, t_b = self.work()
        K.tt("dve", a, b1[:, :], self.cos[:, :], ALU.mult, [t_b1, self.t_cs], [t_a])
        K.tt("dve", b, b2[:, :], self.sin[:, :], ALU.mult, [t_b2, self.t_cs], [t_b])
        K.tt("pool", out1, a, b, ALU.subtract, [t_a, t_b], t_outs)
        c, t_c = self.work()
        d, t_d = self.work()
        K.tt("dve", c, b2[:, :], self.cos[:, :], ALU.mult, [t_b2, self.t_cs], [t_c])
        K.tt("dve", d, b1[:, :], self.sin[:, :], ALU.mult, [t_b1, self.t_cs], [t_d])
        K.tt("pool", out2, c, d, ALU.add, [t_c, t_d], t_outs)

    def supertile(self, h_in, o, h_out=None):
        K, C, full = self.K, self.C, self.full
        NOT = K.trk("none")
        load_tokmajor_T(C, h_in[o * T:(o + 1) * T, :], D, self.xT, self.t_x, self.stage)
        rmsnorm_fm(C, self.xT, self.t_x, 16, self.vec[:, 0:16], self.t_vec, self.uT, self.t_u,
                   self.sq, self.t_sq, self.rstd, self.t_rstd, D)
        self.rope_tables(o)
        for h in range(8):
            if getattr(self, "hook", None) is not None:
                self.hook()
            bi = self.hb_i % len(self.vtm_b)
            self.hb_i += 1
            self.vtm, self.t_vtm = self.vtm_b[bi], self.t_vtm_b[bi]
            self.kdtm, self.t_kdtm = self.kdtm_b[bi], self.t_kdtm_b[bi]
            if self.kr_b is not None:
                self.kr, self.t_kr = self.kr_b[bi], self.t_kr_b[bi]
            load_kv = full and self.use_kvs
            if load_kv:
                K.dma("sp", "kvl%d" % bi, self.kdtm[:, :, :], self.kvs["kd"][o, h], writes=self.t_kdtm)
                K.dma("sp", "kvv%d" % bi, self.vtm[:, :, :], self.kvs["v"][o, h], writes=self.t_vtm)
                K.dma("sp", "kvr%d" % bi, self.kr[:, :, :], self.kvs["kr"][o, h], writes=[self.t_kr])
            else:
                wk1, t_wk1 = self.W.load(self.scr["k"], NOT, 2 * h)
                b1, t_b1 = self.proj_fm(wk1, t_wk1, self.uT, self.t_u)
                wk2, t_wk2 = self.W.load(self.scr["k"], NOT, 2 * h + 1)
                b2, t_b2 = self.proj_fm(wk2, t_wk2, self.uT, self.t_u)
                k1, t_k1 = self.work()
                k2, t_k2 = self.work()
                self.rope_pair(b1, t_b1, b2, t_b2, k1, k2, [t_k1, t_k2])
                for half in range(2):
                    wv, t_wv = self.WV.load(self.scr["v"], NOT, 2 * h + half)
                    for tb in range(4):
                        bk, t_bk = C.bank()
                        for kc in range(16):
                            K.mm(bk[:, 0:256], self.uT[:, kc, tb * 128:(tb + 1) * 128], wv[:, kc, :], kc == 0, kc == 15,
                                 [self.t_u[kc], t_wv], [t_bk])
                        K.copy("act", self.vtm[:, tb, half * 256:(half + 1) * 256], bk[:, 0:256], [t_bk], [self.t_vtm[tb]])
            if full:
                wq1, t_wq1 = self.W.load(self.scr["q"], NOT, 2 * h)
                q1, t_q1 = self.proj_fm(wq1, t_wq1, self.uT, self.t_u)
                wq2, t_wq2 = self.W.load(self.scr["q"], NOT, 2 * h + 1)
                q2, t_q2 = self.proj_fm(wq2, t_wq2, self.uT, self.t_u)
                self.rope_pair(q1, t_q1, q2, t_q2, self.qr[:, 0, :], self.qr[:, 1, :], [self.t_qr])
                dq = self.decq[:, h:h + 1, :].broadcast_to([128, 4, 128])
                for dc in range(2):
                    K.tt("dve", self.qd[:, dc, :].rearrange("p (c t) -> p c t", c=4),
                         self.qr[:, dc, :].rearrange("p (c t) -> p c t", c=4), dq, ALU.mult,
                         [self.t_qr, self.t_cst], [self.t_qd])
            for dc, (kf, t_kf) in (enumerate(((k1, t_k1), (k2, t_k2))) if not load_kv else ()):
                bt, t_bt = C.bank()
                for c in range(4):
                    K.tr(bt[:, c * 128:(c + 1) * 128], kf[:, c * 128:(c + 1) * 128], C.ident[:], [t_kf, C.t_ident],
                         [t_bt], inc=(c == 3))
                K.ts("dve", self.kdtm[:, :, dc * 128:(dc + 1) * 128], bt[:, :].rearrange("p (c d) -> p c d", c=4),
                     self.kdec[:, h:h + 1], None, ALU.mult, None, [t_bt, self.t_cst], [self.t_kdtm[dc]])
                if self.kr_b is not None:
                    K.copy("act", self.kr[:, dc, :], kf, [t_kf], [self.t_kr])
            if self.use_kvs and not full:
                K.dma("pool", "kvs0", self.kvs["kd"][o, h], self.kdtm[:, :, :], reads=self.t_kdtm, writes=[])
                K.dma("pool", "kvs1", self.kvs["v"][o, h], self.vtm[:, :, :], reads=self.t_vtm, writes=[])
                K.dma("pool", "kvs2", self.kvs["kr"][o, h], self.kr[:, :, :], reads=[self.t_kr], writes=[])
            if full:
                bo = [C.bank(hold=True) for _ in range(4)]
                for dc in range(2):
                    K.copy("pool", self.Rb[:, h % 2, dc, :], self.R[:, h, dc, :], [self.t_R[h][dc]], [self.t_Rb[h % 2][dc]])
            for c in range(4):
                cs = slice(c * 128, (c + 1) * 128)
                if full:
                    ba, t_ba = C.bank()
                    for dc in range(2):
                        K.mm(ba[:, 0:128], self.kr[:, dc, cs], self.qr[:, dc, cs], dc == 0, dc == 1,
                             [self.t_kr, self.t_qr], [t_ba])
                    ad, t_ad = self.AD[c % 2], self.t_AD[c % 2]
                    K.tt("dve", ad[:, :], ba[:, 0:128], self.DT[:, h, :], ALU.mult, [t_ba, self.t_cst], [t_ad])
                    for vc in range(4):
                        bov, t_bov = bo[vc]
                        K.mm(bov[:, cs], self.vtm[:, c, vc * 128:(vc + 1) * 128], ad[:, :], True, False,
                             [self.t_vtm[c], t_ad], [t_bov], inc=False)
                        for dc in range(2):
                            K.mm(bov[:, cs], self.Rb[:, h % 2, dc, vc * 128:(vc + 1) * 128], self.qd[:, dc, cs], False, dc == 1,
                                 [self.t_Rb[h % 2][dc], self.t_qd], [t_bov], inc=(dc == 1))
                for dc in range(2):
                    bs, t_bs = C.bank()
                    K.mm(bs[:, :], self.kdtm[:, c, dc * 128:(dc + 1) * 128], self.vtm[:, c, :], True, True,
                         [self.t_kdtm[dc], self.t_vtm[c]], [t_bs])
                    K.stt(self.R[:, h, dc, :], self.R[:, h, dc, :], self.cdec[h], bs[:, :], ALU.mult, ALU.add,
                          [self.t_R[h][dc], t_bs], [self.t_R[h][dc]])
                    if full and c < 3:
                        K.copy("pool", self.Rb[:, h % 2, dc, :], self.R[:, h, dc, :], [self.t_R[h][dc]], [self.t_Rb[h % 2][dc]])
            if full:
                ofs = []
                sqs = []
                for vc in range(4):
                    bov, t_bov = bo[vc]
                    of, t_of = self.work()
                    K.copy("act", of, bov[:, :], [t_bov], [t_of])
                    sqb, t_sqb = self.workb()
                    K.act(sqb[:, :], bov[:, :], AF.Square, [t_bov], [t_sqb])
                    C.release(t_bov)
                    ofs.append((of, t_of))
                    sqs.append((sqb, t_sqb))
                bk, t_bk = C.bank()
                for vc in range(4):
                    K.mm(bk[:, :], C.ones[:, :], sqs[vc][0][:, :], vc == 0, vc == 3, [C.t_ones, sqs[vc][1]], [t_bk])
                ro, t_ro = self.work()
                K.act(ro, bk[:, :], AF.Ln, [t_bk], [t_ro], bias=EPS, scale=1.0 / 512)
                K.act(ro, ro, AF.Exp, [t_ro], [t_ro], scale=-0.5)
                for vc in range(4):
                    wg_, t_wg = self.W.load(self.scr["g"], NOT, 4 * h + vc)
                    bg, t_bg = self.proj_fm(wg_, t_wg, self.uT, self.t_u)
                    sg, t_sg = self.work()
                    K.act(sg, bg[:, :], AF.Silu, [t_bg], [t_sg])
                    of, t_of = ofs[vc]
                    K.tt("dve", of, of, ro, ALU.mult, [t_of, t_ro], [t_of])
                    K.tt("pool", self.onT[:, 4 * h + vc, :], of, sg, ALU.mult, [t_of, t_sg], [self.t_on[4 * h + vc]])
        if full:
            for n in range(16):
                w, t_w = self.WO.load(self.scr["wout"], NOT, n)
                bk, t_bk = self.proj_fm(w, t_w, self.onT, self.t_on, nkc=32)
                K.copy("act", self.xT[:, n, :], bk[:, :], [t_bk], [self.t_x[n]])
            for tb in range(4):
                sb_, t_sb, key = self.stage.next()
                K.dma("sp", key, sb_[:, :], h_in[o * T + tb * 128:o * T + (tb + 1) * 128, :], writes=[t_sb])
                for c4 in range(4):
                    bk, t_bk = C.bank()
                    for cc in range(4):
                        kc = c4 * 4 + cc
                        K.tr(bk[:, cc * 128:(cc + 1) * 128], self.xT[:, kc, tb * 128:(tb + 1) * 128], C.ident[:],
                             [self.t_x[kc], C.t_ident], [t_bk], inc=(cc == 3))
                    K.tt("dve", sb_[:, c4 * 512:(c4 + 1) * 512], bk[:, :], sb_[:, c4 * 512:(c4 + 1) * 512], ALU.add,
                         [t_bk, t_sb], [t_sb])
                K.dma("pool", key + "s", h_out[o * T + tb * 128:o * T + (tb + 1) * 128, :], sb_[:, :], reads=[t_sb],
                      writes=[])

    def store_state(self, R_out):
        K = self.K
        for h in range(8):
            for dc in range(2):
                dst = R_out[h // 2][h % 2, dc, :, :] if isinstance(R_out, list) else R_out[h, dc, :, :]
                K.dma("pool", "rout%d" % dc, dst, self.R[:, h, dc, :], reads=[self.t_R[h][dc]], writes=[])


def build_phaseC(nc, full):
    stack = contextlib.ExitStack()
    with stack:
        K = Kern(nc, stack)
        h_in = dram_in(nc, "h_in", [O_ST * T, D])
        pos = dram_in(nc, "pos", [1, O_ST * T], I32)
        w_in = dram_in(nc, "w_in", [D, 12288])
        vecs = dram_in(nc, "vecsC", [128, 17 + NVC])
        if full:
            w_out = dram_in(nc, "w_out", [4096, D])
            Rprev = dram_in(nc, "Rprev", [NRANK, 8, 2, 128, 512])
            h_out = dram_out(nc, "h_mid", [O_ST * T, D])
        else:
            w_out = None
            R_out = dram_out(nc, "R_out", [8, 2, 128, 512])
        C = Ctx(K)
        P = PhaseC(K, C, nc, full)
        P.prologue(w_in, w_out)
        P.alloc(vecs, pos)
        P.init_state(Rprev if full else None)
        for o in range(O_ST):
            P.supertile(h_in, o, h_out if full else None)
        if not full:
            P.store_state(R_out)
        K.finish()
        K.replay()
    return nc


def _launch(build, maps):
    nc = bass.Bass("TRN2", target_bir_lowering=False)
    build(nc)
    res = run_bass_kernel_spmd(nc, maps, core_ids=list(range(NCORE)))
    return res.results


def _ffn_maps(inp, li, hs):
    maps = []
    vec = ffn_vecs(inp["norm_ffn"][li], inp["norm_ple"][li], inp["ffn_conv_w"][li], inp["ffn_conv_b"][li])
    for c in range(NCORE):
        b, q = c // 4, c % 4
        hprev = np.ascontiguousarray(hs[c - 1][-2:, :]) if q > 0 else np.zeros((2, D), np.float32)
        maps.append({"h_in": hs[c], "hprev": hprev,
                     "p_in": np.ascontiguousarray(inp["p"][li, b, q * SEG:(q + 1) * SEG]),
                     "w_gate": inp["ffn_w_gate"][li], "w_up": inp["ffn_w_up"][li], "w_down": inp["ffn_w_down"][li],
                     "w_pg": inp["ple_w_gate"][li], "w_pp": inp["ple_w_proj"][li], "vecs": vec})
    return maps


def kernel_unfused(**inputs):
    inp = {k: np.asarray(v) for k, v in inputs.items()}
    r = _launch(build_phaseA, [phaseA_inputs(inp, c // 4, c % 4) for c in range(NCORE)])
    hmid0 = [np.ascontiguousarray(x["h_mid"]) for x in r]
    r = _launch(build_ffn_phase, _ffn_maps(inp, 0, hmid0))
    h1 = [np.ascontiguousarray(x["h_out"]) for x in r]
    posm = [np.ascontiguousarray(inp["positions"][c // 4][None, (c % 4) * SEG:(c % 4 + 1) * SEG]) for c in range(NCORE)]
    w_in1 = inp["o_w_in"][0]
    r = _launch(lambda nc: build_phaseC(nc, False),
                [{"h_in": h1[c], "pos": posm[c], "w_in": w_in1, "vecsC": vecsC_host(inp, c % 4)} for c in range(NCORE)])
    Rl = [x["R_out"] for x in r]
    maps = []
    for c in range(NCORE):
        q = c % 4
        Rprev = np.zeros((NRANK, 8, 2, 128, 512), np.float32)
        for r_ in range(q):
            Rprev[r_] = Rl[c - q + r_]
        maps.append({"h_in": h1[c], "pos": posm[c], "w_in": w_in1, "w_out": inp["o_w_out"][0], "Rprev": Rprev,
                     "vecsC": vecsC_host(inp, q)})
    r = _launch(lambda nc: build_phaseC(nc, True), maps)
    hmid1 = [np.ascontiguousarray(x["h_mid"]) for x in r]
    r = _launch(build_ffn_phase, _ffn_maps(inp, 1, hmid1))
    out = np.empty((2, 4 * SEG, D), np.float32)
    for c in range(NCORE):
        out[c // 4, (c % 4) * SEG:(c % 4 + 1) * SEG] = r[c]["h_out"]
    return out


GROUPS = [[0, 1, 2, 3], [4, 5, 6, 7]]
DEBUG_OUT = False


def build_fused(nc):
    stack = contextlib.ExitStack()
    with stack:
        K = Kern(nc, stack)
        NP = NSLOT * SLOT_ST
        ntok = (NP + O_ST) * T
        x_all = dram_in(nc, "x_all", [ntok, D])
        pos_all = dram_in(nc, "pos_all", [1, ntok], I32)
        w_in = dram_in(nc, "w_in", [D, 5184])
        w_uq = dram_in(nc, "w_uq", [512, 1536])
        w_ukv = dram_in(nc, "w_ukv", [512, 2048])
        w_out = dram_in(nc, "w_out", [D, D])
        vecsA = dram_in(nc, "vecsA", [128, NVA])
        ffw = []
        for li in range(2):
            ffw.append(dict(
                p=dram_in(nc, "p%d" % li, [SEG, PLE]),
                gate=dram_in(nc, "w_gate%d" % li, [D, DFF]), up=dram_in(nc, "w_up%d" % li, [D, DFF]),
                down=dram_in(nc, "w_down%d" % li, [DFF, D]), pg=dram_in(nc, "w_pg%d" % li, [D, D]),
                pp=dram_in(nc, "w_pp%d" % li, [PLE, D]), vecs=dram_in(nc, "vecsF%d" % li, [128, NVF])))
        o_w_in = dram_in(nc, "o_w_in", [D, 12288])
        o_w_out = dram_in(nc, "o_w_out", [4096, D])
        vecsC = dram_in(nc, "vecsC", [128, 17 + NVC])
        out = dram_out(nc, "out", [SEG, D])
        internal = lambda name, shape: nc.dram_tensor(name, list(shape), F32, kind="Internal").ap()
        mk = (lambda name, shape: dram_out(nc, name, shape)) if DEBUG_OUT else internal
        hA = mk("hA", [SEG, D])
        hB = mk("hB", [SEG, D])
        hC = mk("hC", [SEG, D])
        hl = [internal("hl%d" % i, [2, D]) for i in range(2)]
        hg = [internal("hg%d" % i, [8, D]) for i in range(2)]
        Rloc = [internal("Rloc%d" % i, [512, 512]) for i in range(4)]
        Rall = [internal("Rall%d" % i, [NRANK * 512, 512]) for i in range(4)]
        C = Ctx(K)
        BYP = mybir.AluOpType.bypass

        with K.phase("A_"):
            A = PhaseA(K, C, nc)
            A.prologue(w_in, w_uq, w_ukv, w_out)
            A.alloc(vecsA, pos_all, ntok)
            for g in range(NP):
                A.front(x_all, g)
                A.mla_kv(g)
                A.hgrn(False, A.vec[:, 69 + g // SLOT_ST:70 + g // SLOT_ST])
            for o in range(O_ST):
                g = NP + o
                A.front(x_all, g)
                A.mla_kv(g)
                A.hgrn(True, None)
                A.mla_q()
                A.attention(g, NP)
                A.out_proj(x_all, g, hA, o)

        def exchange(i, hsrc):
            t_hl = K.trk("hl")
            K.dma("sp", "xch", hl[i][:, :], hsrc[SEG - 2:SEG, :], writes=[t_hl])
            t_g = K.trk("hg")
            K.barrier()
            K.collective("AllGather", BYP, GROUPS, hl[i], hg[i], reads=[t_hl], writes=[t_g])
            K.barrier()
            return t_g

        t_g0 = exchange(0, hA)
        f = ffw[0]
        with K.phase("B_"):
            scr = cast_ffn_weights(K, f["gate"], f["up"], f["down"], f["pg"], f["pp"], pfx="B")
            WCb = WCast(K, nslots=2, rows=1, defer=True, tag="b")
            scr_c1 = {"k": WCb.cast(o_w_in[:, 2048:4096], D, 2048, "sC_k", cw=128),
                      "v": WCb.cast(o_w_in[:, 4096:8192], D, 4096, "sC_v", cw=256)}
            per_j = -(-len(WCb.steps) // (NFF * NST))
            ffn_phase_body(K, C, hA, None, f["p"], f["vecs"], scr, hB, gath=hg[0], t_gath=t_g0,
                           hook=lambda: WCb.pump(per_j), tail=lambda: WCb.pump(len(WCb.steps)))
        pos_own = pos_all[:, NP * T:NP * T + SEG]
        ibf = lambda name, shape: nc.dram_tensor(name, list(shape), BF16, kind="Internal").ap()
        kvs = {"kd": ibf("kvs_kd", [O_ST, 8, 128, 4, 256]), "v": ibf("kvs_v", [O_ST, 8, 128, 4, 512]),
               "kr": ibf("kvs_kr", [O_ST, 8, 128, 2, T])}
        with K.phase("C1_"):
            P1 = PhaseC(K, C, nc, False, kvs)
            P1.prologue(o_w_in, None, scr0=scr_c1)
            P1.alloc(vecsC, pos_own)
            P1.init_state(None)
            WCd = WCast(K, nslots=3, rows=4, defer=True, tag="d")
            scr_c2 = dict(P1.scr)
            scr_c2["q"] = WCd.cast(o_w_in[:, 0:2048], D, 2048, "sC_q", cw=128)
            scr_c2["g"] = WCd.cast(o_w_in[:, 8192:12288], D, 4096, "sC_g", cw=128)
            scr_c2["wout"] = WCd.cast(o_w_out, 4096, D, "sC_wout", cw=128)
            fD = ffw[1]
            scr_d = {"gate": WCd.cast(fD["gate"], D, DFF, "Ds_gate", cw=128),
                     "up": WCd.cast(fD["up"], D, DFF, "Ds_up", cw=128),
                     "down": WCd.cast(fD["down"], DFF, D, "Ds_down", cw=128),
                     "pg": WCd.cast(fD["pg"], D, D, "Ds_pg", cw=256),
                     "pp": WCd.cast(fD["pp"], PLE, D, "Ds_pp", cw=512)}
            per_head = -(-len(WCd.steps) // (8 * O_ST))
            P1.hook = lambda: WCd.pump(per_head)
            for o in range(O_ST):
                P1.supertile(hB, o)
            WCd.pump(len(WCd.steps))
            P1.store_state([Rloc[i].rearrange("(h dc p) n -> h dc p n", h=2, dc=2) for i in range(4)])
        t_R = K.trk("Rall")
        for i in range(4):
            K.collective("AllGather", BYP, GROUPS, Rloc[i], Rall[i], reads=[], writes=[t_R])
            K.barrier()
        with K.phase("C2_"):
            P2 = PhaseC(K, C, nc, True, kvs)
            P2.prologue(o_w_in, o_w_out, scr0=scr_c2)
            P2.alloc(vecsC, pos_own)
            P2.init_state([Rall[i].rearrange("(r h dc p) n -> r h dc p n", r=NRANK, h=2, dc=2) for i in range(4)], t_R)
            for o in range(O_ST):
                P2.supertile(hB, o, hC)
        t_g1 = exchange(1, hC)
        f = ffw[1]
        with K.phase("D_"):
            ffn_phase_body(K, C, hC, None, f["p"], f["vecs"], scr_d, out, gath=hg[1], t_gath=t_g1)
        K.finish()
        K.replay()
    return nc


def fused_inputs(inp, c):
    b, q = c // 4, c % 4
    m = phaseA_inputs(inp, b, q)
    for li in range(2):
        m["p%d" % li] = np.ascontiguousarray(inp["p"][li, b, q * SEG:(q + 1) * SEG])
        m["w_gate%d" % li] = inp["ffn_w_gate"][li]
        m["w_up%d" % li] = inp["ffn_w_up"][li]
        m["w_down%d" % li] = inp["ffn_w_down"][li]
        m["w_pg%d" % li] = inp["ple_w_gate"][li]
        m["w_pp%d" % li] = inp["ple_w_proj"][li]
        m["vecsF%d" % li] = ffn_vecs(inp["norm_ffn"][li], inp["norm_ple"][li], inp["ffn_conv_w"][li],
                                      inp["ffn_conv_b"][li], q)
    m["o_w_in"] = inp["o_w_in"][0]
    m["o_w_out"] = inp["o_w_out"][0]
    m["vecsC"] = vecsC_host(inp, q)
    return m


def kernel(**inputs):
    inp = {k: np.asarray(v) for k, v in inputs.items()}
    nc = bass.Bass("TRN2", target_bir_lowering=False)
    build_fused(nc)
    maps = [fused_inputs(inp, c) for c in range(NCORE)]
    res = run_bass_kernel_spmd(nc, maps, core_ids=list(range(NCORE)))
    out = np.empty((2, 4 * SEG, D), np.float32)
    for c in range(NCORE):
        out[c // 4, (c % 4) * SEG:(c % 4 + 1) * SEG] = res.results[c]["out"]
    return out
```
